# Optimizing a Trainium2 kernel written in Bass

```python
import jax
import jax.numpy as jnp
from jax import lax
import numpy as np


D_MODEL = 1024
BATCH = 8
SEQ = 2048
DEPTH = 1
DEC_BATCH = 128
DEC_SEQ = 4
PAST_LEN = 2048
PAGE_SIZE = 128

HEAD_DIM = 64
HEADS_PER_GROUP = 4
DIL_GROUPS = ((128, 1), (512, 4), (2048, 16))
N_DGROUPS = len(DIL_GROUPS)
ATTN_WIDTH = N_DGROUPS * HEADS_PER_GROUP * HEAD_DIM
COMB_WIDTH = HEADS_PER_GROUP * HEAD_DIM
BLOCK = 128
CHUNK = 128
SG_GROUPS = 4
SG_WIDTH = 512
SG_GROUP_DIM = SG_WIDTH // SG_GROUPS
D_FF = 4 * D_MODEL
IN_WIDTH = 3 * ATTN_WIDTH + 2 * SG_WIDTH + 2 * D_MODEL
EPS = 1e-6
NEG = -1e30
SCALE = HEAD_DIM ** -0.5

kernel_name = 'hybrid_dilated_attn_gmlp_step'


def rms_norm(x, g):
    xf = x.astype(jnp.float32)
    y = xf * lax.rsqrt(jnp.mean(xf * xf, axis=-1, keepdims=True) + EPS)
    return (y * g.astype(jnp.float32)).astype(x.dtype)


def layer_norm(x, g, b):
    xf = x.astype(jnp.float32)
    mu = jnp.mean(xf, axis=-1, keepdims=True)
    var = jnp.mean(jnp.square(xf - mu), axis=-1, keepdims=True)
    y = (xf - mu) * lax.rsqrt(var + EPS)
    return (y * g.astype(jnp.float32) + b.astype(jnp.float32)).astype(x.dtype)


def project_inputs(h, w_in, b_gate):
    p = h @ w_in
    a = ATTN_WIDTH
    zs = 3 * a
    gs = zs + 2 * SG_WIDTH
    heads = h.shape[:-1] + (N_DGROUPS, HEADS_PER_GROUP, HEAD_DIM)
    q = p[..., :a].reshape(heads)
    k = p[..., a:2 * a].reshape(heads)
    v = p[..., 2 * a:zs].reshape(heads)
    z = jax.nn.gelu(p[..., zs:gs], approximate=False)
    gates = jax.nn.sigmoid((p[..., gs:] + b_gate).astype(jnp.float32)).astype(h.dtype)
    return q, k, v, z[..., :SG_WIDTH], z[..., SG_WIDTH:], gates[..., :D_MODEL], gates[..., D_MODEL:]


def banded_attention(q, k, v, reach):
    n, L, H, hd = q.shape
    nb = -(-L // BLOCK)
    lp = nb * BLOCK
    qb = jnp.pad(q, ((0, 0), (0, lp - L), (0, 0), (0, 0))).reshape(n, nb, BLOCK, H, hd)

    def key_blocks(t):
        tp = jnp.pad(t, ((0, 0), (BLOCK, lp - L), (0, 0), (0, 0))).reshape(n, nb + 1, BLOCK, H, hd)
        return jnp.concatenate([tp[:, :-1], tp[:, 1:]], axis=2)

    kb, vb = key_blocks(k), key_blocks(v)
    s = jnp.einsum('nbqhd,nbkhd->nbhqk', qb, kb).astype(jnp.float32) * SCALE
    qi = jnp.arange(BLOCK)[:, None]
    ki = jnp.arange(2 * BLOCK)[None, :]
    dist = qi + BLOCK - ki
    kpos = (jnp.arange(nb)[:, None, None] - 1) * BLOCK + ki[None]
    mask = (dist >= 0) & (dist <= reach) & (kpos >= 0)
    s = jnp.where(mask[None, :, None], s, NEG)
    lse = jax.nn.logsumexp(s, axis=-1)
    p = jnp.exp(s - lse[..., None]).astype(v.dtype)
    o = jnp.einsum('nbhqk,nbkhd->nbqhd', p, vb).reshape(n, lp, H, hd)[:, :L]
    lse = lse.transpose(0, 1, 3, 2).reshape(n, lp, H)[:, :L]
    return o, lse


def dilated_group_prompt(q, k, v, window, dil):
    b, s, H, hd = q.shape
    L = s // dil

    def to_res(t):
        return t.reshape(b, L, dil, H, hd).transpose(0, 2, 1, 3, 4).reshape(b * dil, L, H, hd)

    o, lse = banded_attention(to_res(q), to_res(k), to_res(v), window // dil)
    o = o.reshape(b, dil, L, H, hd).transpose(0, 2, 1, 3, 4).reshape(b, s, H, hd)
    lse = lse.reshape(b, dil, L, H).transpose(0, 2, 1, 3).reshape(b, s, H)
    return o, lse


def dilated_group_sample(q, k, v, buf, window, dil):
    t = q.shape[1]
    wb = buf.shape[1]
    kc = jnp.concatenate([buf[:, :, 0], k], axis=1)
    vc = jnp.concatenate([buf[:, :, 1], v], axis=1)
    n_keys = window // dil + 1
    idx = wb + jnp.arange(t)[:, None] - dil * jnp.arange(n_keys)[None, :]
    valid = idx >= 0
    idx = jnp.maximum(idx, 0)
    kg = kc[:, idx]
    vg = vc[:, idx]
    s = jnp.einsum('bthd,btjhd->bhtj', q, kg).astype(jnp.float32) * SCALE
    s = jnp.where(valid[None, None], s, NEG)
    lse = jax.nn.logsumexp(s, axis=-1)
    p = jnp.exp(s - lse[..., None]).astype(v.dtype)
    o = jnp.einsum('bhtj,btjhd->bthd', p, vg)
    return o, lse.transpose(0, 2, 1)


def merge_dilations(outs, lses):
    alpha = jax.nn.softmax(jnp.stack(lses, 0), axis=0).astype(outs[0].dtype)
    o = jnp.einsum('gbsh,gbshd->bshd', alpha, jnp.stack(outs, 0))
    return o.reshape(o.shape[:2] + (COMB_WIDTH,))


def spatial_gate_prompt(z1, z2, ln_z_g, ln_z_b, w_spatial, b_spatial):
    b, s, _ = z1.shape
    zn = layer_norm(z2, ln_z_g, ln_z_b).reshape(b, s // CHUNK, CHUNK, SG_GROUPS, SG_GROUP_DIM)
    mix = jnp.einsum('gts,bcsgk->bctgk', jnp.tril(w_spatial), zn) + b_spatial.T[:, :, None]
    return z1 * mix.reshape(b, s, SG_WIDTH)


def spatial_gate_sample(z1, z2, ln_z_g, ln_z_b, w_spatial, b_spatial):
    b, t, _ = z1.shape
    zn = layer_norm(z2, ln_z_g, ln_z_b)
    zg = zn.reshape(b, t, SG_GROUPS, SG_GROUP_DIM)
    mix = jnp.einsum('gts,bsgk->btgk', jnp.tril(w_spatial)[:, :t, :t], zg) + b_spatial[:, :t].T[:, :, None]
    return z1 * mix.reshape(b, t, SG_WIDTH), zn


def run_layer(x, bufs, norm_pre_mix, w_in, b_gate, ln_z_g, ln_z_b, w_spatial, b_spatial,
              w_ao, w_bo, w_out, norm_post_mix, norm_pre_ffn, w_up, w_down, norm_post_ffn):
    h = rms_norm(x, norm_pre_mix)
    q, k, v, z1, z2, g_a, g_b = project_inputs(h, w_in, b_gate)
    outs, lses, kv_new = [], [], []
    for gi, (win, dil) in enumerate(DIL_GROUPS):
        qg, kg, vg = q[:, :, gi], k[:, :, gi], v[:, :, gi]
        if bufs is None:
            o, lse = dilated_group_prompt(qg, kg, vg, win, dil)
            keep = min(win, x.shape[1])
            kv_new.append(jnp.stack([kg[:, -keep:], vg[:, -keep:]], axis=2))
        else:
            o, lse = dilated_group_sample(qg, kg, vg, bufs[gi], win, dil)
            kv_new.append(jnp.stack([kg, vg], axis=2))
        outs.append(o)
        lses.append(lse)
    o_a = merge_dilations(outs, lses)
    if bufs is None:
        o_b = spatial_gate_prompt(z1, z2, ln_z_g, ln_z_b, w_spatial, b_spatial)
        sg_new = None
    else:
        o_b, sg_new = spatial_gate_sample(z1, z2, ln_z_g, ln_z_b, w_spatial, b_spatial)
    merged = g_a * (o_a @ w_ao) + g_b * (o_b @ w_bo)
    x = x + rms_norm(merged @ w_out, norm_post_mix)
    h2 = rms_norm(x, norm_pre_ffn)
    f = jnp.square(jax.nn.relu(h2 @ w_up)) @ w_down
    x = x + rms_norm(f, norm_post_ffn)
    return x, kv_new, sg_new


def setup_inputs(seed: int = 0) -> dict:
    key = jax.random.key(seed)
    ks = jax.random.split(key, 24)
    f32 = jnp.float32

    def nrm(k, shape, scale):
        return jax.random.normal(k, shape, f32) * scale

    def gain(k, n):
        return 1.0 + 0.05 * jax.random.normal(k, (DEPTH, n), f32)

    def cache(k, win):
        return nrm(k, (DEPTH, DEC_BATCH, min(win, PAST_LEN), 2, HEADS_PER_GROUP, HEAD_DIM), 1.0)

    return {
        'x_prompt': nrm(ks[0], (BATCH, SEQ, D_MODEL), 1.0),
        'x_sample': nrm(ks[1], (DEC_BATCH, DEC_SEQ, D_MODEL), 1.0),
        'cache_kv_w128': cache(ks[2], DIL_GROUPS[0][0]),
        'cache_kv_w512': cache(ks[3], DIL_GROUPS[1][0]),
        'cache_kv_w2048': cache(ks[4], DIL_GROUPS[2][0]),
        'norm_pre_mix': gain(ks[5], D_MODEL),
        'w_in': nrm(ks[6], (DEPTH, D_MODEL, IN_WIDTH), D_MODEL ** -0.5),
        'b_gate': nrm(ks[7], (DEPTH, 2 * D_MODEL), 0.02),
        'ln_z_g': gain(ks[8], SG_WIDTH),
        'ln_z_b': nrm(ks[9], (DEPTH, SG_WIDTH), 0.02),
        'w_spatial': nrm(ks[10], (DEPTH, SG_GROUPS, CHUNK, CHUNK), CHUNK ** -0.5),
        'b_spatial': 1.0 + nrm(ks[11], (DEPTH, SG_GROUPS, CHUNK), 0.1),
        'w_ao': nrm(ks[12], (DEPTH, COMB_WIDTH, D_MODEL), COMB_WIDTH ** -0.5),
        'w_bo': nrm(ks[13], (DEPTH, SG_WIDTH, D_MODEL), SG_WIDTH ** -0.5),
        'w_out': nrm(ks[14], (DEPTH, D_MODEL, D_MODEL), D_MODEL ** -0.5),
        'norm_post_mix': gain(ks[15], D_MODEL),
        'norm_pre_ffn': gain(ks[16], D_MODEL),
        'w_up': nrm(ks[17], (DEPTH, D_MODEL, D_FF), D_MODEL ** -0.5),
        'w_down': nrm(ks[18], (DEPTH, D_FF, D_MODEL), D_FF ** -0.5),
        'norm_post_ffn': gain(ks[19], D_MODEL),
    }


def reference(x_prompt, x_sample, cache_kv_w128, cache_kv_w512, cache_kv_w2048,
              norm_pre_mix, w_in, b_gate, ln_z_g, ln_z_b, w_spatial, b_spatial,
              w_ao, w_bo, w_out, norm_post_mix, norm_pre_ffn, w_up, w_down, norm_post_ffn):
    y_p, y_s = x_prompt, x_sample
    kv_p = [[] for _ in DIL_GROUPS]
    kv_s = [[] for _ in DIL_GROUPS]
    sg_s = []
    for l in range(DEPTH):
        weights = (norm_pre_mix[l], w_in[l], b_gate[l], ln_z_g[l], ln_z_b[l], w_spatial[l],
                   b_spatial[l], w_ao[l], w_bo[l], w_out[l], norm_post_mix[l], norm_pre_ffn[l],
                   w_up[l], w_down[l], norm_post_ffn[l])
        y_p, kvp, _ = run_layer(y_p, None, *weights)
        bufs = (cache_kv_w128[l], cache_kv_w512[l], cache_kv_w2048[l])
        y_s, kvs, sgn = run_layer(y_s, bufs, *weights)
        for gi in range(N_DGROUPS):
            kv_p[gi].append(kvp[gi])
            kv_s[gi].append(kvs[gi])
        sg_s.append(sgn)
    return (y_p, y_s,
            jnp.stack(kv_p[0]), jnp.stack(kv_p[1]), jnp.stack(kv_p[2]),
            jnp.stack(kv_s[0]), jnp.stack(kv_s[1]), jnp.stack(kv_s[2]),
            jnp.stack(sg_s))
```

```python
import numpy as np
from contextlib import ExitStack
import concourse.bass as bass
import concourse.mybir as mybir
from concourse.bass_utils import run_bass_kernel_spmd

F32 = mybir.dt.float32
BF16 = mybir.dt.bfloat16
AF = mybir.ActivationFunctionType
ALU = mybir.AluOpType
AX = mybir.AxisListType

NCORES = 8
D = 1024
S = 2048
NS = 64
NTOK = S + NS
NT = 512
EPS = 1e-6
NSLOT = 4
import os
STAGE = float(os.environ.get("MK_STAGE", "9"))
SLOTF = 4096


class Buf:
    __slots__ = ("name", "w", "r", "dsem", "dcnt", "ex", "nosync")

    def __init__(self, name, ex=False, nosync=False):
        self.name = name
        self.ex = ex
        self.nosync = nosync
        self.w = {}
        self.r = {}
        self.dsem = None
        self.dcnt = 0


def _merge(d, s):
    for k, v in s.items():
        if v > d.get(k, 0):
            d[k] = v


class Prog:
    CE = ("pe", "act", "dve", "pool")
    ENG = {"pe": "tensor", "act": "scalar", "dve": "vector", "pool": "gpsimd", "sp": "sync"}

    def __init__(self, nc, ctx):
        self.nc = nc
        self.ctx = ctx
        self.sems = {}
        self.cnt = {e: 0 for e in self.CE}
        self.waited = {e: {} for e in self.ENG}
        for e in self.CE:
            self.sems[e] = ctx.enter_context(nc.semaphore("s_" + e))
        self.nsem = 0
        self.pe_pending = None

    def newsem(self, name):
        self.nsem += 1
        key = "d%d_%s" % (self.nsem, name)
        self.sems[key] = self.ctx.enter_context(self.nc.semaphore(key))
        return key

    def _waits(self, e, reads, writes):
        d = {}
        for b in reads:
            _merge(d, b.w)
        for b in writes:
            for src in (b.w, b.r):
                for k, v in src.items():
                    if k == e and b.nosync:
                        continue
                    if v > d.get(k, 0):
                        d[k] = v
        if e == "pe":
            d.pop("pe", None)
        elif d.get("pe", 0) > self.cnt["pe"]:
            assert self.pe_pending is not None and d["pe"] == self.cnt["pe"] + 1
            self.pe_pending.then_inc(self.sems["pe"], 1)
            self.cnt["pe"] += 1
            self.pe_pending = None
        out = []
        wd = self.waited[e]
        for k, v in d.items():
            if v > wd.get(k, 0):
                wd[k] = v
                out.append((k, v))
        return out

    def _emit(self, e, waits, fn, inc):
        eng = getattr(self.nc, self.ENG[e])
        for k, v in waits:
            eng.wait_ge(self.sems[k], v)
        if fn is None:
            return None
        ins = fn(eng)
        if inc is not None:
            ins.then_inc(self.sems[inc[0]], inc[1])
        return ins

    def op(self, e, fn, reads=(), writes=(), inc=True):
        exr = [b for b in reads if b.ex]
        if exr:
            writes = list(writes) + exr
        waits = self._waits(e, reads, writes)
        if inc:
            self.cnt[e] += 1
            tok = self.cnt[e]
        else:
            tok = self.cnt[e] + 1
        ins = self._emit(e, waits, fn, (e, 1) if inc else None)
        if e == "pe":
            self.pe_pending = None if inc else ins
        for b in reads:
            if tok > b.r.get(e, 0):
                b.r[e] = tok
        for b in writes:
            if tok > b.w.get(e, 0):
                b.w[e] = tok

    def dma(self, q, out, in_, reads=(), writes=(), sb=None, post_writes=(), **kw):
        if sb is None:
            sb = writes[0] if writes else (post_writes[0] if post_writes else reads[0])
        if sb.dsem is None:
            sb.dsem = self.newsem(sb.name)
        waits = self._waits(q, reads, writes)
        sb.dcnt += 16
        tok = sb.dcnt
        key = sb.dsem
        self._emit(q, waits, lambda eng: eng.dma_start(out=out, in_=in_, **kw), (key, 16))
        for b in reads:
            if tok > b.r.get(key, 0):
                b.r[key] = tok
        for b in list(writes) + list(post_writes):
            if tok > b.w.get(key, 0):
                b.w[key] = tok

    def inherit(self, new_bufs, old_bufs):
        d = {}
        for b in old_bufs:
            _merge(d, b.w)
            _merge(d, b.r)
        for b in new_bufs:
            _merge(b.r, d)

    def finish(self, final_bufs):
        d = {}
        for b in final_bufs:
            _merge(d, b.w)
        self._emit("sp", list(d.items()), None, None)


def build_program():
    nc = bass.Bass("TRN2", target_bir_lowering=False)

    def din(name, shape):
        return nc.dram_tensor(name, shape, F32, kind="ExternalInput")

    def dout(name, shape):
        return nc.dram_tensor(name, shape, F32, kind="ExternalOutput")

    xT_h = din("xT", [D, NTOK]); xT = xT_h.ap()
    xT16 = din("xT16", [D, S]).ap()
    xN = din("xN", [NTOK, D]).ap()
    c128 = din("c128", [16, 128, 512]).ap()
    c512 = din("c512", [16, 512, 512]).ap()
    c2048 = din("c2048", [16, 2048, 512]).ap()
    w_in = din("w_in", [D, 5376]).ap()
    w_ao = din("w_ao", [256, D]).ap()
    w_bo = din("w_bo", [512, D]).ap()
    w_out = din("w_out", [D, D]).ap()
    w_up = din("w_up", [D, 4096]).ap()
    w_down = din("w_down", [4096, D]).ap()
    gpre_d = din("gpre", [128, 8]).ap()
    bgate_d = din("bgate", [128, 16]).ap()
    gpm_d = din("gpm", [128, D]).ap()
    gffn_d = din("gffn", [128, 8]).ap()
    gpf_d = din("gpf", [128, D]).ap()
    lng_d = din("lng", [128, 512]).ap()
    lnb_d = din("lnb", [128, 512]).ap()
    wspT_d = din("wspT", [128, 512]).ap()
    bsp_d = din("bsp", [128, 512]).ap()
    wsS_d = din("wsS", [64, 256]).ap()
    bspS_d = din("bspS", [128, 256]).ap()

    y_d = dout("y", [NTOK, D]).ap()
    kvp128_d = dout("kvp128", [128, 512]).ap()
    kvp512_d = dout("kvp512", [512, 512]).ap()
    kvp2048_d = dout("kvp2048", [2048, 512]).ap()
    kvs_h = dout("kvs", [3, 64, 512]); kvs_d = kvs_h.ap()
    sg_d = dout("sg", [64, 512]).ap()

    qs_h = nc.dram_tensor("qs_scr", [64, 768], F32, kind="Internal")
    od_h = nc.dram_tensor("od_scr", [64, 3, 4, 257], F32, kind="Internal")

    with ExitStack() as ctx:
        P = Prog(nc, ctx)

        def sbt(name, shape, dt, c=ctx):
            return c.enter_context(nc.sbuf_tensor("sb_" + name, shape, dt))

        w_in_v = w_in.rearrange("(kc p) n -> p kc n", p=128)
        chunks = {}

        def add_chunk(name, F, parts):
            t = nc.dram_tensor("sc_" + name, [128, F], BF16, kind="Internal").ap()
            chunks[name] = {"ap": t, "F": F, "buf": Buf("sc_" + name), "parts": parts}

        for g in range(3):
            add_chunk("KV%d" % g, 4096, [(w_in_v[:, :, 768 + 256 * g: 1024 + 256 * g], 8, 512, 0, 256),
                                         (w_in_v[:, :, 1536 + 256 * g: 1792 + 256 * g], 8, 512, 256, 256)])
        add_chunk("Q01", 4096, [(w_in_v[:, :, 0:512], 8, 512, 0, 512)])
        add_chunk("Q2", 2048, [(w_in_v[:, :, 512:768], 8, 256, 0, 256)])
        add_chunk("Z1", 4096, [(w_in_v[:, :, 2304:2816], 8, 512, 0, 512)])
        add_chunk("Z2", 4096, [(w_in_v[:, :, 2816:3328], 8, 512, 0, 512)])
        for j in range(4):
            add_chunk("GP%d" % j, 4096, [(w_in_v[:, :, 3328 + 256 * j: 3584 + 256 * j], 8, 512, 0, 256),
                                         (w_in_v[:, :, 4352 + 256 * j: 4608 + 256 * j], 8, 512, 256, 256)])
        add_chunk("AO", 2048, [(w_ao.rearrange("(hp p) n -> p hp n", p=128), 2, 1024, 0, 1024)])
        add_chunk("BO", 4096, [(w_bo.rearrange("(kc p) n -> p kc n", p=128), 4, 1024, 0, 1024)])
        w_out_v = w_out.rearrange("(kc p) n -> p kc n", p=128)
        for j in range(2):
            add_chunk("OUT%d" % j, 4096, [(w_out_v[:, :, 512 * j: 512 * j + 512], 8, 512, 0, 512)])
        w_up_v = w_up.rearrange("(kc p) n -> p kc n", p=128)
        for j in range(8):
            add_chunk("UP%d" % j, 4096, [(w_up_v[:, :, 512 * j: 512 * j + 512], 8, 512, 0, 512)])
        w_dn_v = w_down.rearrange("(fc p) n -> p fc n", p=128)
        for hf in range(2):
            for j in range(4):
                add_chunk("DN%d_%d" % (hf, j), 4096,
                          [(w_dn_v[:, 8 * j: 8 * j + 8, 512 * hf: 512 * hf + 512], 8, 512, 0, 512)])

        cast_order = (["KV0", "KV1", "Q01", "KV2", "Q2", "Z2", "Z1", "AO", "BO"] + ["GP%d" % j for j in range(4)]
                      + ["OUT0", "OUT1"] + ["UP%d" % j for j in range(8)]
                      + ["DN%d_%d" % (hf, j) for hf in range(2) for j in range(4)])

        c_head = ["AO", "BO", "GP0", "GP1", "GP2", "GP3", "OUT0", "OUT1"]
        c_tail = ["UP%d" % j for j in range(8)] + ["DN%d_%d" % (hf, j) for hf in range(2) for j in range(4)]
        worder = ["KV0", "KV1", "Q01"] * 4 + ["KV0", "KV1", "KV2", "Q01", "Q2"] + ["KV2", "Q2"] * 4 + ["Z2", "Z1"]
        for i in range(5):
            worder += c_head + (["Z2", "Z1"] if i < 4 else []) + c_tail
        slots = [sbt("wslot%d" % i, [128, SLOTF], BF16) for i in range(NSLOT)]
        slot_b = [Buf("wslot%d" % i) for i in range(NSLOT)]
        slotc_b = [Buf("wslotc%d" % i) for i in range(NSLOT)]

        class WS:
            i_load = 0
            i_use = 0
            free = list(range(NSLOT))
            loaded = {}

        def w_pump():
            while WS.free and WS.i_load < len(worder):
                s = WS.free.pop(0)
                ch = chunks[worder[WS.i_load]]
                if not ch.get("cast_done"):
                    nm = worder[WS.i_load]
                    uses = ch.get("uses", 0)
                    ch["uses"] = uses + 1
                    writeback = not ((nm.startswith("UP") or nm.startswith("DN")) and uses == 0)
                    if writeback:
                        ch["cast_done"] = True
                    first = True
                    for (src, kc, n, off, w) in ch["parts"]:
                        dst = slots[s][:, 0:ch["F"]].rearrange("p (kc n) -> p kc n", kc=kc)[:, :, off:off + w]
                        if first:
                            P.dma("pool", dst, src, writes=[slot_b[s]], sb=slotc_b[s])
                        else:
                            P.dma("pool", dst, src, post_writes=[slot_b[s]], sb=slotc_b[s])
                        first = False
                    if writeback:
                        P.dma("sp", ch["ap"], slots[s][:, 0:ch["F"]], reads=[slot_b[s]], writes=[ch["buf"]], sb=ch["buf"])
                else:
                    P.dma("sp", slots[s][:, 0:ch["F"]], ch["ap"], reads=[ch["buf"]], writes=[slot_b[s]])
                WS.loaded[WS.i_load] = s
                WS.i_load += 1

        def w_acquire(name):
            w_pump()
            assert worder[WS.i_use] == name, (worder[WS.i_use], name)
            s = WS.loaded.pop(WS.i_use)
            WS.i_use += 1
            return s

        def w_release(s):
            WS.free.append(s)
            w_pump()

        _saved_free = WS.free[1:]
        WS.free = WS.free[:1]
        w_pump()
        WS.free = _saved_free


        NPS = 7
        ps_t = [ctx.enter_context(nc.psum_tensor("ps%d" % i, [128, 512], F32)) for i in range(NPS)]
        ps_b = [Buf("ps%d" % i, ex=True) for i in range(NPS)]
        psT_t = ctx.enter_context(nc.psum_tensor("psT", [128, 1024], BF16))
        psT_b = Buf("psT", ex=True)
        ps_rr = [0]

        def next_ps():
            i = ps_rr[0]
            ps_rr[0] = (i + 1) % NPS
            return ps_t[i], ps_b[i]

        def mm(out, lhsT, rhs, start, stop, reads, writes):
            P.op("pe", lambda e: e.matmul(out, lhsT=lhsT, rhs=rhs, start=start, stop=stop),
                 reads=reads, writes=writes, inc=stop)

        ident = sbt("ident", [128, 128], BF16); ident_b = Buf("ident")
        identf = sbt("identf", [128, 128], F32)
        maskc = sbt("maskc", [128, 128], BF16)
        maskp = sbt("maskp", [128, 128], BF16)
        ones = sbt("ones", [128, 128], BF16)
        epsT = sbt("epsT", [128, 1], F32)
        tmpc = sbt("tmpc", [128, 512], F32)
        gpre = sbt("gpre", [128, 8], F32)
        bgate = sbt("bgate", [128, 16], F32)
        gffn = sbt("gffn", [128, 8], F32)
        wspT = sbt("wspT", [128, 512], BF16)
        bsp_hi = sbt("bsp_hi", [128, 512], BF16)
        bsp_lo = sbt("bsp_lo", [128, 512], BF16)
        wsS = sbt("wsS", [64, 256], BF16)
        bspS = sbt("bspS", [128, 256], F32)
        cst = Buf("consts")

        P.op("pool", lambda e: e.memset(tmpc[:, 0:128], 0.0), writes=[cst])
        P.op("pool", lambda e: e.affine_select(out=tmpc[:, 0:128], in_=tmpc[:, 0:128], compare_op=ALU.not_equal, fill=1.0,
                                                base=0, pattern=[[-1, 128]], channel_multiplier=1), reads=[cst], writes=[cst])
        P.op("pool", lambda e: e.tensor_copy(out=ident[:], in_=tmpc[:, 0:128]), reads=[cst], writes=[cst])
        P.op("pool", lambda e: e.tensor_copy(out=identf[:], in_=tmpc[:, 0:128]), reads=[cst], writes=[cst])
        P.op("pool", lambda e: e.memset(tmpc[:, 0:128], 0.0), reads=[cst], writes=[cst])
        P.op("pool", lambda e: e.affine_select(out=tmpc[:, 0:128], in_=tmpc[:, 0:128], compare_op=ALU.is_ge, fill=-30000.0,
                                                base=0, pattern=[[1, 128]], channel_multiplier=-1), reads=[cst], writes=[cst])
        P.op("pool", lambda e: e.tensor_copy(out=maskc[:], in_=tmpc[:, 0:128]), reads=[cst], writes=[cst])
        P.op("pool", lambda e: e.memset(tmpc[:, 0:128], 0.0), reads=[cst], writes=[cst])
        P.op("pool", lambda e: e.affine_select(out=tmpc[:, 0:128], in_=tmpc[:, 0:128], compare_op=ALU.is_ge, fill=-30000.0,
                                                base=0, pattern=[[-1, 128]], channel_multiplier=1), reads=[cst], writes=[cst])
        P.op("pool", lambda e: e.tensor_copy(out=maskp[:], in_=tmpc[:, 0:128]), reads=[cst], writes=[cst])
        P.op("pool", lambda e: e.memset(ones[:], 1.0), writes=[cst])
        P.op("pool", lambda e: e.memset(epsT[:], EPS), writes=[cst])
        for (t, d_) in ((gpre, gpre_d), (bgate, bgate_d), (gffn, gffn_d), (bspS, bspS_d)):
            P.dma("sp", t[:], d_, writes=[cst])
        P.dma("sp", tmpc[:], wspT_d, reads=[cst], writes=[cst])
        for g4 in range(4):
            P.op("pool", lambda e, g4=g4: e.affine_select(out=tmpc[:, g4 * 128:(g4 + 1) * 128], in_=tmpc[:, g4 * 128:(g4 + 1) * 128],
                                                         compare_op=ALU.is_ge, fill=0.0, base=0, pattern=[[1, 128]],
                                                         channel_multiplier=-1), reads=[cst], writes=[cst])
        P.op("pool", lambda e: e.tensor_copy(out=wspT[:], in_=tmpc[:]), reads=[cst], writes=[cst])
        P.dma("sp", tmpc[:], bsp_d, reads=[cst], writes=[cst])
        P.op("pool", lambda e: e.tensor_copy(out=bsp_hi[:], in_=tmpc[:]), reads=[cst], writes=[cst])
        P.op("pool", lambda e: e.tensor_tensor(out=tmpc[:], in0=tmpc[:], in1=bsp_hi[:], op=ALU.subtract), reads=[cst], writes=[cst])
        P.op("pool", lambda e: e.tensor_copy(out=bsp_lo[:], in_=tmpc[:]), reads=[cst], writes=[cst])
        P.dma("sp", tmpc[0:64, 0:256], wsS_d, reads=[cst], writes=[cst])
        for g4 in range(4):
            P.op("pool", lambda e, g4=g4: e.affine_select(out=tmpc[0:64, g4 * 64:(g4 + 1) * 64], in_=tmpc[0:64, g4 * 64:(g4 + 1) * 64],
                                                         compare_op=ALU.is_ge, fill=0.0, base=0, pattern=[[1, 64]],
                                                         channel_multiplier=-1), reads=[cst], writes=[cst])
        P.op("pool", lambda e: e.tensor_copy(out=wsS[:], in_=tmpc[0:64, 0:256]), reads=[cst], writes=[cst])

        def emit_cast(name, dep=()):
            ch = chunks[name]
            for (src, kc, n, off, w) in ch["parts"]:
                dst = ch["ap"].rearrange("p (kc n) -> p kc n", kc=kc)[:, :, off:off + w]
                P.dma("pool", dst, src, reads=list(dep), post_writes=[ch["buf"]], sb=ch["buf"])

        deferred = []

        def emit_deferred(n, dep):
            for _ in range(n):
                if deferred:
                    emit_cast(deferred.pop(0), dep)

        xTs = sbt("xTs", [128, 8, NT], F32); xTs_b = Buf("xTs")
        sq = sbt("sq", [128, 4, NT], BF16); sq_b = Buf("sq")
        hTt = sbt("hTt", [128, 8, NT], BF16); hTt_b = Buf("hTt")
        sdt = sbt("sdt", [128, NT], F32); sdt_b = Buf("sdt")
        rstdb = sbt("rstdb", [128, NTOK], F32); rstdb_b = Buf("rstdb")
        OaT = sbt("OaT", [128, 2, NTOK], BF16); OaT_b = Buf("OaT")
        junk = sbt("junk", [128, 1024], BF16); junk_b = Buf("junk", nosync=True)

        def hT_load(src, col0, nt, mode=None, tok0=None, q="act"):
            srcv = src.rearrange("(kc p) n -> p kc n", p=128)
            P.dma(q, xTs[:, :, 0:nt], srcv[:, :, col0:col0 + nt], writes=[xTs_b])

        def hT_part1(src, col0, nt, mode, tok0=None, load=True):
            if load:
                hT_load(src, col0, nt)
            if mode == "reuse":
                return rstdb[:, tok0:tok0 + nt], rstdb_b
            if mode == "tmp":
                i16 = col0 // NT
                return rstdb[:, 0:S].rearrange("p (m r) -> p r m", r=16)[:, 4 * i16:4 * i16 + 4, :], rstdb_b
            pst, psb = next_ps()
            for hf in range(2):
                P.op("act", lambda e, hf=hf: e.activation(out=sq[:, :, 0:nt], in_=xTs[:, 4 * hf:4 * hf + 4, 0:nt], func=AF.Square),
                     reads=[xTs_b], writes=[sq_b])
                for j in range(4):
                    mm(pst[:, 0:nt], ones[:], sq[:, j, 0:nt], hf == 0 and j == 0, hf == 1 and j == 3, [sq_b, cst], [psb])
            P.op("act", lambda e: e.activation(out=sdt[:, 0:nt], in_=pst[:, 0:nt], func=AF.Sqrt, bias=epsT[:, 0:1], scale=1.0 / D),
                 reads=[psb, cst], writes=[sdt_b])
            if mode == "store":
                rs = rstdb[:, tok0:tok0 + nt]
                rs_b = rstdb_b
            else:
                rs = sdt[:, 0:nt]
                rs_b = sdt_b
            P.op("dve", lambda e: e.reciprocal(out=rs, in_=sdt[:, 0:nt]), reads=[sdt_b], writes=[rs_b])
            return rs, rs_b

        def hT_part2(nt, rs, rs_b, dst, dst_b):
            v3 = (len(rs.shape) == 3)
            for kc in range(8):
                o_ = dst[:, kc, 0:nt]
                i_ = xTs[:, kc, 0:nt]
                if v3:
                    o_ = o_.rearrange("p (r m) -> p r m", r=4)
                    i_ = i_.rearrange("p (r m) -> p r m", r=4)
                P.op("dve", lambda e, kc=kc, o_=o_, i_=i_: e.scalar_tensor_tensor(out=o_, in0=i_, scalar=gpre[:, kc:kc + 1],
                                                                                  in1=rs, op0=ALU.mult, op1=ALU.mult),
                     reads=[xTs_b, rs_b, cst], writes=[dst_b])

        def compute_hT(src, col0, nt, mode, tok0=None):
            rs, rs_b = hT_part1(src, col0, nt, mode, tok0)
            hT_part2(nt, rs, rs_b, hTt, hTt_b)

        cAB = ExitStack()
        qT = sbt("qT", [128, 6, S], BF16, cAB); qT_b = Buf("qT")
        kT = sbt("kT", [128, 6, S], BF16, cAB); kT_b = Buf("kT")
        VW = 48 * 260
        Vaug = sbt("Vaug", [128, VW], BF16, cAB); Vaug_b = Buf("Vaug")
        onesf = sbt("onesf", [128, 64], F32, cAB)
        Oacc = sbt("Oacc", [128, S], F32, cAB); Oacc_b = Buf("Oacc")
        rden = sbt("rden", [128, 512], F32, cAB); rden_b = Buf("rden")
        NPT = 3
        pT = [sbt("pT%d" % i, [128, 256], BF16, cAB) for i in range(NPT)]
        pT_b = [Buf("pT%d" % i) for i in range(NPT)]
        kvst = [sbt("kvst%d" % i, [128, 512], F32, cAB) for i in range(2)]
        kvst_b = [Buf("kvst%d" % i) for i in range(2)]
        kvs_s = sbt("kvs_s", [64, 3, 512], F32, cAB); kvs_sb = Buf("kvs_s")
        qs_s = sbt("qs_s", [64, 768], F32, cAB); qs_sb = Buf("qs_s")
        NKV = 7
        xflat = xTs[:].rearrange("p k t -> p (k t)")
        KVt = [xflat[:, i * 520:i * 520 + 513] for i in range(NKV)]
        KVt_b = [Buf("KVt%d" % i) for i in range(NKV)]
        qb = [sbt("qb%d" % i, [128, 768], F32, cAB) for i in range(2)]
        qb_b = [Buf("qb%d" % i) for i in range(2)]
        AB_bufs = [qT_b, kT_b, Vaug_b, Oacc_b, rden_b] + pT_b + kvst_b + [kvs_sb, qs_sb] + KVt_b + qb_b

        kvp128_b = Buf("kvp128"); kvp512_b = Buf("kvp512"); kvp2048_b = Buf("kvp2048")
        kvs_ob = Buf("kvs_o"); sg_ob = Buf("sg_o"); y_ob = Buf("y_o")
        qs_db = Buf("qs_d"); od_db = Buf("od_d")

        P.op("pool", lambda e: e.memset(Vaug[:, :], 1.0), writes=[Vaug_b])
        P.op("pool", lambda e: e.memset(onesf[:, :], 1.0), writes=[Vaug_b])

        def voff(g, blk):
            return (g * 16 + blk) * 260

        kvst_rr = [0]
        evac_rr = [0]

        def evac_copy(out, in_, reads, writes):
            evac_rr[0] ^= 1
            if evac_rr[0]:
                P.op("act", lambda e: e.activation(out=out, in_=in_, func=AF.Copy), reads=reads, writes=writes)
            else:
                P.op("dve", lambda e: e.tensor_copy(out=out, in_=in_), reads=reads, writes=writes)

        hT2 = Oacc[:].bitcast(BF16).rearrange("p (k t) -> p k t", k=8)
        hT2_b = Buf("hT2")
        hbufs = [(hTt, hTt_b), (hT2, hT2_b)]
        HC = {"t": hTt, "b": hTt_b}

        def featmajor_proj(s, colbase, ncols_per_kc, dst_fn):
            pst, psb = next_ps()
            for kc in range(8):
                mm(pst[:, 0:NT], slots[s][:, kc * ncols_per_kc + colbase: kc * ncols_per_kc + colbase + 128], HC["t"][:, kc, 0:NT],
                   kc == 0, kc == 7, [slot_b[s], HC["b"]], [psb])
            dst_fn(pst, psb)

        def kv_tok(s, lhs_fn, g, blk, out_dram=None, out_buf=None):
            pst, psb = next_ps()
            c0 = 0 if out_dram is not None else 256
            for kc in range(8):
                mm(pst[:, c0:512], lhs_fn(kc), slots[s][:, kc * 512 + c0:(kc + 1) * 512], kc == 0, kc == 7, [slot_b[s], HC["b"]], [psb])
            o = voff(g, blk)
            evac_copy(Vaug[:, o:o + 260].rearrange("p (h e) -> p h e", e=65)[:, :, 0:64],
                      pst[:, 256:512].rearrange("p (h d) -> p h d", d=64), [psb], [Vaug_b])
            if out_dram is not None and os.environ.get("MK_NOOUT") == "2" and g != 0:
                out_dram = None
            if out_dram is not None and os.environ.get("MK_NOOUT") == "3" and g == 0:
                out_dram = None
            if out_dram is not None and os.environ.get("MK_NOOUT") != "1":
                i = kvst_rr[0]
                kvst_rr[0] ^= 1
                evac_copy(kvst[i][:], pst[:, 0:512], [psb], [kvst_b[i]])
                P.dma(os.environ.get("MK_STQ", "pool"), out_dram, kvst[i][:], reads=[kvst_b[i]], post_writes=[out_buf], sb=kvst_b[i])

        if STAGE < 1:
            P.finish([y_ob, cst] + slot_b + [chunks[n]["buf"] for n in cast_order])
            cAB.close()
            return nc
        a_steps = [(xT, NT * i, NT, "store", NT * i) for i in range(4)] + [(xT, S, NS, "store", S)] + \
            [(xT16, NT * i, NT, "tmp", None) for i in range(4)]
        a_state = {"k": 0, "rs": None}

        def a_begin():
            k = a_state["k"]
            HC["t"], HC["b"] = hbufs[k % 2]
            if k == 0 and len(a_steps) > 1:
                hT_load(*a_steps[1])

        def a_mid():
            k = a_state["k"]
            if k + 1 < len(a_steps):
                rs, rs_b = hT_part1(*a_steps[k + 1], load=False)
                dst, dst_b = hbufs[(k + 1) % 2]
                hT_part2(a_steps[k + 1][2], rs, rs_b, dst, dst_b)
                if k + 2 < len(a_steps):
                    hT_load(*a_steps[k + 2], q="sp")
            a_state["k"] = k + 1

        rs0, rs0_b = hT_part1(*a_steps[0])
        hT_part2(NT, rs0, rs0_b, hTt, hTt_b)
        emit_deferred(1, [hTt_b])
        for i in range(4):
            tok0 = NT * i
            a_begin()
            if STAGE < 1.1:
                P.finish([y_ob, kvp128_b, kvp512_b, kvp2048_b, hTt_b, kT_b, qT_b, Vaug_b] + slot_b + ps_b)
                cAB.close()
                return nc
            s = w_acquire("KV0")
            for p in range(2):
                featmajor_proj(s, p * 128, 512, lambda pst, psb, p=p: evac_copy(kT[:, p, tok0:tok0 + NT], pst[:, 0:NT], [psb], [kT_b]))
            if STAGE < 1.2:
                P.finish([y_ob, kvp128_b, kvp512_b, kvp2048_b, hTt_b, kT_b, qT_b, Vaug_b] + slot_b + ps_b)
                cAB.close()
                return nc
            for b in range(4):
                last = (i == 3 and b == 3) or (os.environ.get('MK_FORCEOUT') and b == 3)
                kv_tok(s, lambda kc, b=b: HC["t"][:, kc, b * 128:(b + 1) * 128], 0, 4 * i + b,
                       kvp128_d if last else None, kvp128_b)
            w_release(s)
            a_mid()
            if STAGE < 1.3:
                P.finish([y_ob, kvp128_b, kvp512_b, kvp2048_b, hTt_b, kT_b, qT_b, Vaug_b] + slot_b + ps_b)
                cAB.close()
                return nc
            s = w_acquire("KV1")
            for p in range(2):
                featmajor_proj(s, p * 128, 512, lambda pst, psb, p=p: evac_copy(
                    kT[:, 2 + p, :].rearrange("q (r m) -> q r m", r=4)[:, :, 128 * i:128 * i + 128],
                    pst[:, 0:NT].rearrange("q (m r) -> q r m", r=4), [psb], [kT_b]))
            if STAGE < 1.4:
                P.finish([y_ob, kvp128_b, kvp512_b, kvp2048_b, hTt_b, kT_b, qT_b, Vaug_b] + slot_b + ps_b)
                cAB.close()
                return nc
            for r in range(4):
                kv_tok(s, lambda kc, r=r: HC["t"][:, kc, r:NT:4], 1, r * 4 + i,
                       kvp512_d[r:512:4, :] if i == 3 else None, kvp512_b)
            w_release(s)
            s = w_acquire("Q01")
            for p in range(2):
                featmajor_proj(s, p * 128, 512, lambda pst, psb, p=p: evac_copy(qT[:, p, tok0:tok0 + NT], pst[:, 0:NT], [psb], [qT_b]))
            for p in range(2):
                featmajor_proj(s, 256 + p * 128, 512, lambda pst, psb, p=p: evac_copy(
                    qT[:, 2 + p, :].rearrange("q (r m) -> q r m", r=4)[:, :, 128 * i:128 * i + 128],
                    pst[:, 0:NT].rearrange("q (m r) -> q r m", r=4), [psb], [qT_b]))
            w_release(s)

        a_begin()
        for g in range(3):
            s = w_acquire("KV%d" % g)
            pst, psb = next_ps()
            for kc in range(8):
                mm(pst[0:NS, 0:512], HC["t"][:, kc, 0:NS], slots[s][:, kc * 512:(kc + 1) * 512], kc == 0, kc == 7, [slot_b[s], HC["b"]], [psb])
            evac_copy(kvs_s[:, g, :], pst[0:NS, 0:512], [psb], [kvs_sb])
            w_release(s)
            if g == 0:
                a_mid()
        P.dma("pool", kvs_d.rearrange("g t n -> t g n"), kvs_s[:], reads=[kvs_sb], writes=[kvs_ob], sb=kvs_sb)
        s = w_acquire("Q01")
        pst, psb = next_ps()
        for kc in range(8):
            mm(pst[0:NS, 0:512], HC["t"][:, kc, 0:NS], slots[s][:, kc * 512:(kc + 1) * 512], kc == 0, kc == 7, [slot_b[s], HC["b"]], [psb])
        evac_copy(qs_s[:, 0:512], pst[0:NS, 0:512], [psb], [qs_sb])
        w_release(s)
        s = w_acquire("Q2")
        pst, psb = next_ps()
        for kc in range(8):
            mm(pst[0:NS, 0:256], HC["t"][:, kc, 0:NS], slots[s][:, kc * 256:(kc + 1) * 256], kc == 0, kc == 7, [slot_b[s], HC["b"]], [psb])
        evac_copy(qs_s[:, 512:768], pst[0:NS, 0:256], [psb], [qs_sb])
        w_release(s)
        P.dma("pool", qs_h.ap(), qs_s[:], reads=[qs_sb], writes=[qs_db], sb=qs_sb)

        for i in range(4):
            a_begin()
            s = w_acquire("KV2")
            for p in range(2):
                featmajor_proj(s, p * 128, 512, lambda pst, psb, p=p: evac_copy(kT[:, 4 + p, NT * i:NT * i + NT], pst[:, 0:NT], [psb], [kT_b]))
            for b in range(4):
                r = 4 * i + b
                kv_tok(s, lambda kc, b=b: HC["t"][:, kc, b * 128:(b + 1) * 128], 2, r, kvp2048_d[r:2048:16, :], kvp2048_b)
            w_release(s)
            a_mid()
            s = w_acquire("Q2")
            for p in range(2):
                featmajor_proj(s, p * 128, 256, lambda pst, psb, p=p: evac_copy(qT[:, 4 + p, NT * i:NT * i + NT], pst[:, 0:NT], [psb], [qT_b]))
            w_release(s)

        P.inherit(KVt_b, [xTs_b])
        P.inherit([Oacc_b], [hT2_b])
        for i in range(NKV):
            P.op("pool", lambda e, i=i: e.memset(KVt[i][:, 512:513], 1.0), writes=[KVt_b[i]])
        def vaug_ap(g, blk, h):
            o = voff(g, blk) + h * 65
            return Vaug[:, o:o + 65]

        pt_rr = [0]

        def attn_p1(u):
            h, g, qc, kc_cur, kc_prev, blk, blk_prev, tok_ap, first = u
            p = h // 2
            R0 = 64 * (h % 2)
            gi = 2 * g + p
            pst, psb = next_ps()
            W = 256 if kc_prev is not None else 128
            mm(pst[:, 0:128], kT[R0:R0 + 64, gi, kc_cur:kc_cur + 128], qT[R0:R0 + 64, gi, qc:qc + 128], True, False, [kT_b, qT_b], [psb])
            mm(pst[:, 0:128], ident[:], maskc[:], False, True, [cst], [psb])
            if kc_prev is not None:
                mm(pst[:, 128:256], kT[R0:R0 + 64, gi, kc_prev:kc_prev + 128], qT[R0:R0 + 64, gi, qc:qc + 128], True, False, [kT_b, qT_b], [psb])
                mm(pst[:, 128:256], ident[:], maskp[:], False, True, [cst], [psb])
            i = pt_rr[0]
            pt_rr[0] = (i + 1) % NPT
            P.op("act", lambda e: e.activation(out=pT[i][:, 0:W], in_=pst[:, 0:W], func=AF.Exp, scale=0.125), reads=[psb], writes=[pT_b[i]])
            return i

        def attn_p2(u, i):
            h, g, qc, kc_cur, kc_prev, blk, blk_prev, tok_ap, first = u
            pso, psob = next_ps()
            mm(pso[0:65, 0:128], vaug_ap(g, blk, h), pT[i][:, 0:128], True, kc_prev is None, [Vaug_b, pT_b[i]], [psob])
            if kc_prev is not None:
                mm(pso[0:65, 0:128], vaug_ap(g, blk_prev, h), pT[i][:, 128:256], False, True, [Vaug_b, pT_b[i]], [psob])
            if first:
                P.op("dve", lambda e: e.tensor_copy(out=tok_ap, in_=pso[0:65, 0:128]), reads=[psob], writes=[Oacc_b])
            else:
                P.op("dve", lambda e: e.tensor_tensor(out=tok_ap, in0=tok_ap, in1=pso[0:65, 0:128], op=ALU.add), reads=[psob, Oacc_b], writes=[Oacc_b])

        caches = (c128, c512, c2048)
        combos = [(b, t, g) for b in range(16) for t in range(4) for g in range(3)]
        if STAGE == 4:
            combos = []
        NCB = len(combos)
        NPR, NVB, NS4, NOS = 3, 4, 3, 2
        prod = [sbt("prod%d" % i, [128, 256], F32, cAB) for i in range(NPR)]
        prod_b = [Buf("prod%d" % i) for i in range(NPR)]
        Vb = [sbt("Vb%d" % i, [128, 257], BF16, cAB) for i in range(NVB)]
        Vb_b = [Buf("Vb%d" % i) for i in range(NVB)]
        s4 = [sbt("s4_%d" % i, [128, 4], F32, cAB) for i in range(NS4)]
        s4_b = [Buf("s4_%d" % i) for i in range(NS4)]
        p4 = [sbt("p4_%d" % i, [128, 4], BF16, cAB) for i in range(NS4)]
        p4_b = [Buf("p4_%d" % i) for i in range(NS4)]
        ost = [sbt("ost%d" % i, [4, 257], F32, cAB) for i in range(NOS)]
        ost_b = [Buf("ost%d" % i) for i in range(NOS)]
        AB_bufs += prod_b + Vb_b + s4_b + p4_b + ost_b
        ost_ap = [t[:] for t in ost] + [kvst[0][0:4, 0:257], kvst[1][0:4, 0:257], sdt[0:4, 0:257]]
        ost_b = ost_b + [kvst_b[0], kvst_b[1], sdt_b]
        NOS = len(ost_ap)
        ost_semb = [Buf("ostsem%d" % i) for i in range(NOS)]

        hflat = hTt[:].rearrange("p k t -> p (k t)").bitcast(F32)
        qb_ap = [qb[0][:], qb[1][:], hflat[:, 0:768], hflat[:, 1024:1792]]
        qbx_b = [Buf("qbx0"), Buf("qbx1")]
        P.inherit(qbx_b, [hTt_b])
        qb_bb = [qb_b[0], qb_b[1]] + qbx_b
        NQB = 4
        LEAD = 6

        def cb_A(k):
            b, t, g = combos[k]
            tok = 4 * b + t
            qi = (k // 3) % NQB
            if g == 0:
                P.dma("sp", qb_ap[qi], bass.AP(qs_h, tok * 768, [[0, 128], [1, 768]]), reads=[qs_db], writes=[qb_bb[qi]])
            ki = k % NKV
            if g == 0:
                if t == 0:
                    P.dma("sp", KVt[ki][:, 0:512], c128[b, :, :], writes=[KVt_b[ki]])
                else:
                    P.dma("sp", KVt[ki][16:128, 0:512], c128[b, 16:128, :], writes=[KVt_b[ki]])
                    P.dma("sp", KVt[ki][t:16, 0:512], c128[b, t:16, :], post_writes=[KVt_b[ki]])
                    P.dma("sp", KVt[ki][0:t, 0:512], kvs_d[0, 4 * b:4 * b + t, :], reads=[kvs_ob], post_writes=[KVt_b[ki]])
            elif g == 1:
                P.dma("sp", KVt[ki][:, 0:512], c512[b, t:512:4, :], writes=[KVt_b[ki]])
            else:
                P.dma("sp", KVt[ki][:, 0:512], c2048[b, t:2048:16, :], writes=[KVt_b[ki]])

        def cb_B(k):
            b, t, g = combos[k]
            qi = (k // 3) % NQB
            ki = k % NKV
            j = k % NPR
            v = k % NVB
            P.op("dve", lambda e: e.tensor_tensor(out=prod[j][:], in0=KVt[ki][:, 0:256], in1=qb_ap[qi][:, g * 256:(g + 1) * 256], op=ALU.mult),
                 reads=[KVt_b[ki], qb_bb[qi]], writes=[prod_b[j]])
            P.op("pool", lambda e: e.tensor_copy(out=Vb[v][:], in_=KVt[ki][:, 256:513]), reads=[KVt_b[ki]], writes=[Vb_b[v]])

        def cb_C(k):
            j = k % NPR
            q = k % NS4
            P.op("dve", lambda e: e.tensor_reduce(out=s4[q][:], in_=prod[j][:].rearrange("p (h d) -> p h d", h=4), axis=AX.X, op=ALU.add),
                 reads=[prod_b[j]], writes=[s4_b[q]])

        def cb_D(k):
            q = k % NS4
            P.op("act", lambda e: e.activation(out=p4[q][:], in_=s4[q][:], func=AF.Exp, scale=0.125), reads=[s4_b[q]], writes=[p4_b[q]])

        def cb_E(k):
            b, t, g = combos[k]
            tok = 4 * b + t
            q = k % NS4
            v = k % NVB
            o = k % NOS
            pst, psb = next_ps()
            mm(pst[0:4, 0:257], p4[q][:, 0:4], Vb[v][:, 0:257], True, True, [p4_b[q], Vb_b[v]], [psb])
            P.op("act", lambda e: e.activation(out=ost_ap[o], in_=pst[0:4, 0:257], func=AF.Copy), reads=[psb], writes=[ost_b[o]])
            P.dma("act", od_h.ap()[tok, g, :, :], ost_ap[o], reads=[ost_b[o]], post_writes=[od_db], sb=ost_semb[o])

        combo_i = [0]

        def emit_combo():
            k = combo_i[0]
            if k >= NCB + LEAD + 3:
                return
            combo_i[0] += 1
            for lag, fn in ((0, cb_A), (LEAD, cb_B), (LEAD + 1, cb_C), (LEAD + 2, cb_D), (LEAD + 3, cb_E)):
                kk = k - lag
                if 0 <= kk < NCB:
                    fn(kk)
            if k % 8 == 7 and k >= LEAD:
                emit_deferred(1, [Vb_b[(k - LEAD) % NVB]])

        SK = 2
        for h in range(4):
            p = h // 2
            units = []
            for r in range(16):
                units.append((h, 2, r * 128, r * 128, None, r, None, Oacc[0:65, r:S:16], True))
            for r in range(4):
                for j in range(4):
                    units.append((h, 1, r * 512 + 128 * j, r * 512 + 128 * j, (r * 512 + 128 * (j - 1)) if j > 0 else None,
                                  r * 4 + j, r * 4 + j - 1, Oacc[0:65, 512 * j + r:512 * j + 512:4], False))
            for j in range(16):
                units.append((h, 0, 128 * j, 128 * j, 128 * (j - 1) if j > 0 else None, j, j - 1, Oacc[0:65, 128 * j:128 * j + 128], False))
            pend = []
            for u in units:
                pend.append((u, attn_p1(u)))
                if len(pend) > SK:
                    attn_p2(*pend.pop(0))
                    emit_combo()
            while pend:
                attn_p2(*pend.pop(0))
                emit_combo()
            Ro = slice(0, 64) if h % 2 == 0 else slice(64, 128)
            for c4 in range(4):
                cs = slice(512 * c4, 512 * c4 + 512)
                pst, psb = next_ps()
                mm(pst[0:64, 0:512], onesf[64:65, 0:64], Oacc[64:65, cs], True, True, [Oacc_b, Vaug_b], [psb])
                P.op("dve", lambda e, pst=pst: e.reciprocal(out=rden[0:64, :], in_=pst[0:64, 0:512]), reads=[psb], writes=[rden_b])
                P.op("dve", lambda e, cs=cs, Ro=Ro: e.tensor_tensor(out=OaT[Ro, p, cs], in0=Oacc[0:64, cs], in1=rden[0:64, :], op=ALU.mult),
                     reads=[Oacc_b, rden_b], writes=[OaT_b])
        while combo_i[0] < NCB + LEAD + 3:
            emit_combo()
        emit_deferred(100, [])

        if STAGE < 6:
            P.finish([y_ob, kvp128_b, kvp512_b, kvp2048_b, kvs_ob, sg_ob, qs_db, od_db, OaT_b] + slot_b)
            cAB.close()
            return nc
        if os.environ.get("MK_DBG"):
            print("SBUF remaining in AB region (before finalisation tensors):", nc.sbuf_bytes_remaining)
        pv = sbt("pv", [64, 768], F32, cAB); pv_b = Buf("pv")
        dn = sbt("dn", [64, 12], F32, cAB); dn_b = Buf("dn")
        sprod = sbt("sprod", [64, 768], F32, cAB); sprod_b = Buf("sprod")
        sself = sbt("sself", [64, 12], F32, cAB); sself_b = Buf("sself")
        pself = sbt("pself", [64, 12], F32, cAB); pself_b = Buf("pself")
        Osm = sbt("Osm", [64, 256], F32, cAB); Osm_b = Buf("Osm")
        den4 = sbt("den4", [64, 4], F32, cAB); den4_b = Buf("den4")
        AB_bufs += [pv_b, dn_b, sprod_b, sself_b, pself_b, Osm_b, den4_b]
        for g in range(3):
            P.dma("sp", pv[:, g * 256:(g + 1) * 256].rearrange("t (h d) -> t h d", h=4),
                  bass.AP(od_h, g * 1028, [[3084, 64], [321, 4], [1, 64]]), reads=[od_db], writes=[pv_b])
            P.dma("sp", dn[:, g * 4:(g + 1) * 4],
                  bass.AP(od_h, g * 1028 + 256, [[3084, 64], [257, 4]]), reads=[od_db], writes=[dn_b], allow_slow_non_contiguous=True)
        P.op("dve", lambda e: e.tensor_tensor(out=sprod[:].rearrange("t (g n) -> t g n", g=3), in0=qs_s[:].rearrange("t (g n) -> t g n", g=3),
                                              in1=kvs_s[:, :, 0:256], op=ALU.mult), reads=[qs_sb, kvs_sb], writes=[sprod_b])
        P.op("dve", lambda e: e.tensor_reduce(out=sself[:], in_=sprod[:].rearrange("t (a d) -> t a d", d=64), axis=AX.X, op=ALU.add),
             reads=[sprod_b], writes=[sself_b])
        P.op("act", lambda e: e.activation(out=pself[:], in_=sself[:], func=AF.Exp, scale=0.125), reads=[sself_b], writes=[pself_b])
        P.op("dve", lambda e: e.tensor_tensor(out=sprod[:].rearrange("t (g h d) -> t g h d", g=3, h=4),
                                              in0=kvs_s[:, :, 256:512].rearrange("t g (h d) -> t g h d", h=4),
                                              in1=bass.AP(pself, 0, [[12, 64], [4, 3], [1, 4], [0, 64]]), op=ALU.mult),
             reads=[kvs_sb, pself_b, sprod_b], writes=[sprod_b])
        P.op("dve", lambda e: e.tensor_tensor(out=pv[:], in0=pv[:], in1=sprod[:], op=ALU.add), reads=[pv_b, sprod_b], writes=[pv_b])
        P.op("dve", lambda e: e.tensor_tensor(out=Osm[:], in0=pv[:, 0:256], in1=pv[:, 256:512], op=ALU.add), reads=[pv_b], writes=[Osm_b])
        P.op("dve", lambda e: e.tensor_tensor(out=Osm[:], in0=Osm[:], in1=pv[:, 512:768], op=ALU.add), reads=[pv_b, Osm_b], writes=[Osm_b])
        P.op("dve", lambda e: e.tensor_tensor(out=dn[:], in0=dn[:], in1=pself[:], op=ALU.add), reads=[dn_b, pself_b], writes=[dn_b])
        P.op("dve", lambda e: e.tensor_tensor(out=den4[:], in0=dn[:, 0:4], in1=dn[:, 4:8], op=ALU.add), reads=[dn_b], writes=[den4_b])
        P.op("dve", lambda e: e.tensor_tensor(out=den4[:], in0=den4[:], in1=dn[:, 8:12], op=ALU.add), reads=[dn_b, den4_b], writes=[den4_b])
        P.op("dve", lambda e: e.reciprocal(out=den4[:], in_=den4[:]), reads=[den4_b], writes=[den4_b])
        P.op("dve", lambda e: e.tensor_tensor(out=Osm[:].rearrange("t (h d) -> t h d", h=4), in0=Osm[:].rearrange("t (h d) -> t h d", h=4),
                                              in1=bass.AP(den4, 0, [[4, 64], [1, 4], [0, 64]]), op=ALU.mult),
             reads=[Osm_b, den4_b], writes=[Osm_b])
        for pp in range(2):
            pst, psb = next_ps()
            mm(pst[:, 0:64], Osm[0:64, pp * 128:(pp + 1) * 128], identf[0:64, 0:64], True, True, [Osm_b, cst], [psb])
            evac_copy(OaT[:, pp, S:S + NS], pst[:, 0:64], [psb], [OaT_b])

        if STAGE < 7:
            P.finish([y_ob, kvp128_b, kvp512_b, kvp2048_b, kvs_ob, sg_ob, qs_db, od_db, OaT_b] + slot_b)
            cAB.close()
            return nc
        cAB.close()

        cC = ExitStack()
        xNt = sbt("xNt", [128, 4, D], F32, cC); xNt_b = Buf("xNt")
        fst = sbt("fst", [128, 4, 512], F32, cC); fst_b = Buf("fst")
        fsl_b = [Buf("fsl%d" % i) for i in range(4)]
        mT = sbt("mT", [128, 8, NT], BF16, cC); mT_b = Buf("mT")
        h2b = sbt("h2b", [128, D], BF16, cC); h2b_b = Buf("h2b")
        h2T = sbt("h2T", [128, 8, NT], BF16, cC); h2T_b = Buf("h2T")
        uT = sbt("uT", [128, 32, NT], BF16, cC); uT_b = Buf("uT")
        rt = [sbt("rt%d" % i, [128, NT], BF16, cC) for i in range(2)]
        rt_b = [Buf("rt%d" % i) for i in range(2)]
        gA = [sbt("gA%d" % i, [128, NT], F32, cC) for i in range(2)]
        gA_b = [Buf("gA%d" % i) for i in range(2)]
        gB = [sbt("gB%d" % i, [128, NT], F32, cC) for i in range(2)]
        gB_b = [Buf("gB%d" % i) for i in range(2)]
        obt = sbt("obt", [128, 4, NT], BF16, cC); obt_b = Buf("obt")
        z2f = [fst[:, i, :] for i in range(2)]
        z2f_b = [fsl_b[0], fsl_b[1]]
        hTt2 = sbt("hTt2", [128, 8, NT], BF16, cC); hTt2_b = Buf("hTt2")
        hC = [(hTt, hTt_b), (hTt2, hTt2_b)]
        znb = [sbt("znb%d" % i, [128, 512], BF16, cC) for i in range(2)]
        znb_b = [Buf("znb%d" % i) for i in range(2)]
        tmpf = [sbt("tmpf%d" % i, [128, 512], F32, cC) for i in range(2)]
        tmpf_b = [Buf("tmpf%d" % i) for i in range(2)]
        st = [sbt("st%d" % i, [128, 8], F32, cC) for i in range(2)]
        st_b = [Buf("st%d" % i) for i in range(2)]
        gpm = sbt("gpm", [128, D], F32, cC)
        gpf = sbt("gpf", [128, D], F32, cC)
        lng = sbt("lng", [128, 512], F32, cC)
        lnb = sbt("lnb", [128, 512], F32, cC)
        cst2 = Buf("consts2")
        xNb = [Buf("xNb%d" % i) for i in range(4)]
        C_bufs = xNb + [xNt_b, fst_b, mT_b] + fsl_b + [h2b_b, h2T_b, uT_b, obt_b, cst2] + rt_b + gA_b + gB_b + z2f_b + znb_b + tmpf_b + st_b
        P.inherit(C_bufs, AB_bufs)
        P.inherit([xTs_b], KVt_b)
        P.inherit([hTt_b], qbx_b)
        for (t, d_) in ((gpm, gpm_d), (gpf, gpf_d), (lng, lng_d), (lnb, lnb_d)):
            P.dma("sp", t[:], d_, writes=[cst2])

        rr = {"z": 0, "g": 0, "t": 0, "r": 0, "s": 0}

        def nxt(k, n=2):
            v = rr[k]
            rr[k] = (v + 1) % n
            return v

        def rstd_from_sums(si, cols, n, bs):
            stt, stb = st[si], st_b[si]
            if len(cols) == 2:
                P.op("dve", lambda e: e.tensor_tensor(out=stt[0:bs, 5:6], in0=stt[0:bs, cols[0]:cols[0] + 1], in1=stt[0:bs, cols[1]:cols[1] + 1], op=ALU.add),
                     reads=[stb], writes=[stb])
                c = 5
            else:
                c = cols[0]
            P.op("act", lambda e: e.activation(out=stt[0:bs, 6:7], in_=stt[0:bs, c:c + 1], func=AF.Sqrt, bias=epsT[0:bs, 0:1], scale=1.0 / n),
                 reads=[stb, cst], writes=[stb])
            P.op("dve", lambda e: e.reciprocal(out=stt[0:bs, 7:8], in_=stt[0:bs, 6:7]), reads=[stb], writes=[stb])
            return stt[0:bs, 7:8]

        z2f_all = [fst[:, i, :] for i in range(4)]
        z2f_allb = list(fsl_b)
        P.inherit([hTt2_b], AB_bufs)
        st4 = [sbt("st4_%d" % i, [128, 8], F32, cC) for i in range(4)]
        st4_b = [Buf("st4_%d" % i) for i in range(4)]
        st5 = [sbt("st5_%d" % i, [128, 8], F32, cC) for i in range(4)]
        st5_b = [Buf("st5_%d" % i) for i in range(4)]
        P.inherit(st5_b, AB_bufs)
        P.inherit(z2f_allb[2:] + st4_b, AB_bufs)
        znb4 = [sbt("znb4_%d" % i, [128, 512], BF16, cC) for i in range(2)]
        znb_all = znb + znb4
        znb_allb = znb_b + [Buf("znb4_%d" % i) for i in range(2)]
        P.inherit(znb_allb[2:], AB_bufs)
        h2b2 = sbt("h2b2", [128, D], BF16, cC)
        h2bs = [h2b, h2b2]
        h2bs_b = [h2b_b, Buf("h2b2")]
        P.inherit([h2bs_b[1]], AB_bufs)

        if os.environ.get("MK_DBG"):
            print("SBUF remaining in C region:", nc.sbuf_bytes_remaining)

        ysem_b = Buf("ysem")
        uTf_b = [Buf("uTf%d" % i) for i in range(8)]

        def gm_front_a(tok0, nt, nblk, bs, H):
            sample = (nblk == 1)
            s = w_acquire("Z2")
            zps = []
            for b in range(nblk):
                pst, psb = next_ps()
                for kc in range(8):
                    mm(pst[0:bs, 0:512], H[0][:, kc, b * bs:(b + 1) * bs], slots[s][:, kc * 512:(kc + 1) * 512], kc == 0, kc == 7,
                       [slot_b[s], H[1]], [psb])
                zps.append((pst, psb))
            w_release(s)
            zts = [(z2f_all[b], z2f_allb[b], st4[b], st4_b[b]) for b in range(nblk)]
            for b in range(nblk):
                P.op("pool", lambda e, b=b: e.memset(zts[b][2][:], 0.0), writes=[zts[b][3]])
            for b in range(nblk):
                pst, psb = zps[b]
                zt, ztb, stt, stb = zts[b]
                P.op("act", lambda e, zt=zt, pst=pst, stt=stt: e.activation(out=zt[0:bs, :], in_=pst[0:bs, 0:512], func=AF.Gelu, accum_out=stt[0:bs, 0:1]),
                     reads=[psb, stb], writes=[ztb, stb])
                P.op("act", lambda e, zt=zt, stt=stt: e.activation(out=junk[0:bs, 0:512], in_=zt[0:bs, :], func=AF.Square, accum_out=stt[0:bs, 1:2]),
                     reads=[ztb, stb], writes=[junk_b, stb])
            return zts

        def gm_front_b(tok0, nt, nblk, bs, H, zts):
            sample = (nblk == 1)
            s = w_acquire("Z1")
            for c in range(4):
                pst, psb = next_ps()
                for kc in range(8):
                    mm(pst[:, 0:nt], slots[s][:, kc * 512 + c * 128: kc * 512 + c * 128 + 128], H[0][:, kc, 0:nt], kc == 0, kc == 7,
                       [slot_b[s], H[1]], [psb])
                P.op("act", lambda e, c=c, pst=pst: e.activation(out=obt[:, c, 0:nt], in_=pst[:, 0:nt], func=AF.Gelu), reads=[psb], writes=[obt_b])
            w_release(s)
            for b in range(nblk):
                zt, ztb, stt, stb = zts[b]
                P.op("dve", lambda e, stt=stt: e.tensor_scalar(out=stt[0:bs, 2:3], in0=stt[0:bs, 0:1], scalar1=1.0 / 512, scalar2=None, op0=ALU.mult),
                     reads=[stb], writes=[stb])
                P.op("dve", lambda e, stt=stt: e.tensor_tensor(out=stt[0:bs, 3:4], in0=stt[0:bs, 2:3], in1=stt[0:bs, 2:3], op=ALU.mult),
                     reads=[stb], writes=[stb])
                P.op("dve", lambda e, stt=stt: e.scalar_tensor_tensor(out=stt[0:bs, 4:5], in0=stt[0:bs, 1:2], scalar=1.0 / 512, in1=stt[0:bs, 3:4],
                                                                      op0=ALU.mult, op1=ALU.subtract), reads=[stb], writes=[stb])
            for b in range(nblk):
                zt, ztb, stt, stb = zts[b]
                P.op("act", lambda e, stt=stt: e.activation(out=stt[0:bs, 6:7], in_=stt[0:bs, 4:5], func=AF.Sqrt, bias=epsT[0:bs, 0:1], scale=1.0),
                     reads=[stb, cst], writes=[stb])
            for b in range(nblk):
                zt, ztb, stt, stb = zts[b]
                P.op("dve", lambda e, stt=stt: e.reciprocal(out=stt[0:bs, 7:8], in_=stt[0:bs, 6:7]), reads=[stb], writes=[stb])
                P.op("dve", lambda e, zt=zt, stt=stt: e.tensor_scalar(out=zt[0:bs, :], in0=zt[0:bs, :], scalar1=stt[0:bs, 2:3], scalar2=stt[0:bs, 7:8],
                                                                      op0=ALU.subtract, op1=ALU.mult), reads=[ztb, stb], writes=[ztb])
                P.op("dve", lambda e, zt=zt: e.tensor_tensor(out=zt[0:bs, :], in0=zt[0:bs, :], in1=lng[0:bs, :], op=ALU.mult), reads=[ztb, cst2], writes=[ztb])
                zb, zbb = znb_all[b], znb_allb[b]
                if not sample:
                    P.op("pool", lambda e, zt=zt, zb=zb: e.tensor_tensor(out=zb[0:bs, :], in0=zt[0:bs, :], in1=lnb[0:bs, :], op=ALU.add), reads=[ztb, cst2], writes=[zbb])
                else:
                    P.op("pool", lambda e, zt=zt: e.tensor_tensor(out=zt[0:bs, :], in0=zt[0:bs, :], in1=lnb[0:bs, :], op=ALU.add), reads=[ztb, cst2], writes=[ztb])
                    P.op("pool", lambda e, zt=zt, zb=zb: e.tensor_copy(out=zb[0:bs, :], in_=zt[0:bs, :]), reads=[ztb], writes=[zbb])
                    P.dma("pool", sg_d, zt[0:bs, :], reads=[ztb], writes=[sg_ob], sb=ztb)

        def gm_back(tok0, nt, nblk, bs):
            sample = (nblk == 1)
            for b in range(nblk):
                zb, zbb = znb_all[b], znb_allb[b]
                pm, pmb = next_ps()
                if not sample:
                    for g4 in range(4):
                        cs = slice(g4 * 128, (g4 + 1) * 128)
                        mm(pm[:, cs], zb[:, cs], wspT[:, cs], True, False, [zbb, cst], [pmb])
                        mm(pm[:, cs], ident[:], bsp_hi[:, cs], False, False, [cst], [pmb])
                        mm(pm[:, cs], ident[:], bsp_lo[:, cs], False, True, [cst], [pmb])
                    ov = obt[:, :, b * 128:(b + 1) * 128]
                    P.op("dve", lambda e, ov=ov, pm=pm: e.tensor_tensor(out=ov, in0=ov, in1=pm[:, 0:512].rearrange("p (g t) -> p g t", g=4), op=ALU.mult),
                         reads=[pmb, obt_b], writes=[obt_b])
                else:
                    for g4 in range(4):
                        mm(pm[:, g4 * 64:(g4 + 1) * 64], zb[0:64, g4 * 128:(g4 + 1) * 128], wsS[0:64, g4 * 64:(g4 + 1) * 64], True, True, [zbb, cst], [pmb])
                    ti = nxt("t")
                    P.op("dve", lambda e, pm=pm, ti=ti: e.tensor_tensor(out=tmpf[ti][:, 0:256], in0=pm[:, 0:256], in1=bspS[:, 0:256], op=ALU.add),
                         reads=[pmb, cst], writes=[tmpf_b[ti]])
                    ov = obt[:, :, 0:64]
                    P.op("dve", lambda e, ov=ov, ti=ti: e.tensor_tensor(out=ov, in0=ov, in1=tmpf[ti][:, 0:256].rearrange("p (g t) -> p g t", g=4), op=ALU.mult),
                         reads=[tmpf_b[ti], obt_b], writes=[obt_b])

        def phase_c(tok0, nt, nblk, bs, H, nxt_tile):
            sample = (nblk == 1)
            for b in range(nblk):
                P.dma("sp", xNt[0:bs, b, :], xN[tok0 + b * bs: tok0 + (b + 1) * bs, :], writes=[xNb[b]])
            nx_rs = None
            if nxt_tile is not None:
                nx_rs = hT_part1(xT, nxt_tile[0], nxt_tile[1], "reuse", nxt_tile[0])
            sA = w_acquire("AO")
            sB = w_acquire("BO")
            for j in range(4):
                sG = w_acquire("GP%d" % j)
                for cc in range(2):
                    c = 2 * j + cc
                    pga, pgab = next_ps()
                    for kc in range(8):
                        mm(pga[:, 0:nt], slots[sG][:, kc * 512 + cc * 128: kc * 512 + cc * 128 + 128], H[0][:, kc, 0:nt], kc == 0, kc == 7,
                           [slot_b[sG], H[1]], [pgab])
                    pgb, pgbb = next_ps()
                    for kc in range(8):
                        mm(pgb[:, 0:nt], slots[sG][:, kc * 512 + 256 + cc * 128: kc * 512 + 256 + cc * 128 + 128], H[0][:, kc, 0:nt], kc == 0, kc == 7,
                           [slot_b[sG], H[1]], [pgbb])
                    pa, pab = next_ps()
                    for pp in range(2):
                        mm(pa[:, 0:nt], slots[sA][:, pp * 1024 + c * 128: pp * 1024 + c * 128 + 128], OaT[:, pp, tok0:tok0 + nt], pp == 0, pp == 1,
                           [slot_b[sA], OaT_b], [pab])
                    pb, pbb = next_ps()
                    for kc in range(4):
                        mm(pb[:, 0:nt], slots[sB][:, kc * 1024 + c * 128: kc * 1024 + c * 128 + 128], obt[:, kc, 0:nt], kc == 0, kc == 3,
                           [slot_b[sB], obt_b], [pbb])
                    gi = nxt("g")
                    P.op("act", lambda e, gi=gi, pga=pga, c=c: e.activation(out=gA[gi][:, 0:nt], in_=pga[:, 0:nt], func=AF.Sigmoid, bias=bgate[:, c:c + 1], scale=1.0),
                         reads=[pgab, cst], writes=[gA_b[gi]])
                    P.op("act", lambda e, gi=gi, pgb=pgb, c=c: e.activation(out=gB[gi][:, 0:nt], in_=pgb[:, 0:nt], func=AF.Sigmoid, bias=bgate[:, 8 + c:9 + c], scale=1.0),
                         reads=[pgbb, cst], writes=[gB_b[gi]])
                    P.op("dve", lambda e, gi=gi, pa=pa: e.tensor_tensor(out=gA[gi][:, 0:nt], in0=gA[gi][:, 0:nt], in1=pa[:, 0:nt], op=ALU.mult),
                         reads=[gA_b[gi], pab], writes=[gA_b[gi]])
                    P.op("dve", lambda e, gi=gi, pb=pb: e.tensor_tensor(out=gB[gi][:, 0:nt], in0=gB[gi][:, 0:nt], in1=pb[:, 0:nt], op=ALU.mult),
                         reads=[gB_b[gi], pbb], writes=[gB_b[gi]])
                    P.op("pool", lambda e, gi=gi, c=c: e.tensor_tensor(out=mT[:, c, 0:nt], in0=gA[gi][:, 0:nt], in1=gB[gi][:, 0:nt], op=ALU.add),
                         reads=[gA_b[gi], gB_b[gi]], writes=[mT_b])
                w_release(sG)
                if j == 0 and nxt_tile is not None:
                    hT_part2(nxt_tile[1], nx_rs[0], nx_rs[1], nxt_tile[4][0], nxt_tile[4][1])
            w_release(sA)
            w_release(sB)
            so = [w_acquire("OUT0"), w_acquire("OUT1")]
            blk_state = {}

            uTf = uT[:].rearrange("p f t -> p (f t)").bitcast(F32)
            P.inherit(uTf_b, [uT_b])
            sts = [(st5[b], st5_b[b]) for b in range(nblk)]

            def pre(b, hf):
                i0 = 2 * b + hf
                return uTf[0:bs, i0 * 512:(i0 + 1) * 512], uTf_b[i0]

            for b in range(nblk):
                stt, stb = sts[b]
                P.op("pool", lambda e, stt=stt: e.memset(stt[:], 0.0), writes=[stb])
                for hf in range(2):
                    pst, psb = next_ps()
                    for kc in range(8):
                        mm(pst[0:bs, 0:512], mT[:, kc, b * bs:(b + 1) * bs], slots[so[hf]][:, kc * 512:(kc + 1) * 512], kc == 0, kc == 7,
                           [slot_b[so[hf]], mT_b], [psb])
                    P.op("act", lambda e, pst=pst, stt=stt, hf=hf: e.activation(out=junk[0:bs, 0:512], in_=pst[0:bs, 0:512], func=AF.Square,
                                                                                accum_out=stt[0:bs, hf:hf + 1]),
                         reads=[psb, stb], writes=[junk_b, stb])
                    pa, pab = pre(b, hf)
                    P.op("dve", lambda e, pst=pst, pa=pa, hf=hf: e.tensor_tensor(out=pa, in0=pst[0:bs, 0:512], in1=gpm[0:bs, hf * 512:(hf + 1) * 512],
                                                                                 op=ALU.mult), reads=[psb, cst2], writes=[pab])
            w_release(so[0])
            w_release(so[1])
            for b in range(nblk):
                stt, stb = sts[b]
                P.op("dve", lambda e, stt=stt: e.tensor_tensor(out=stt[0:bs, 5:6], in0=stt[0:bs, 0:1], in1=stt[0:bs, 1:2], op=ALU.add), reads=[stb], writes=[stb])
            for b in range(nblk):
                stt, stb = sts[b]
                P.op("act", lambda e, stt=stt: e.activation(out=stt[0:bs, 6:7], in_=stt[0:bs, 5:6], func=AF.Sqrt, bias=epsT[0:bs, 0:1], scale=1.0 / D),
                     reads=[stb, cst], writes=[stb])
            for b in range(nblk):
                stt, stb = sts[b]
                P.op("dve", lambda e, stt=stt: e.reciprocal(out=stt[0:bs, 7:8], in_=stt[0:bs, 6:7]), reads=[stb], writes=[stb])
                for hf in range(2):
                    pa, pab = pre(b, hf)
                    P.op("dve", lambda e, pa=pa, hf=hf, b=b, stt=stt: e.scalar_tensor_tensor(
                        out=xNt[0:bs, b, hf * 512:(hf + 1) * 512], in0=pa, scalar=stt[0:bs, 7:8], in1=xNt[0:bs, b, hf * 512:(hf + 1) * 512],
                        op0=ALU.mult, op1=ALU.add), reads=[pab, stb, xNb[b]], writes=[xNb[b]])
            for b in range(nblk):
                stt, stb = sts[b]
                P.op("act", lambda e, stt=stt, b=b: e.activation(out=junk[0:bs, :], in_=xNt[0:bs, b, :], func=AF.Square, accum_out=stt[0:bs, 2:3]),
                     reads=[xNb[b], stb], writes=[junk_b, stb])
            for b in range(nblk):
                stt, stb = sts[b]
                P.op("act", lambda e, stt=stt: e.activation(out=stt[0:bs, 3:4], in_=stt[0:bs, 2:3], func=AF.Sqrt, bias=epsT[0:bs, 0:1], scale=1.0 / D),
                     reads=[stb, cst], writes=[stb])
            zts_n = None
            if nxt_tile is not None:
                zts_n = gm_front_a(*nxt_tile)
            for b in range(nblk):
                stt, stb = sts[b]
                P.op("dve", lambda e, stt=stt: e.reciprocal(out=stt[0:bs, 4:5], in_=stt[0:bs, 3:4]), reads=[stb], writes=[stb])
                hb, hbb = h2bs[b % 2], h2bs_b[b % 2]
                P.op("dve", lambda e, b=b, stt=stt, hb=hb: e.tensor_scalar(out=hb[0:bs, :], in0=xNt[0:bs, b, :], scalar1=stt[0:bs, 4:5], scalar2=None, op0=ALU.mult),
                     reads=[xNb[b], stb], writes=[hbb])
                for kc in range(8):
                    P.op("pe", lambda e, kc=kc, hb=hb: e.transpose(out=psT_t[:, kc * 128: kc * 128 + bs], in_=hb[0:bs, kc * 128:(kc + 1) * 128], identity=ident[0:bs, 0:bs]),
                         reads=[hbb, cst], writes=[psT_b], inc=(kc == 7))
                P.op("dve", lambda e, b=b: e.tensor_tensor(out=h2T[:, :, b * bs:(b + 1) * bs], in0=psT_t[:, :].rearrange("p (k t) -> p k t", k=8)[:, :, 0:bs],
                                                           in1=bass.AP(gffn, 0, [[8, 128], [1, 8], [0, bs]]), op=ALU.mult),
                     reads=[psT_b, cst], writes=[h2T_b])
            P.inherit([uT_b], uTf_b)
            if nxt_tile is not None:
                gm_front_b(*nxt_tile, zts_n)
            for j in range(8):
                s = w_acquire("UP%d" % j)
                for cc in range(4):
                    fc = 4 * j + cc
                    pst, psb = next_ps()
                    for kc in range(8):
                        mm(pst[:, 0:nt], slots[s][:, kc * 512 + cc * 128: kc * 512 + cc * 128 + 128], h2T[:, kc, 0:nt], kc == 0, kc == 7,
                           [slot_b[s], h2T_b], [psb])
                    ri = nxt("r")
                    P.op("act", lambda e, pst=pst, ri=ri: e.activation(out=rt[ri][:, 0:nt], in_=pst[:, 0:nt], func=AF.Relu), reads=[psb], writes=[rt_b[ri]])
                    P.op("pool", lambda e, ri=ri, fc=fc: e.tensor_tensor(out=uT[:, fc, 0:nt], in0=rt[ri][:, 0:nt], in1=rt[ri][:, 0:nt], op=ALU.mult),
                         reads=[rt_b[ri]], writes=[uT_b])
                w_release(s)
            if nxt_tile is not None:
                gm_back(*nxt_tile[:4])
            sis = []
            for b in range(nblk):
                si_ = b % 2
                sis.append(si_)
            P.op("pool", lambda e: e.memset(st[0][:], 0.0), writes=[st_b[0]])
            P.op("pool", lambda e: e.memset(st[1][:], 0.0), writes=[st_b[1]])
            def scol(b, hf):
                return (b // 2) * 2 + hf
            for hf in range(2):
                acc = [next_ps() for _ in range(nblk)]
                for j in range(4):
                    s = w_acquire("DN%d_%d" % (hf, j))
                    for b in range(nblk):
                        pst, psb = acc[b]
                        for f8 in range(8):
                            fc = 8 * j + f8
                            mm(pst[0:bs, 0:512], uT[:, fc, b * bs:(b + 1) * bs], slots[s][:, f8 * 512:(f8 + 1) * 512], j == 0 and f8 == 0, j == 3 and f8 == 7,
                               [slot_b[s], uT_b], [psb])
                    w_release(s)
                for b in range(nblk):
                    pst, psb = acc[b]
                    stt, stb = st[b % 2], st_b[b % 2]
                    c0 = scol(b, hf)
                    P.op("act", lambda e, pst=pst, stt=stt, c0=c0: e.activation(out=junk[0:bs, 0:512], in_=pst[0:bs, 0:512], func=AF.Square,
                                                                                accum_out=stt[0:bs, c0:c0 + 1]),
                         reads=[psb, stb], writes=[junk_b, stb])
                    if hf == 0:
                        P.op("dve", lambda e, pst=pst, b=b: e.tensor_tensor(out=fst[0:bs, b, :], in0=pst[0:bs, 0:512], in1=gpf[0:bs, 0:512], op=ALU.mult),
                             reads=[psb, cst2], writes=[fsl_b[b]])
                    else:
                        ti = nxt("t")
                        P.op("dve", lambda e, pst=pst, ti=ti: e.tensor_tensor(out=tmpf[ti][0:bs, :], in0=pst[0:bs, 0:512], in1=gpf[0:bs, 512:1024], op=ALU.mult),
                             reads=[psb, cst2], writes=[tmpf_b[ti]])
                        rs = rstd_from_sums(b % 2, [scol(b, 0), scol(b, 1)], D, bs)
                        P.op("dve", lambda e, b=b, rs=rs: e.scalar_tensor_tensor(out=xNt[0:bs, b, 0:512], in0=fst[0:bs, b, :], scalar=rs, in1=xNt[0:bs, b, 0:512],
                                                                                 op0=ALU.mult, op1=ALU.add), reads=[fsl_b[b], stb, xNb[b]], writes=[xNb[b]])
                        P.op("dve", lambda e, b=b, rs=rs, ti=ti: e.scalar_tensor_tensor(out=xNt[0:bs, b, 512:1024], in0=tmpf[ti][0:bs, :], scalar=rs,
                                                                                        in1=xNt[0:bs, b, 512:1024], op0=ALU.mult, op1=ALU.add),
                             reads=[tmpf_b[ti], stb, xNb[b]], writes=[xNb[b]])
                        P.dma("pool", y_d[tok0 + b * bs: tok0 + (b + 1) * bs, :], xNt[0:bs, b, :], reads=[xNb[b]], post_writes=[y_ob], sb=ysem_b)

        compute_hT(xT, 0, NT, "reuse", 0)
        tiles = [(NT * i, NT, 4, 128, hC[i % 2]) for i in range(4)] + [(S, NS, 1, 64, hC[0])]
        zts0 = gm_front_a(*tiles[0])
        gm_front_b(*tiles[0], zts0)
        gm_back(*tiles[0][:4])
        for i in range(5):
            phase_c(*tiles[i], tiles[i + 1] if i < 4 else None)
        assert WS.i_use == len(worder), (WS.i_use, len(worder))

        P.finish([y_ob, kvp128_b, kvp512_b, kvp2048_b, kvs_ob, sg_ob])
        cC.close()
    return nc


_NC_CACHE = {}


def kernel(x_prompt, x_sample, cache_kv_w128, cache_kv_w512, cache_kv_w2048,
           norm_pre_mix, w_in, b_gate, ln_z_g, ln_z_b, w_spatial, b_spatial,
           w_ao, w_bo, w_out, norm_post_mix, norm_pre_ffn, w_up, w_down, norm_post_ffn):
    f = lambda a: np.ascontiguousarray(np.asarray(a, dtype=np.float32))
    x_prompt = np.asarray(x_prompt, dtype=np.float32)
    x_sample = np.asarray(x_sample, dtype=np.float32)
    c128 = np.asarray(cache_kv_w128, dtype=np.float32)[0].reshape(128, 128, 512)
    c512 = np.asarray(cache_kv_w512, dtype=np.float32)[0].reshape(128, 512, 512)
    c2048 = np.asarray(cache_kv_w2048, dtype=np.float32)[0].reshape(128, 2048, 512)
    rep = lambda v, n=128: f(np.broadcast_to(np.asarray(v, dtype=np.float32).reshape(1, -1), (n, np.asarray(v).size)))
    pk = lambda v: f(np.asarray(v, dtype=np.float32).reshape(-1, 128).T)
    wsp = np.asarray(w_spatial, dtype=np.float32)[0]
    bsp = np.asarray(b_spatial, dtype=np.float32)[0]
    wspT = f(wsp.transpose(2, 0, 1).reshape(128, 512))
    wsS = np.zeros((64, 4, 64), np.float32)
    for b in range(16):
        wsS[4 * b:4 * b + 4, :, 4 * b:4 * b + 4] = wsp[:, 0:4, 0:4].transpose(2, 0, 1)
    bspS = np.tile(bsp[:, 0:4], (1, 16)).reshape(1, 256)
    shared = {
        "w_in": f(np.asarray(w_in)[0]), "w_ao": f(np.asarray(w_ao)[0]), "w_bo": f(np.asarray(w_bo)[0]),
        "w_out": f(np.asarray(w_out)[0]), "w_up": f(np.asarray(w_up)[0]), "w_down": f(np.asarray(w_down)[0]),
        "gpre": pk(np.asarray(norm_pre_mix)[0]), "bgate": pk(np.asarray(b_gate)[0]),
        "gpm": rep(np.asarray(norm_post_mix)[0]), "gffn": pk(np.asarray(norm_pre_ffn)[0]),
        "gpf": rep(np.asarray(norm_post_ffn)[0]), "lng": rep(np.asarray(ln_z_g)[0]), "lnb": rep(np.asarray(ln_z_b)[0]),
        "wspT": wspT, "bsp": rep(bsp.reshape(-1)), "wsS": f(wsS.reshape(64, 256)), "bspS": rep(bspS),
    }
    in_maps = []
    for c in range(NCORES):
        xp = x_prompt[c]
        xs = x_sample[16 * c:16 * c + 16].reshape(NS, D)
        xNc = np.concatenate([xp, xs], axis=0)
        m = dict(shared)
        m["xN"] = f(xNc)
        m["xT"] = f(xNc.T)
        m["xT16"] = f(xp.T.reshape(D, 128, 16).transpose(0, 2, 1).reshape(D, S))
        m["c128"] = f(c128[16 * c:16 * c + 16])
        m["c512"] = f(c512[16 * c:16 * c + 16])
        m["c2048"] = f(c2048[16 * c:16 * c + 16])
        in_maps.append(m)
    if "nc" not in _NC_CACHE:
        _NC_CACHE["nc"] = build_program()
    nc = _NC_CACHE["nc"]
    ndbg = int(os.environ.get("MK_NCORES", str(NCORES)))
    if ndbg < NCORES:
        res = run_bass_kernel_spmd(nc, in_maps[:ndbg], core_ids=list(range(ndbg)))
        R = list(res.results) + [res.results[0]] * (NCORES - ndbg)
    else:
        res = run_bass_kernel_spmd(nc, in_maps, core_ids=list(range(NCORES)))
        R = res.results
    y_p = np.stack([R[c]["y"][0:S] for c in range(NCORES)], 0)
    y_s = np.concatenate([R[c]["y"][S:].reshape(16, 4, D) for c in range(NCORES)], 0)
    kvp128 = np.stack([R[c]["kvp128"].reshape(128, 2, 4, 64) for c in range(NCORES)], 0)[None]
    kvp512 = np.stack([R[c]["kvp512"].reshape(512, 2, 4, 64) for c in range(NCORES)], 0)[None]
    kvp2048 = np.stack([R[c]["kvp2048"].reshape(2048, 2, 4, 64) for c in range(NCORES)], 0)[None]
    kvs = [np.concatenate([R[c]["kvs"][g].reshape(16, 4, 2, 4, 64) for c in range(NCORES)], 0)[None] for g in range(3)]
    sg = np.concatenate([R[c]["sg"].reshape(16, 4, 512) for c in range(NCORES)], 0)[None]
    outs = (y_p, y_s, kvp128, kvp512, kvp2048, kvs[0], kvs[1], kvs[2], sg)
    return tuple(np.ascontiguousarray(o, dtype=np.float32) for o in outs)
```

```python
import numpy as np
from contextlib import ExitStack
import concourse.bass as bass
import concourse.mybir as mybir
from concourse.bass_utils import run_bass_kernel_spmd

F32 = mybir.dt.float32
BF16 = mybir.dt.bfloat16
AF = mybir.ActivationFunctionType
ALU = mybir.AluOpType
AX = mybir.AxisListType

NCORES = 8
D = 1024
S = 2048
NS = 64
NTOK = S + NS
NT = 512
EPS = 1e-6
NSLOT = 4
import os
STAGE = float(os.environ.get("MK_STAGE", "9"))
SLOTF = 4096


class Buf:
    __slots__ = ("name", "w", "r", "dsem", "dcnt", "ex", "nosync")

    def __init__(self, name, ex=False, nosync=False):
        self.name = name
        self.ex = ex
        self.nosync = nosync
        self.w = {}
        self.r = {}
        self.dsem = None
        self.dcnt = 0


def _merge(d, s):
    for k, v in s.items():
        if v > d.get(k, 0):
            d[k] = v


class Prog:
    CE = ("pe", "act", "dve", "pool")
    ENG = {"pe": "tensor", "act": "scalar", "dve": "vector", "pool": "gpsimd", "sp": "sync"}

    def __init__(self, nc, ctx):
        self.nc = nc
        self.ctx = ctx
        self.sems = {}
        self.cnt = {e: 0 for e in self.CE}
        self.waited = {e: {} for e in self.ENG}
        for e in self.CE:
            self.sems[e] = ctx.enter_context(nc.semaphore("s_" + e))
        self.nsem = 0
        self.pe_pending = None

    def newsem(self, name):
        self.nsem += 1
        key = "d%d_%s" % (self.nsem, name)
        self.sems[key] = self.ctx.enter_context(self.nc.semaphore(key))
        return key

    def _waits(self, e, reads, writes):
        d = {}
        for b in reads:
            _merge(d, b.w)
        for b in writes:
            for src in (b.w, b.r):
                for k, v in src.items():
                    if k == e and b.nosync:
                        continue
                    if v > d.get(k, 0):
                        d[k] = v
        if e == "pe":
            d.pop("pe", None)
        elif d.get("pe", 0) > self.cnt["pe"]:
            assert self.pe_pending is not None and d["pe"] == self.cnt["pe"] + 1
            self.pe_pending.then_inc(self.sems["pe"], 1)
            self.cnt["pe"] += 1
            self.pe_pending = None
        out = []
        wd = self.waited[e]
        for k, v in d.items():
            if v > wd.get(k, 0):
                wd[k] = v
                out.append((k, v))
        return out

    def _emit(self, e, waits, fn, inc):
        eng = getattr(self.nc, self.ENG[e])
        for k, v in waits:
            eng.wait_ge(self.sems[k], v)
        if fn is None:
            return None
        ins = fn(eng)
        if inc is not None:
            ins.then_inc(self.sems[inc[0]], inc[1])
        return ins

    def op(self, e, fn, reads=(), writes=(), inc=True):
        exr = [b for b in reads if b.ex]
        if exr:
            writes = list(writes) + exr
        waits = self._waits(e, reads, writes)
        if inc:
            self.cnt[e] += 1
            tok = self.cnt[e]
        else:
            tok = self.cnt[e] + 1
        ins = self._emit(e, waits, fn, (e, 1) if inc else None)
        if e == "pe":
            self.pe_pending = None if inc else ins
        for b in reads:
            if tok > b.r.get(e, 0):
                b.r[e] = tok
        for b in writes:
            if tok > b.w.get(e, 0):
                b.w[e] = tok

    def dma(self, q, out, in_, reads=(), writes=(), sb=None, post_writes=(), **kw):
        if sb is None:
            sb = writes[0] if writes else (post_writes[0] if post_writes else reads[0])
        if sb.dsem is None:
            sb.dsem = self.newsem(sb.name)
        waits = self._waits(q, reads, writes)
        sb.dcnt += 16
        tok = sb.dcnt
        key = sb.dsem
        self._emit(q, waits, lambda eng: eng.dma_start(out=out, in_=in_, **kw), (key, 16))
        for b in reads:
            if tok > b.r.get(key, 0):
                b.r[key] = tok
        for b in list(writes) + list(post_writes):
            if tok > b.w.get(key, 0):
                b.w[key] = tok

    def inherit(self, new_bufs, old_bufs):
        d = {}
        for b in old_bufs:
            _merge(d, b.w)
            _merge(d, b.r)
        for b in new_bufs:
            _merge(b.r, d)

    def finish(self, final_bufs):
        d = {}
        for b in final_bufs:
            _merge(d, b.w)
        self._emit("sp", list(d.items()), None, None)


def build_program():
    nc = bass.Bass("TRN2", target_bir_lowering=False)

    def din(name, shape):
        return nc.dram_tensor(name, shape, F32, kind="ExternalInput")

    def dout(name, shape):
        return nc.dram_tensor(name, shape, F32, kind="ExternalOutput")

    xT_h = din("xT", [D, NTOK]); xT = xT_h.ap()
    xT16 = din("xT16", [D, S]).ap()
    xN = din("xN", [NTOK, D]).ap()
    c128 = din("c128", [16, 128, 512]).ap()
    c512 = din("c512", [16, 512, 512]).ap()
    c2048 = din("c2048", [16, 2048, 512]).ap()
    w_in = din("w_in", [D, 5376]).ap()
    w_ao = din("w_ao", [256, D]).ap()
    w_bo = din("w_bo", [512, D]).ap()
    w_out = din("w_out", [D, D]).ap()
    w_up = din("w_up", [D, 4096]).ap()
    w_down = din("w_down", [4096, D]).ap()
    gpre_d = din("gpre", [128, 8]).ap()
    bgate_d = din("bgate", [128, 16]).ap()
    gpm_d = din("gpm", [128, D]).ap()
    gffn_d = din("gffn", [128, 8]).ap()
    gpf_d = din("gpf", [128, D]).ap()
    lng_d = din("lng", [128, 512]).ap()
    lnb_d = din("lnb", [128, 512]).ap()
    wspT_d = din("wspT", [128, 512]).ap()
    bsp_d = din("bsp", [128, 512]).ap()
    wsS_d = din("wsS", [64, 256]).ap()
    bspS_d = din("bspS", [128, 256]).ap()

    y_d = dout("y", [NTOK, D]).ap()
    kvp128_d = dout("kvp128", [128, 512]).ap()
    kvp512_d = dout("kvp512", [512, 512]).ap()
    kvp2048_d = dout("kvp2048", [2048, 512]).ap()
    kvs_h = dout("kvs", [3, 64, 512]); kvs_d = kvs_h.ap()
    sg_d = dout("sg", [64, 512]).ap()

    qs_h = nc.dram_tensor("qs_scr", [64, 768], F32, kind="Internal")
    od_h = nc.dram_tensor("od_scr", [64, 3, 4, 257], F32, kind="Internal")

    with ExitStack() as ctx:
        P = Prog(nc, ctx)

        def sbt(name, shape, dt, c=ctx):
            return c.enter_context(nc.sbuf_tensor("sb_" + name, shape, dt))

        w_in_v = w_in.rearrange("(kc p) n -> p kc n", p=128)
        chunks = {}

        def add_chunk(name, F, parts):
            t = nc.dram_tensor("sc_" + name, [128, F], BF16, kind="Internal").ap()
            chunks[name] = {"ap": t, "F": F, "buf": Buf("sc_" + name), "parts": parts}

        for g in range(3):
            add_chunk("KV%d" % g, 4096, [(w_in_v[:, :, 768 + 256 * g: 1024 + 256 * g], 8, 512, 0, 256),
                                         (w_in_v[:, :, 1536 + 256 * g: 1792 + 256 * g], 8, 512, 256, 256)])
        add_chunk("Q01", 4096, [(w_in_v[:, :, 0:512], 8, 512, 0, 512)])
        add_chunk("Q2", 2048, [(w_in_v[:, :, 512:768], 8, 256, 0, 256)])
        add_chunk("Z1", 4096, [(w_in_v[:, :, 2304:2816], 8, 512, 0, 512)])
        add_chunk("Z2", 4096, [(w_in_v[:, :, 2816:3328], 8, 512, 0, 512)])
        for j in range(4):
            add_chunk("GP%d" % j, 4096, [(w_in_v[:, :, 3328 + 256 * j: 3584 + 256 * j], 8, 512, 0, 256),
                                         (w_in_v[:, :, 4352 + 256 * j: 4608 + 256 * j], 8, 512, 256, 256)])
        add_chunk("AO", 2048, [(w_ao.rearrange("(hp p) n -> p hp n", p=128), 2, 1024, 0, 1024)])
        add_chunk("BO", 4096, [(w_bo.rearrange("(kc p) n -> p kc n", p=128), 4, 1024, 0, 1024)])
        w_out_v = w_out.rearrange("(kc p) n -> p kc n", p=128)
        for j in range(2):
            add_chunk("OUT%d" % j, 4096, [(w_out_v[:, :, 512 * j: 512 * j + 512], 8, 512, 0, 512)])
        w_up_v = w_up.rearrange("(kc p) n -> p kc n", p=128)
        for j in range(8):
            add_chunk("UP%d" % j, 4096, [(w_up_v[:, :, 512 * j: 512 * j + 512], 8, 512, 0, 512)])
        w_dn_v = w_down.rearrange("(fc p) n -> p fc n", p=128)
        for hf in range(2):
            for j in range(4):
                add_chunk("DN%d_%d" % (hf, j), 4096,
                          [(w_dn_v[:, 8 * j: 8 * j + 8, 512 * hf: 512 * hf + 512], 8, 512, 0, 512)])

        cast_order = (["KV0", "KV1", "Q01", "KV2", "Q2", "Z2", "Z1", "AO", "BO"] + ["GP%d" % j for j in range(4)]
                      + ["OUT0", "OUT1"] + ["UP%d" % j for j in range(8)]
                      + ["DN%d_%d" % (hf, j) for hf in range(2) for j in range(4)])

        c_head = ["AO", "BO", "GP0", "GP1", "GP2", "GP3", "OUT0", "OUT1"]
        c_tail = ["UP%d" % j for j in range(8)] + ["DN%d_%d" % (hf, j) for hf in range(2) for j in range(4)]
        worder = ["KV0", "KV1", "Q01"] * 4 + ["KV0", "KV1", "KV2", "Q01", "Q2"] + ["KV2", "Q2"] * 4 + ["Z2", "Z1"]
        for i in range(5):
            worder += c_head + (["Z2", "Z1"] if i < 4 else []) + c_tail
        slots = [sbt("wslot%d" % i, [128, SLOTF], BF16) for i in range(NSLOT)]
        slot_b = [Buf("wslot%d" % i) for i in range(NSLOT)]
        slotc_b = [Buf("wslotc%d" % i) for i in range(NSLOT)]

        class WS:
            i_load = 0
            i_use = 0
            free = list(range(NSLOT))
            loaded = {}

        def w_pump():
            while WS.free and WS.i_load < len(worder):
                s = WS.free.pop(0)
                ch = chunks[worder[WS.i_load]]
                if not ch.get("cast_done"):
                    nm = worder[WS.i_load]
                    uses = ch.get("uses", 0)
                    ch["uses"] = uses + 1
                    writeback = not ((nm.startswith("UP") or nm.startswith("DN")) and uses == 0)
                    if writeback:
                        ch["cast_done"] = True
                    first = True
                    for (src, kc, n, off, w) in ch["parts"]:
                        dst = slots[s][:, 0:ch["F"]].rearrange("p (kc n) -> p kc n", kc=kc)[:, :, off:off + w]
                        if first:
                            P.dma("pool", dst, src, writes=[slot_b[s]], sb=slotc_b[s])
                        else:
                            P.dma("pool", dst, src, post_writes=[slot_b[s]], sb=slotc_b[s])
                        first = False
                    if writeback:
                        P.dma("sp", ch["ap"], slots[s][:, 0:ch["F"]], reads=[slot_b[s]], writes=[ch["buf"]], sb=ch["buf"])
                else:
                    P.dma("sp", slots[s][:, 0:ch["F"]], ch["ap"], reads=[ch["buf"]], writes=[slot_b[s]])
                WS.loaded[WS.i_load] = s
                WS.i_load += 1

        def w_acquire(name):
            w_pump()
            assert worder[WS.i_use] == name, (worder[WS.i_use], name)
            s = WS.loaded.pop(WS.i_use)
            WS.i_use += 1
            return s

        def w_release(s):
            WS.free.append(s)
            w_pump()

        _saved_free = WS.free[1:]
        WS.free = WS.free[:1]
        w_pump()
        WS.free = _saved_free


        NPS = 7
        ps_t = [ctx.enter_context(nc.psum_tensor("ps%d" % i, [128, 512], F32)) for i in range(NPS)]
        ps_b = [Buf("ps%d" % i, ex=True) for i in range(NPS)]
        psT_t = ctx.enter_context(nc.psum_tensor("psT", [128, 1024], BF16))
        psT_b = Buf("psT", ex=True)
        ps_rr = [0]

        def next_ps():
            i = ps_rr[0]
            ps_rr[0] = (i + 1) % NPS
            return ps_t[i], ps_b[i]

        def mm(out, lhsT, rhs, start, stop, reads, writes):
            P.op("pe", lambda e: e.matmul(out, lhsT=lhsT, rhs=rhs, start=start, stop=stop),
                 reads=reads, writes=writes, inc=stop)

        ident = sbt("ident", [128, 128], BF16); ident_b = Buf("ident")
        identf = sbt("identf", [128, 128], F32)
        maskc = sbt("maskc", [128, 128], BF16)
        maskp = sbt("maskp", [128, 128], BF16)
        ones = sbt("ones", [128, 128], BF16)
        epsT = sbt("epsT", [128, 1], F32)
        tmpc = sbt("tmpc", [128, 512], F32)
        gpre = sbt("gpre", [128, 8], F32)
        bgate = sbt("bgate", [128, 16], F32)
        gffn = sbt("gffn", [128, 8], F32)
        wspT = sbt("wspT", [128, 512], BF16)
        bsp_hi = sbt("bsp_hi", [128, 512], BF16)
        bsp_lo = sbt("bsp_lo", [128, 512], BF16)
        wsS = sbt("wsS", [64, 256], BF16)
        bspS = sbt("bspS", [128, 256], F32)
        cst = Buf("consts")

        P.op("pool", lambda e: e.memset(tmpc[:, 0:128], 0.0), writes=[cst])
        P.op("pool", lambda e: e.affine_select(out=tmpc[:, 0:128], in_=tmpc[:, 0:128], compare_op=ALU.not_equal, fill=1.0,
                                                base=0, pattern=[[-1, 128]], channel_multiplier=1), reads=[cst], writes=[cst])
        P.op("pool", lambda e: e.tensor_copy(out=ident[:], in_=tmpc[:, 0:128]), reads=[cst], writes=[cst])
        P.op("pool", lambda e: e.tensor_copy(out=identf[:], in_=tmpc[:, 0:128]), reads=[cst], writes=[cst])
        P.op("pool", lambda e: e.memset(tmpc[:, 0:128], 0.0), reads=[cst], writes=[cst])
        P.op("pool", lambda e: e.affine_select(out=tmpc[:, 0:128], in_=tmpc[:, 0:128], compare_op=ALU.is_ge, fill=-30000.0,
                                                base=0, pattern=[[1, 128]], channel_multiplier=-1), reads=[cst], writes=[cst])
        P.op("pool", lambda e: e.tensor_copy(out=maskc[:], in_=tmpc[:, 0:128]), reads=[cst], writes=[cst])
        P.op("pool", lambda e: e.memset(tmpc[:, 0:128], 0.0), reads=[cst], writes=[cst])
        P.op("pool", lambda e: e.affine_select(out=tmpc[:, 0:128], in_=tmpc[:, 0:128], compare_op=ALU.is_ge, fill=-30000.0,
                                                base=0, pattern=[[-1, 128]], channel_multiplier=1), reads=[cst], writes=[cst])
        P.op("pool", lambda e: e.tensor_copy(out=maskp[:], in_=tmpc[:, 0:128]), reads=[cst], writes=[cst])
        P.op("pool", lambda e: e.memset(ones[:], 1.0), writes=[cst])
        P.op("pool", lambda e: e.memset(epsT[:], EPS), writes=[cst])
        for (t, d_) in ((gpre, gpre_d), (bgate, bgate_d), (gffn, gffn_d), (bspS, bspS_d)):
            P.dma("sp", t[:], d_, writes=[cst])
        cstL = Buf("consts_late")
        P.inherit([cstL], [cst])
        P.dma("sp", tmpc[:], wspT_d, reads=[cstL], writes=[cstL])
        for g4 in range(4):
            P.op("pool", lambda e, g4=g4: e.affine_select(out=tmpc[:, g4 * 128:(g4 + 1) * 128], in_=tmpc[:, g4 * 128:(g4 + 1) * 128],
                                                         compare_op=ALU.is_ge, fill=0.0, base=0, pattern=[[1, 128]],
                                                         channel_multiplier=-1), reads=[cstL], writes=[cstL])
        P.op("pool", lambda e: e.tensor_copy(out=wspT[:], in_=tmpc[:]), reads=[cstL], writes=[cstL])
        P.dma("sp", tmpc[:], bsp_d, reads=[cstL], writes=[cstL])
        P.op("pool", lambda e: e.tensor_copy(out=bsp_hi[:], in_=tmpc[:]), reads=[cstL], writes=[cstL])
        P.op("pool", lambda e: e.tensor_tensor(out=tmpc[:], in0=tmpc[:], in1=bsp_hi[:], op=ALU.subtract), reads=[cstL], writes=[cstL])
        P.op("pool", lambda e: e.tensor_copy(out=bsp_lo[:], in_=tmpc[:]), reads=[cstL], writes=[cstL])
        P.dma("sp", tmpc[0:64, 0:256], wsS_d, reads=[cstL], writes=[cstL])
        for g4 in range(4):
            P.op("pool", lambda e, g4=g4: e.affine_select(out=tmpc[0:64, g4 * 64:(g4 + 1) * 64], in_=tmpc[0:64, g4 * 64:(g4 + 1) * 64],
                                                         compare_op=ALU.is_ge, fill=0.0, base=0, pattern=[[1, 64]],
                                                         channel_multiplier=-1), reads=[cstL], writes=[cstL])
        P.op("pool", lambda e: e.tensor_copy(out=wsS[:], in_=tmpc[0:64, 0:256]), reads=[cstL], writes=[cstL])

        def emit_cast(name, dep=()):
            ch = chunks[name]
            for (src, kc, n, off, w) in ch["parts"]:
                dst = ch["ap"].rearrange("p (kc n) -> p kc n", kc=kc)[:, :, off:off + w]
                P.dma("pool", dst, src, reads=list(dep), post_writes=[ch["buf"]], sb=ch["buf"])

        deferred = []

        def emit_deferred(n, dep):
            for _ in range(n):
                if deferred:
                    emit_cast(deferred.pop(0), dep)

        xTs = sbt("xTs", [128, 8, NT], F32); xTs_b = Buf("xTs")
        sq = sbt("sq", [128, 4, NT], BF16); sq_b = Buf("sq")
        hTt = sbt("hTt", [128, 8, NT], BF16); hTt_b = Buf("hTt")
        sdt = sbt("sdt", [128, NT], F32); sdt_b = Buf("sdt")
        rstdb = sbt("rstdb", [128, NTOK], F32); rstdb_b = Buf("rstdb")
        OaT = sbt("OaT", [128, 2, NTOK], BF16); OaT_b = Buf("OaT")
        junk = sbt("junk", [128, 1024], BF16); junk_b = Buf("junk", nosync=True)

        def hT_load(src, col0, nt, mode=None, tok0=None, q="act"):
            srcv = src.rearrange("(kc p) n -> p kc n", p=128)
            P.dma(q, xTs[:, :, 0:nt], srcv[:, :, col0:col0 + nt], writes=[xTs_b])

        def hT_part1(src, col0, nt, mode, tok0=None, load=True):
            if load:
                hT_load(src, col0, nt)
            if mode == "reuse":
                return rstdb[:, tok0:tok0 + nt], rstdb_b
            if mode == "tmp":
                i16 = col0 // NT
                return rstdb[:, 0:S].rearrange("p (m r) -> p r m", r=16)[:, 4 * i16:4 * i16 + 4, :], rstdb_b
            pst, psb = next_ps()
            for hf in range(2):
                P.op("act", lambda e, hf=hf: e.activation(out=sq[:, :, 0:nt], in_=xTs[:, 4 * hf:4 * hf + 4, 0:nt], func=AF.Square),
                     reads=[xTs_b], writes=[sq_b])
                for j in range(4):
                    mm(pst[:, 0:nt], ones[:], sq[:, j, 0:nt], hf == 0 and j == 0, hf == 1 and j == 3, [sq_b, cst], [psb])
            P.op("act", lambda e: e.activation(out=sdt[:, 0:nt], in_=pst[:, 0:nt], func=AF.Sqrt, bias=epsT[:, 0:1], scale=1.0 / D),
                 reads=[psb, cst], writes=[sdt_b])
            if mode == "store":
                rs = rstdb[:, tok0:tok0 + nt]
                rs_b = rstdb_b
            else:
                rs = sdt[:, 0:nt]
                rs_b = sdt_b
            P.op("dve", lambda e: e.reciprocal(out=rs, in_=sdt[:, 0:nt]), reads=[sdt_b], writes=[rs_b])
            return rs, rs_b

        def hT_part2(nt, rs, rs_b, dst, dst_b):
            v3 = (len(rs.shape) == 3)
            for kc in range(8):
                o_ = dst[:, kc, 0:nt]
                i_ = xTs[:, kc, 0:nt]
                if v3:
                    o_ = o_.rearrange("p (r m) -> p r m", r=4)
                    i_ = i_.rearrange("p (r m) -> p r m", r=4)
                P.op("dve", lambda e, kc=kc, o_=o_, i_=i_: e.scalar_tensor_tensor(out=o_, in0=i_, scalar=gpre[:, kc:kc + 1],
                                                                                  in1=rs, op0=ALU.mult, op1=ALU.mult),
                     reads=[xTs_b, rs_b, cst], writes=[dst_b])

        def compute_hT(src, col0, nt, mode, tok0=None):
            rs, rs_b = hT_part1(src, col0, nt, mode, tok0)
            hT_part2(nt, rs, rs_b, hTt, hTt_b)

        cAB = ExitStack()
        qT = sbt("qT", [128, 6, S], BF16, cAB); qT_b = Buf("qT")
        kT = sbt("kT", [128, 6, S], BF16, cAB); kT_b = Buf("kT")
        VW = 48 * 260
        Vaug = sbt("Vaug", [128, VW], BF16, cAB); Vaug_b = Buf("Vaug")
        onesf = sbt("onesf", [128, 64], F32, cAB)
        Oacc = sbt("Oacc", [128, S], F32, cAB); Oacc_b = Buf("Oacc")
        rden = sbt("rden", [128, 512], F32, cAB); rden_b = Buf("rden")
        NPT = 3
        pT = [sbt("pT%d" % i, [128, 256], BF16, cAB) for i in range(NPT)]
        pT_b = [Buf("pT%d" % i) for i in range(NPT)]
        kvst = [sbt("kvst%d" % i, [128, 512], F32, cAB) for i in range(2)]
        kvst_b = [Buf("kvst%d" % i) for i in range(2)]
        kvs_s = sbt("kvs_s", [64, 3, 512], F32, cAB); kvs_sb = Buf("kvs_s")
        qs_s = sbt("qs_s", [64, 768], F32, cAB); qs_sb = Buf("qs_s")
        NKV = 7
        xflat = xTs[:].rearrange("p k t -> p (k t)")
        KVt = [xflat[:, i * 520:i * 520 + 513] for i in range(NKV)]
        KVt_b = [Buf("KVt%d" % i) for i in range(NKV)]
        qb = [sbt("qb%d" % i, [128, 768], F32, cAB) for i in range(2)]
        qb_b = [Buf("qb%d" % i) for i in range(2)]
        AB_bufs = [qT_b, kT_b, Vaug_b, Oacc_b, rden_b] + pT_b + kvst_b + [kvs_sb, qs_sb] + KVt_b + qb_b

        kvp128_b = Buf("kvp128"); kvp512_b = Buf("kvp512"); kvp2048_b = Buf("kvp2048")
        kvs_ob = Buf("kvs_o"); sg_ob = Buf("sg_o"); y_ob = Buf("y_o")
        qs_db = Buf("qs_d"); od_db = Buf("od_d")

        P.op("dve", lambda e: e.memset(Vaug[:, :], 1.0), writes=[Vaug_b])
        P.op("pool", lambda e: e.memset(onesf[:, :], 1.0), writes=[Vaug_b])

        def voff(g, blk):
            return (g * 16 + blk) * 260

        kvst_rr = [0]
        evac_rr = [0]

        def evac_copy(out, in_, reads, writes):
            evac_rr[0] ^= 1
            if evac_rr[0]:
                P.op("act", lambda e: e.activation(out=out, in_=in_, func=AF.Copy), reads=reads, writes=writes)
            else:
                P.op("dve", lambda e: e.tensor_copy(out=out, in_=in_), reads=reads, writes=writes)

        hT2 = Oacc[:].bitcast(BF16).rearrange("p (k t) -> p k t", k=8)
        hT2_b = Buf("hT2")
        hbufs = [(hTt, hTt_b), (hT2, hT2_b)]
        HC = {"t": hTt, "b": hTt_b}

        def featmajor_proj(s, colbase, ncols_per_kc, dst_fn):
            pst, psb = next_ps()
            for kc in range(8):
                mm(pst[:, 0:NT], slots[s][:, kc * ncols_per_kc + colbase: kc * ncols_per_kc + colbase + 128], HC["t"][:, kc, 0:NT],
                   kc == 0, kc == 7, [slot_b[s], HC["b"]], [psb])
            dst_fn(pst, psb)

        def kv_tok(s, lhs_fn, g, blk, out_dram=None, out_buf=None):
            pst, psb = next_ps()
            c0 = 0 if out_dram is not None else 256
            for kc in range(8):
                mm(pst[:, c0:512], lhs_fn(kc), slots[s][:, kc * 512 + c0:(kc + 1) * 512], kc == 0, kc == 7, [slot_b[s], HC["b"]], [psb])
            o = voff(g, blk)
            evac_copy(Vaug[:, o:o + 260].rearrange("p (h e) -> p h e", e=65)[:, :, 0:64],
                      pst[:, 256:512].rearrange("p (h d) -> p h d", d=64), [psb], [Vaug_b])
            if out_dram is not None and os.environ.get("MK_NOOUT") == "2" and g != 0:
                out_dram = None
            if out_dram is not None and os.environ.get("MK_NOOUT") == "3" and g == 0:
                out_dram = None
            if out_dram is not None and os.environ.get("MK_NOOUT") != "1":
                i = kvst_rr[0]
                kvst_rr[0] ^= 1
                evac_copy(kvst[i][:], pst[:, 0:512], [psb], [kvst_b[i]])
                P.dma(os.environ.get("MK_STQ", "pool"), out_dram, kvst[i][:], reads=[kvst_b[i]], post_writes=[out_buf], sb=kvst_b[i])

        if STAGE < 1:
            P.finish([y_ob, cst] + slot_b + [chunks[n]["buf"] for n in cast_order])
            cAB.close()
            return nc
        a_steps = [(xT, NT * i, NT, "store", NT * i) for i in range(4)] + [(xT, S, NS, "store", S)] + \
            [(xT16, NT * i, NT, "tmp", None) for i in range(4)]
        a_state = {"k": 0, "rs": None}

        def a_begin():
            k = a_state["k"]
            HC["t"], HC["b"] = hbufs[k % 2]
            if k == 0 and len(a_steps) > 1:
                hT_load(*a_steps[1])

        def a_mid():
            k = a_state["k"]
            if k + 1 < len(a_steps):
                rs, rs_b = hT_part1(*a_steps[k + 1], load=False)
                dst, dst_b = hbufs[(k + 1) % 2]
                hT_part2(a_steps[k + 1][2], rs, rs_b, dst, dst_b)
                if k + 2 < len(a_steps):
                    hT_load(*a_steps[k + 2], q="sp")
            a_state["k"] = k + 1

        rs0, rs0_b = hT_part1(*a_steps[0])
        hT_part2(NT, rs0, rs0_b, hTt, hTt_b)
        emit_deferred(1, [hTt_b])
        for i in range(4):
            tok0 = NT * i
            a_begin()
            if STAGE < 1.1:
                P.finish([y_ob, kvp128_b, kvp512_b, kvp2048_b, hTt_b, kT_b, qT_b, Vaug_b] + slot_b + ps_b)
                cAB.close()
                return nc
            s = w_acquire("KV0")
            for p in range(2):
                featmajor_proj(s, p * 128, 512, lambda pst, psb, p=p: evac_copy(kT[:, p, tok0:tok0 + NT], pst[:, 0:NT], [psb], [kT_b]))
            if STAGE < 1.2:
                P.finish([y_ob, kvp128_b, kvp512_b, kvp2048_b, hTt_b, kT_b, qT_b, Vaug_b] + slot_b + ps_b)
                cAB.close()
                return nc
            for b in range(4):
                last = (i == 3 and b == 3) or (os.environ.get('MK_FORCEOUT') and b == 3)
                kv_tok(s, lambda kc, b=b: HC["t"][:, kc, b * 128:(b + 1) * 128], 0, 4 * i + b,
                       kvp128_d if last else None, kvp128_b)
            w_release(s)
            a_mid()
            if STAGE < 1.3:
                P.finish([y_ob, kvp128_b, kvp512_b, kvp2048_b, hTt_b, kT_b, qT_b, Vaug_b] + slot_b + ps_b)
                cAB.close()
                return nc
            s = w_acquire("KV1")
            for p in range(2):
                featmajor_proj(s, p * 128, 512, lambda pst, psb, p=p: evac_copy(
                    kT[:, 2 + p, :].rearrange("q (r m) -> q r m", r=4)[:, :, 128 * i:128 * i + 128],
                    pst[:, 0:NT].rearrange("q (m r) -> q r m", r=4), [psb], [kT_b]))
            if STAGE < 1.4:
                P.finish([y_ob, kvp128_b, kvp512_b, kvp2048_b, hTt_b, kT_b, qT_b, Vaug_b] + slot_b + ps_b)
                cAB.close()
                return nc
            for r in range(4):
                kv_tok(s, lambda kc, r=r: HC["t"][:, kc, r:NT:4], 1, r * 4 + i,
                       kvp512_d[r:512:4, :] if i == 3 else None, kvp512_b)
            w_release(s)
            s = w_acquire("Q01")
            for p in range(2):
                featmajor_proj(s, p * 128, 512, lambda pst, psb, p=p: evac_copy(qT[:, p, tok0:tok0 + NT], pst[:, 0:NT], [psb], [qT_b]))
            for p in range(2):
                featmajor_proj(s, 256 + p * 128, 512, lambda pst, psb, p=p: evac_copy(
                    qT[:, 2 + p, :].rearrange("q (r m) -> q r m", r=4)[:, :, 128 * i:128 * i + 128],
                    pst[:, 0:NT].rearrange("q (m r) -> q r m", r=4), [psb], [qT_b]))
            w_release(s)

        a_begin()
        for g in range(3):
            s = w_acquire("KV%d" % g)
            pst, psb = next_ps()
            for kc in range(8):
                mm(pst[0:NS, 0:512], HC["t"][:, kc, 0:NS], slots[s][:, kc * 512:(kc + 1) * 512], kc == 0, kc == 7, [slot_b[s], HC["b"]], [psb])
            evac_copy(kvs_s[:, g, :], pst[0:NS, 0:512], [psb], [kvs_sb])
            w_release(s)
            if g == 0:
                a_mid()
        P.dma("pool", kvs_d.rearrange("g t n -> t g n"), kvs_s[:], reads=[kvs_sb], writes=[kvs_ob], sb=kvs_sb)
        s = w_acquire("Q01")
        pst, psb = next_ps()
        for kc in range(8):
            mm(pst[0:NS, 0:512], HC["t"][:, kc, 0:NS], slots[s][:, kc * 512:(kc + 1) * 512], kc == 0, kc == 7, [slot_b[s], HC["b"]], [psb])
        evac_copy(qs_s[:, 0:512], pst[0:NS, 0:512], [psb], [qs_sb])
        w_release(s)
        s = w_acquire("Q2")
        pst, psb = next_ps()
        for kc in range(8):
            mm(pst[0:NS, 0:256], HC["t"][:, kc, 0:NS], slots[s][:, kc * 256:(kc + 1) * 256], kc == 0, kc == 7, [slot_b[s], HC["b"]], [psb])
        evac_copy(qs_s[:, 512:768], pst[0:NS, 0:256], [psb], [qs_sb])
        w_release(s)
        P.dma("pool", qs_h.ap(), qs_s[:], reads=[qs_sb], writes=[qs_db], sb=qs_sb)

        for i in range(4):
            a_begin()
            s = w_acquire("KV2")
            for p in range(2):
                featmajor_proj(s, p * 128, 512, lambda pst, psb, p=p: evac_copy(kT[:, 4 + p, NT * i:NT * i + NT], pst[:, 0:NT], [psb], [kT_b]))
            for b in range(4):
                r = 4 * i + b
                kv_tok(s, lambda kc, b=b: HC["t"][:, kc, b * 128:(b + 1) * 128], 2, r, kvp2048_d[r:2048:16, :], kvp2048_b)
            w_release(s)
            a_mid()
            s = w_acquire("Q2")
            for p in range(2):
                featmajor_proj(s, p * 128, 256, lambda pst, psb, p=p: evac_copy(qT[:, 4 + p, NT * i:NT * i + NT], pst[:, 0:NT], [psb], [qT_b]))
            w_release(s)

        P.inherit(KVt_b, [xTs_b])
        P.inherit([Oacc_b], [hT2_b])
        for i in range(NKV):
            P.op("pool", lambda e, i=i: e.memset(KVt[i][:, 512:513], 1.0), writes=[KVt_b[i]])
        def vaug_ap(g, blk, h):
            o = voff(g, blk) + h * 65
            return Vaug[:, o:o + 65]

        pt_rr = [0]

        def attn_p1(u):
            h, g, qc, kc_cur, kc_prev, blk, blk_prev, tok_ap, first = u
            p = h // 2
            R0 = 64 * (h % 2)
            gi = 2 * g + p
            pst, psb = next_ps()
            W = 256 if kc_prev is not None else 128
            mm(pst[:, 0:128], kT[R0:R0 + 64, gi, kc_cur:kc_cur + 128], qT[R0:R0 + 64, gi, qc:qc + 128], True, False, [kT_b, qT_b], [psb])
            mm(pst[:, 0:128], ident[:], maskc[:], False, True, [cst], [psb])
            if kc_prev is not None:
                mm(pst[:, 128:256], kT[R0:R0 + 64, gi, kc_prev:kc_prev + 128], qT[R0:R0 + 64, gi, qc:qc + 128], True, False, [kT_b, qT_b], [psb])
                mm(pst[:, 128:256], ident[:], maskp[:], False, True, [cst], [psb])
            i = pt_rr[0]
            pt_rr[0] = (i + 1) % NPT
            P.op("act", lambda e: e.activation(out=pT[i][:, 0:W], in_=pst[:, 0:W], func=AF.Exp, scale=0.125), reads=[psb], writes=[pT_b[i]])
            return i

        def attn_p2(u, i):
            h, g, qc, kc_cur, kc_prev, blk, blk_prev, tok_ap, first = u
            pso, psob = next_ps()
            mm(pso[0:65, 0:128], vaug_ap(g, blk, h), pT[i][:, 0:128], True, kc_prev is None, [Vaug_b, pT_b[i]], [psob])
            if kc_prev is not None:
                mm(pso[0:65, 0:128], vaug_ap(g, blk_prev, h), pT[i][:, 128:256], False, True, [Vaug_b, pT_b[i]], [psob])
            if first:
                P.op("dve", lambda e: e.tensor_copy(out=tok_ap, in_=pso[0:65, 0:128]), reads=[psob], writes=[Oacc_b])
            else:
                P.op("dve", lambda e: e.tensor_tensor(out=tok_ap, in0=tok_ap, in1=pso[0:65, 0:128], op=ALU.add), reads=[psob, Oacc_b], writes=[Oacc_b])

        caches = (c128, c512, c2048)
        combos = [(b, t, g) for b in range(16) for t in range(4) for g in range(3)]
        if STAGE == 4:
            combos = []
        NCB = len(combos)
        NPR, NVB, NS4, NOS = 3, 4, 3, 2
        prod = [sbt("prod%d" % i, [128, 256], F32, cAB) for i in range(NPR)]
        prod_b = [Buf("prod%d" % i) for i in range(NPR)]
        Vb = [sbt("Vb%d" % i, [128, 257], BF16, cAB) for i in range(NVB)]
        Vb_b = [Buf("Vb%d" % i) for i in range(NVB)]
        s4 = [sbt("s4_%d" % i, [128, 4], F32, cAB) for i in range(NS4)]
        s4_b = [Buf("s4_%d" % i) for i in range(NS4)]
        p4 = [sbt("p4_%d" % i, [128, 4], BF16, cAB) for i in range(NS4)]
        p4_b = [Buf("p4_%d" % i) for i in range(NS4)]
        ost = [sbt("ost%d" % i, [4, 257], F32, cAB) for i in range(NOS)]
        ost_b = [Buf("ost%d" % i) for i in range(NOS)]
        AB_bufs += prod_b + Vb_b + s4_b + p4_b + ost_b
        ost_ap = [t[:] for t in ost] + [kvst[0][0:4, 0:257], kvst[1][0:4, 0:257], sdt[0:4, 0:257]]
        ost_b = ost_b + [kvst_b[0], kvst_b[1], sdt_b]
        NOS = len(ost_ap)
        ost_semb = [Buf("ostsem%d" % i) for i in range(NOS)]

        hflat = hTt[:].rearrange("p k t -> p (k t)").bitcast(F32)
        qb_ap = [qb[0][:], qb[1][:], hflat[:, 0:768], hflat[:, 1024:1792]]
        qbx_b = [Buf("qbx0"), Buf("qbx1")]
        P.inherit(qbx_b, [hTt_b])
        qb_bb = [qb_b[0], qb_b[1]] + qbx_b
        NQB = 4
        LEAD = 6

        def cb_A(k):
            b, t, g = combos[k]
            tok = 4 * b + t
            qi = (k // 3) % NQB
            if g == 0:
                P.dma("sp", qb_ap[qi], bass.AP(qs_h, tok * 768, [[0, 128], [1, 768]]), reads=[qs_db], writes=[qb_bb[qi]])
            ki = k % NKV
            if g == 0:
                if t == 0:
                    P.dma("sp", KVt[ki][:, 0:512], c128[b, :, :], writes=[KVt_b[ki]])
                else:
                    P.dma("sp", KVt[ki][16:128, 0:512], c128[b, 16:128, :], writes=[KVt_b[ki]])
                    P.dma("sp", KVt[ki][t:16, 0:512], c128[b, t:16, :], post_writes=[KVt_b[ki]])
                    P.dma("sp", KVt[ki][0:t, 0:512], kvs_d[0, 4 * b:4 * b + t, :], reads=[kvs_ob], post_writes=[KVt_b[ki]])
            elif g == 1:
                P.dma("sp", KVt[ki][:, 0:512], c512[b, t:512:4, :], writes=[KVt_b[ki]])
            else:
                P.dma("sp", KVt[ki][:, 0:512], c2048[b, t:2048:16, :], writes=[KVt_b[ki]])

        def cb_B(k):
            b, t, g = combos[k]
            qi = (k // 3) % NQB
            ki = k % NKV
            j = k % NPR
            v = k % NVB
            P.op("dve", lambda e: e.tensor_tensor(out=prod[j][:], in0=KVt[ki][:, 0:256], in1=qb_ap[qi][:, g * 256:(g + 1) * 256], op=ALU.mult),
                 reads=[KVt_b[ki], qb_bb[qi]], writes=[prod_b[j]])
            P.op("pool", lambda e: e.tensor_copy(out=Vb[v][:], in_=KVt[ki][:, 256:513]), reads=[KVt_b[ki]], writes=[Vb_b[v]])

        def cb_C(k):
            j = k % NPR
            q = k % NS4
            P.op("dve", lambda e: e.tensor_reduce(out=s4[q][:], in_=prod[j][:].rearrange("p (h d) -> p h d", h=4), axis=AX.X, op=ALU.add),
                 reads=[prod_b[j]], writes=[s4_b[q]])

        def cb_D(k):
            q = k % NS4
            P.op("act", lambda e: e.activation(out=p4[q][:], in_=s4[q][:], func=AF.Exp, scale=0.125), reads=[s4_b[q]], writes=[p4_b[q]])

        def cb_E(k):
            b, t, g = combos[k]
            tok = 4 * b + t
            q = k % NS4
            v = k % NVB
            o = k % NOS
            pst, psb = next_ps()
            mm(pst[0:4, 0:257], p4[q][:, 0:4], Vb[v][:, 0:257], True, True, [p4_b[q], Vb_b[v]], [psb])
            P.op("act", lambda e: e.activation(out=ost_ap[o], in_=pst[0:4, 0:257], func=AF.Copy), reads=[psb], writes=[ost_b[o]])
            P.dma("act", od_h.ap()[tok, g, :, :], ost_ap[o], reads=[ost_b[o]], post_writes=[od_db], sb=ost_semb[o])

        combo_i = [0]

        def emit_combo():
            k = combo_i[0]
            if k >= NCB + LEAD + 3:
                return
            combo_i[0] += 1
            for lag, fn in ((0, cb_A), (LEAD, cb_B), (LEAD + 1, cb_C), (LEAD + 2, cb_D), (LEAD + 3, cb_E)):
                kk = k - lag
                if 0 <= kk < NCB:
                    fn(kk)
            if k % 8 == 7 and k >= LEAD:
                emit_deferred(1, [Vb_b[(k - LEAD) % NVB]])

        SK = 2
        for h in range(4):
            p = h // 2
            units = []
            for r in range(16):
                units.append((h, 2, r * 128, r * 128, None, r, None, Oacc[0:65, r:S:16], True))
            for r in range(4):
                for j in range(4):
                    units.append((h, 1, r * 512 + 128 * j, r * 512 + 128 * j, (r * 512 + 128 * (j - 1)) if j > 0 else None,
                                  r * 4 + j, r * 4 + j - 1, Oacc[0:65, 512 * j + r:512 * j + 512:4], False))
            for j in range(16):
                units.append((h, 0, 128 * j, 128 * j, 128 * (j - 1) if j > 0 else None, j, j - 1, Oacc[0:65, 128 * j:128 * j + 128], False))
            pend = []
            for u in units:
                pend.append((u, attn_p1(u)))
                if len(pend) > SK:
                    attn_p2(*pend.pop(0))
                    emit_combo()
            while pend:
                attn_p2(*pend.pop(0))
                emit_combo()
            Ro = slice(0, 64) if h % 2 == 0 else slice(64, 128)
            for c4 in range(4):
                cs = slice(512 * c4, 512 * c4 + 512)
                pst, psb = next_ps()
                mm(pst[0:64, 0:512], onesf[64:65, 0:64], Oacc[64:65, cs], True, True, [Oacc_b, Vaug_b], [psb])
                P.op("dve", lambda e, pst=pst: e.reciprocal(out=rden[0:64, :], in_=pst[0:64, 0:512]), reads=[psb], writes=[rden_b])
                P.op("dve", lambda e, cs=cs, Ro=Ro: e.tensor_tensor(out=OaT[Ro, p, cs], in0=Oacc[0:64, cs], in1=rden[0:64, :], op=ALU.mult),
                     reads=[Oacc_b, rden_b], writes=[OaT_b])
        while combo_i[0] < NCB + LEAD + 3:
            emit_combo()
        emit_deferred(100, [])

        if STAGE < 6:
            P.finish([y_ob, kvp128_b, kvp512_b, kvp2048_b, kvs_ob, sg_ob, qs_db, od_db, OaT_b] + slot_b)
            cAB.close()
            return nc
        if os.environ.get("MK_DBG"):
            print("SBUF remaining in AB region (before finalisation tensors):", nc.sbuf_bytes_remaining)
        pv = sbt("pv", [64, 768], F32, cAB); pv_b = Buf("pv")
        dn = sbt("dn", [64, 12], F32, cAB); dn_b = Buf("dn")
        sprod = sbt("sprod", [64, 768], F32, cAB); sprod_b = Buf("sprod")
        sself = sbt("sself", [64, 12], F32, cAB); sself_b = Buf("sself")
        pself = sbt("pself", [64, 12], F32, cAB); pself_b = Buf("pself")
        Osm = sbt("Osm", [64, 256], F32, cAB); Osm_b = Buf("Osm")
        den4 = sbt("den4", [64, 4], F32, cAB); den4_b = Buf("den4")
        AB_bufs += [pv_b, dn_b, sprod_b, sself_b, pself_b, Osm_b, den4_b]
        for g in range(3):
            P.dma("sp", pv[:, g * 256:(g + 1) * 256].rearrange("t (h d) -> t h d", h=4),
                  bass.AP(od_h, g * 1028, [[3084, 64], [321, 4], [1, 64]]), reads=[od_db], writes=[pv_b])
            P.dma("sp", dn[:, g * 4:(g + 1) * 4],
                  bass.AP(od_h, g * 1028 + 256, [[3084, 64], [257, 4]]), reads=[od_db], writes=[dn_b], allow_slow_non_contiguous=True)
        P.op("dve", lambda e: e.tensor_tensor(out=sprod[:].rearrange("t (g n) -> t g n", g=3), in0=qs_s[:].rearrange("t (g n) -> t g n", g=3),
                                              in1=kvs_s[:, :, 0:256], op=ALU.mult), reads=[qs_sb, kvs_sb], writes=[sprod_b])
        P.op("dve", lambda e: e.tensor_reduce(out=sself[:], in_=sprod[:].rearrange("t (a d) -> t a d", d=64), axis=AX.X, op=ALU.add),
             reads=[sprod_b], writes=[sself_b])
        P.op("act", lambda e: e.activation(out=pself[:], in_=sself[:], func=AF.Exp, scale=0.125), reads=[sself_b], writes=[pself_b])
        P.op("dve", lambda e: e.tensor_tensor(out=sprod[:].rearrange("t (g h d) -> t g h d", g=3, h=4),
                                              in0=kvs_s[:, :, 256:512].rearrange("t g (h d) -> t g h d", h=4),
                                              in1=bass.AP(pself, 0, [[12, 64], [4, 3], [1, 4], [0, 64]]), op=ALU.mult),
             reads=[kvs_sb, pself_b, sprod_b], writes=[sprod_b])
        P.op("dve", lambda e: e.tensor_tensor(out=pv[:], in0=pv[:], in1=sprod[:], op=ALU.add), reads=[pv_b, sprod_b], writes=[pv_b])
        P.op("dve", lambda e: e.tensor_tensor(out=Osm[:], in0=pv[:, 0:256], in1=pv[:, 256:512], op=ALU.add), reads=[pv_b], writes=[Osm_b])
        P.op("dve", lambda e: e.tensor_tensor(out=Osm[:], in0=Osm[:], in1=pv[:, 512:768], op=ALU.add), reads=[pv_b, Osm_b], writes=[Osm_b])
        P.op("dve", lambda e: e.tensor_tensor(out=dn[:], in0=dn[:], in1=pself[:], op=ALU.add), reads=[dn_b, pself_b], writes=[dn_b])
        P.op("dve", lambda e: e.tensor_tensor(out=den4[:], in0=dn[:, 0:4], in1=dn[:, 4:8], op=ALU.add), reads=[dn_b], writes=[den4_b])
        P.op("dve", lambda e: e.tensor_tensor(out=den4[:], in0=den4[:], in1=dn[:, 8:12], op=ALU.add), reads=[dn_b, den4_b], writes=[den4_b])
        P.op("dve", lambda e: e.reciprocal(out=den4[:], in_=den4[:]), reads=[den4_b], writes=[den4_b])
        P.op("dve", lambda e: e.tensor_tensor(out=Osm[:].rearrange("t (h d) -> t h d", h=4), in0=Osm[:].rearrange("t (h d) -> t h d", h=4),
                                              in1=bass.AP(den4, 0, [[4, 64], [1, 4], [0, 64]]), op=ALU.mult),
             reads=[Osm_b, den4_b], writes=[Osm_b])
        for pp in range(2):
            pst, psb = next_ps()
            mm(pst[:, 0:64], Osm[0:64, pp * 128:(pp + 1) * 128], identf[0:64, 0:64], True, True, [Osm_b, cst], [psb])
            evac_copy(OaT[:, pp, S:S + NS], pst[:, 0:64], [psb], [OaT_b])

        if STAGE < 7:
            P.finish([y_ob, kvp128_b, kvp512_b, kvp2048_b, kvs_ob, sg_ob, qs_db, od_db, OaT_b] + slot_b)
            cAB.close()
            return nc
        cAB.close()

        cC = ExitStack()
        xNt = sbt("xNt", [128, 4, D], F32, cC); xNt_b = Buf("xNt")
        fst = sbt("fst", [128, 4, 512], F32, cC); fst_b = Buf("fst")
        fsl_b = [Buf("fsl%d" % i) for i in range(4)]
        mT = sbt("mT", [128, 8, NT], BF16, cC); mT_b = Buf("mT")
        h2b = sbt("h2b", [128, D], BF16, cC); h2b_b = Buf("h2b")
        h2T = sbt("h2T", [128, 8, NT], BF16, cC); h2T_b = Buf("h2T")
        uT = sbt("uT", [128, 32, NT], BF16, cC); uT_b = Buf("uT")
        rt = [sbt("rt%d" % i, [128, NT], BF16, cC) for i in range(2)]
        rt_b = [Buf("rt%d" % i) for i in range(2)]
        gA = [sbt("gA%d" % i, [128, NT], F32, cC) for i in range(2)]
        gA_b = [Buf("gA%d" % i) for i in range(2)]
        gB = [sbt("gB%d" % i, [128, NT], F32, cC) for i in range(2)]
        gB_b = [Buf("gB%d" % i) for i in range(2)]
        obt = sbt("obt", [128, 4, NT], BF16, cC); obt_b = Buf("obt")
        z2f = [fst[:, i, :] for i in range(2)]
        z2f_b = [fsl_b[0], fsl_b[1]]
        hTt2 = sbt("hTt2", [128, 8, NT], BF16, cC); hTt2_b = Buf("hTt2")
        hC = [(hTt, hTt_b), (hTt2, hTt2_b)]
        znb = [sbt("znb%d" % i, [128, 512], BF16, cC) for i in range(2)]
        znb_b = [Buf("znb%d" % i) for i in range(2)]
        tmpf = [sbt("tmpf%d" % i, [128, 512], F32, cC) for i in range(2)]
        tmpf_b = [Buf("tmpf%d" % i) for i in range(2)]
        st = [sbt("st%d" % i, [128, 8], F32, cC) for i in range(2)]
        st_b = [Buf("st%d" % i) for i in range(2)]
        gpm = sbt("gpm", [128, D], F32, cC)
        gpf = sbt("gpf", [128, D], F32, cC)
        lng = sbt("lng", [128, 512], F32, cC)
        lnb = sbt("lnb", [128, 512], F32, cC)
        cst2 = Buf("consts2")
        xNb = [Buf("xNb%d" % i) for i in range(4)]
        C_bufs = xNb + [xNt_b, fst_b, mT_b] + fsl_b + [h2b_b, h2T_b, uT_b, obt_b, cst2] + rt_b + gA_b + gB_b + z2f_b + znb_b + tmpf_b + st_b
        P.inherit(C_bufs, AB_bufs)
        P.inherit([xTs_b], KVt_b)
        P.inherit([hTt_b], qbx_b)
        for (t, d_) in ((gpm, gpm_d), (gpf, gpf_d), (lng, lng_d), (lnb, lnb_d)):
            P.dma("sp", t[:], d_, writes=[cst2])

        rr = {"z": 0, "g": 0, "t": 0, "r": 0, "s": 0}

        def nxt(k, n=2):
            v = rr[k]
            rr[k] = (v + 1) % n
            return v

        def rstd_from_sums(si, cols, n, bs):
            stt, stb = st[si], st_b[si]
            if len(cols) == 2:
                P.op("dve", lambda e: e.tensor_tensor(out=stt[0:bs, 5:6], in0=stt[0:bs, cols[0]:cols[0] + 1], in1=stt[0:bs, cols[1]:cols[1] + 1], op=ALU.add),
                     reads=[stb], writes=[stb])
                c = 5
            else:
                c = cols[0]
            P.op("act", lambda e: e.activation(out=stt[0:bs, 6:7], in_=stt[0:bs, c:c + 1], func=AF.Sqrt, bias=epsT[0:bs, 0:1], scale=1.0 / n),
                 reads=[stb, cst], writes=[stb])
            P.op("dve", lambda e: e.reciprocal(out=stt[0:bs, 7:8], in_=stt[0:bs, 6:7]), reads=[stb], writes=[stb])
            return stt[0:bs, 7:8]

        z2f_all = [fst[:, i, :] for i in range(4)]
        z2f_allb = list(fsl_b)
        P.inherit([hTt2_b], AB_bufs)
        st4 = [sbt("st4_%d" % i, [128, 8], F32, cC) for i in range(4)]
        st4_b = [Buf("st4_%d" % i) for i in range(4)]
        st5 = [sbt("st5_%d" % i, [128, 8], F32, cC) for i in range(4)]
        st5_b = [Buf("st5_%d" % i) for i in range(4)]
        P.inherit(st5_b, AB_bufs)
        P.inherit(z2f_allb[2:] + st4_b, AB_bufs)
        znb4 = [sbt("znb4_%d" % i, [128, 512], BF16, cC) for i in range(2)]
        znb_all = znb + znb4
        znb_allb = znb_b + [Buf("znb4_%d" % i) for i in range(2)]
        P.inherit(znb_allb[2:], AB_bufs)
        h2b2 = sbt("h2b2", [128, D], BF16, cC)
        h2bs = [h2b, h2b2]
        h2bs_b = [h2b_b, Buf("h2b2")]
        P.inherit([h2bs_b[1]], AB_bufs)

        if os.environ.get("MK_DBG"):
            print("SBUF remaining in C region:", nc.sbuf_bytes_remaining)

        ysem_b = Buf("ysem")
        uTf_b = [Buf("uTf%d" % i) for i in range(8)]

        def gm_front_a(tok0, nt, nblk, bs, H):
            sample = (nblk == 1)
            s = w_acquire("Z2")
            zps = []
            for b in range(nblk):
                pst, psb = next_ps()
                for kc in range(8):
                    mm(pst[0:bs, 0:512], H[0][:, kc, b * bs:(b + 1) * bs], slots[s][:, kc * 512:(kc + 1) * 512], kc == 0, kc == 7,
                       [slot_b[s], H[1]], [psb])
                zps.append((pst, psb))
            w_release(s)
            zts = [(z2f_all[b], z2f_allb[b], st4[b], st4_b[b]) for b in range(nblk)]
            for b in range(nblk):
                P.op("pool", lambda e, b=b: e.memset(zts[b][2][:], 0.0), writes=[zts[b][3]])
            for b in range(nblk):
                pst, psb = zps[b]
                zt, ztb, stt, stb = zts[b]
                P.op("act", lambda e, zt=zt, pst=pst, stt=stt: e.activation(out=zt[0:bs, :], in_=pst[0:bs, 0:512], func=AF.Gelu, accum_out=stt[0:bs, 0:1]),
                     reads=[psb, stb], writes=[ztb, stb])
                P.op("act", lambda e, zt=zt, stt=stt: e.activation(out=junk[0:bs, 0:512], in_=zt[0:bs, :], func=AF.Square, accum_out=stt[0:bs, 1:2]),
                     reads=[ztb, stb], writes=[junk_b, stb])
            return zts

        def gm_front_b(tok0, nt, nblk, bs, H, zts):
            sample = (nblk == 1)
            s = w_acquire("Z1")
            for c in range(4):
                pst, psb = next_ps()
                for kc in range(8):
                    mm(pst[:, 0:nt], slots[s][:, kc * 512 + c * 128: kc * 512 + c * 128 + 128], H[0][:, kc, 0:nt], kc == 0, kc == 7,
                       [slot_b[s], H[1]], [psb])
                P.op("act", lambda e, c=c, pst=pst: e.activation(out=obt[:, c, 0:nt], in_=pst[:, 0:nt], func=AF.Gelu), reads=[psb], writes=[obt_b])
            w_release(s)
            for b in range(nblk):
                zt, ztb, stt, stb = zts[b]
                P.op("dve", lambda e, stt=stt: e.tensor_scalar(out=stt[0:bs, 2:3], in0=stt[0:bs, 0:1], scalar1=1.0 / 512, scalar2=None, op0=ALU.mult),
                     reads=[stb], writes=[stb])
                P.op("dve", lambda e, stt=stt: e.tensor_tensor(out=stt[0:bs, 3:4], in0=stt[0:bs, 2:3], in1=stt[0:bs, 2:3], op=ALU.mult),
                     reads=[stb], writes=[stb])
                P.op("dve", lambda e, stt=stt: e.scalar_tensor_tensor(out=stt[0:bs, 4:5], in0=stt[0:bs, 1:2], scalar=1.0 / 512, in1=stt[0:bs, 3:4],
                                                                      op0=ALU.mult, op1=ALU.subtract), reads=[stb], writes=[stb])
            for b in range(nblk):
                zt, ztb, stt, stb = zts[b]
                P.op("act", lambda e, stt=stt: e.activation(out=stt[0:bs, 6:7], in_=stt[0:bs, 4:5], func=AF.Sqrt, bias=epsT[0:bs, 0:1], scale=1.0),
                     reads=[stb, cst], writes=[stb])
            for b in range(nblk):
                zt, ztb, stt, stb = zts[b]
                P.op("dve", lambda e, stt=stt: e.reciprocal(out=stt[0:bs, 7:8], in_=stt[0:bs, 6:7]), reads=[stb], writes=[stb])
                P.op("dve", lambda e, zt=zt, stt=stt: e.tensor_scalar(out=zt[0:bs, :], in0=zt[0:bs, :], scalar1=stt[0:bs, 2:3], scalar2=stt[0:bs, 7:8],
                                                                      op0=ALU.subtract, op1=ALU.mult), reads=[ztb, stb], writes=[ztb])
                P.op("dve", lambda e, zt=zt: e.tensor_tensor(out=zt[0:bs, :], in0=zt[0:bs, :], in1=lng[0:bs, :], op=ALU.mult), reads=[ztb, cst2], writes=[ztb])
                zb, zbb = znb_all[b], znb_allb[b]
                if not sample:
                    P.op("pool", lambda e, zt=zt, zb=zb: e.tensor_tensor(out=zb[0:bs, :], in0=zt[0:bs, :], in1=lnb[0:bs, :], op=ALU.add), reads=[ztb, cst2], writes=[zbb])
                else:
                    P.op("pool", lambda e, zt=zt: e.tensor_tensor(out=zt[0:bs, :], in0=zt[0:bs, :], in1=lnb[0:bs, :], op=ALU.add), reads=[ztb, cst2], writes=[ztb])
                    P.op("pool", lambda e, zt=zt, zb=zb: e.tensor_copy(out=zb[0:bs, :], in_=zt[0:bs, :]), reads=[ztb], writes=[zbb])
                    P.dma("pool", sg_d, zt[0:bs, :], reads=[ztb], writes=[sg_ob], sb=ztb)

        def gm_back(tok0, nt, nblk, bs):
            sample = (nblk == 1)
            for b in range(nblk):
                zb, zbb = znb_all[b], znb_allb[b]
                pm, pmb = next_ps()
                if not sample:
                    for g4 in range(4):
                        cs = slice(g4 * 128, (g4 + 1) * 128)
                        mm(pm[:, cs], zb[:, cs], wspT[:, cs], True, False, [zbb, cst, cstL], [pmb])
                        mm(pm[:, cs], ident[:], bsp_hi[:, cs], False, False, [cst, cstL], [pmb])
                        mm(pm[:, cs], ident[:], bsp_lo[:, cs], False, True, [cst, cstL], [pmb])
                    ov = obt[:, :, b * 128:(b + 1) * 128]
                    P.op("dve", lambda e, ov=ov, pm=pm: e.tensor_tensor(out=ov, in0=ov, in1=pm[:, 0:512].rearrange("p (g t) -> p g t", g=4), op=ALU.mult),
                         reads=[pmb, obt_b], writes=[obt_b])
                else:
                    for g4 in range(4):
                        mm(pm[:, g4 * 64:(g4 + 1) * 64], zb[0:64, g4 * 128:(g4 + 1) * 128], wsS[0:64, g4 * 64:(g4 + 1) * 64], True, True, [zbb, cst, cstL], [pmb])
                    ti = nxt("t")
                    P.op("dve", lambda e, pm=pm, ti=ti: e.tensor_tensor(out=tmpf[ti][:, 0:256], in0=pm[:, 0:256], in1=bspS[:, 0:256], op=ALU.add),
                         reads=[pmb, cst, cstL], writes=[tmpf_b[ti]])
                    ov = obt[:, :, 0:64]
                    P.op("dve", lambda e, ov=ov, ti=ti: e.tensor_tensor(out=ov, in0=ov, in1=tmpf[ti][:, 0:256].rearrange("p (g t) -> p g t", g=4), op=ALU.mult),
                         reads=[tmpf_b[ti], obt_b], writes=[obt_b])

        def phase_c(tok0, nt, nblk, bs, H, nxt_tile):
            sample = (nblk == 1)
            for b in range(nblk):
                P.dma("sp", xNt[0:bs, b, :], xN[tok0 + b * bs: tok0 + (b + 1) * bs, :], writes=[xNb[b]])
            nx_rs = None
            if nxt_tile is not None:
                nx_rs = hT_part1(xT, nxt_tile[0], nxt_tile[1], "reuse", nxt_tile[0])
            sA = w_acquire("AO")
            sB = w_acquire("BO")
            for j in range(4):
                sG = w_acquire("GP%d" % j)
                for cc in range(2):
                    c = 2 * j + cc
                    pga, pgab = next_ps()
                    for kc in range(8):
                        mm(pga[:, 0:nt], slots[sG][:, kc * 512 + cc * 128: kc * 512 + cc * 128 + 128], H[0][:, kc, 0:nt], kc == 0, kc == 7,
                           [slot_b[sG], H[1]], [pgab])
                    pgb, pgbb = next_ps()
                    for kc in range(8):
                        mm(pgb[:, 0:nt], slots[sG][:, kc * 512 + 256 + cc * 128: kc * 512 + 256 + cc * 128 + 128], H[0][:, kc, 0:nt], kc == 0, kc == 7,
                           [slot_b[sG], H[1]], [pgbb])
                    pa, pab = next_ps()
                    for pp in range(2):
                        mm(pa[:, 0:nt], slots[sA][:, pp * 1024 + c * 128: pp * 1024 + c * 128 + 128], OaT[:, pp, tok0:tok0 + nt], pp == 0, pp == 1,
                           [slot_b[sA], OaT_b], [pab])
                    pb, pbb = next_ps()
                    for kc in range(4):
                        mm(pb[:, 0:nt], slots[sB][:, kc * 1024 + c * 128: kc * 1024 + c * 128 + 128], obt[:, kc, 0:nt], kc == 0, kc == 3,
                           [slot_b[sB], obt_b], [pbb])
                    gi = nxt("g")
                    P.op("act", lambda e, gi=gi, pga=pga, c=c: e.activation(out=gA[gi][:, 0:nt], in_=pga[:, 0:nt], func=AF.Sigmoid, bias=bgate[:, c:c + 1], scale=1.0),
                         reads=[pgab, cst], writes=[gA_b[gi]])
                    P.op("act", lambda e, gi=gi, pgb=pgb, c=c: e.activation(out=gB[gi][:, 0:nt], in_=pgb[:, 0:nt], func=AF.Sigmoid, bias=bgate[:, 8 + c:9 + c], scale=1.0),
                         reads=[pgbb, cst], writes=[gB_b[gi]])
                    P.op("dve", lambda e, gi=gi, pa=pa: e.tensor_tensor(out=gA[gi][:, 0:nt], in0=gA[gi][:, 0:nt], in1=pa[:, 0:nt], op=ALU.mult),
                         reads=[gA_b[gi], pab], writes=[gA_b[gi]])
                    P.op("dve", lambda e, gi=gi, pb=pb: e.tensor_tensor(out=gB[gi][:, 0:nt], in0=gB[gi][:, 0:nt], in1=pb[:, 0:nt], op=ALU.mult),
                         reads=[gB_b[gi], pbb], writes=[gB_b[gi]])
                    P.op("pool", lambda e, gi=gi, c=c: e.tensor_tensor(out=mT[:, c, 0:nt], in0=gA[gi][:, 0:nt], in1=gB[gi][:, 0:nt], op=ALU.add),
                         reads=[gA_b[gi], gB_b[gi]], writes=[mT_b])
                w_release(sG)
                if j == 0 and nxt_tile is not None:
                    hT_part2(nxt_tile[1], nx_rs[0], nx_rs[1], nxt_tile[4][0], nxt_tile[4][1])
            w_release(sA)
            w_release(sB)
            so = [w_acquire("OUT0"), w_acquire("OUT1")]
            blk_state = {}

            uTf = uT[:].rearrange("p f t -> p (f t)").bitcast(F32)
            P.inherit(uTf_b, [uT_b])
            sts = [(st5[b], st5_b[b]) for b in range(nblk)]

            def pre(b, hf):
                i0 = 2 * b + hf
                return uTf[0:bs, i0 * 512:(i0 + 1) * 512], uTf_b[i0]

            for b in range(nblk):
                stt, stb = sts[b]
                P.op("pool", lambda e, stt=stt: e.memset(stt[:], 0.0), writes=[stb])
                for hf in range(2):
                    pst, psb = next_ps()
                    for kc in range(8):
                        mm(pst[0:bs, 0:512], mT[:, kc, b * bs:(b + 1) * bs], slots[so[hf]][:, kc * 512:(kc + 1) * 512], kc == 0, kc == 7,
                           [slot_b[so[hf]], mT_b], [psb])
                    P.op("act", lambda e, pst=pst, stt=stt, hf=hf: e.activation(out=junk[0:bs, 0:512], in_=pst[0:bs, 0:512], func=AF.Square,
                                                                                accum_out=stt[0:bs, hf:hf + 1]),
                         reads=[psb, stb], writes=[junk_b, stb])
                    pa, pab = pre(b, hf)
                    P.op("dve", lambda e, pst=pst, pa=pa, hf=hf: e.tensor_tensor(out=pa, in0=pst[0:bs, 0:512], in1=gpm[0:bs, hf * 512:(hf + 1) * 512],
                                                                                 op=ALU.mult), reads=[psb, cst2], writes=[pab])
            w_release(so[0])
            w_release(so[1])
            for b in range(nblk):
                stt, stb = sts[b]
                P.op("dve", lambda e, stt=stt: e.tensor_tensor(out=stt[0:bs, 5:6], in0=stt[0:bs, 0:1], in1=stt[0:bs, 1:2], op=ALU.add), reads=[stb], writes=[stb])
            for b in range(nblk):
                stt, stb = sts[b]
                P.op("act", lambda e, stt=stt: e.activation(out=stt[0:bs, 6:7], in_=stt[0:bs, 5:6], func=AF.Sqrt, bias=epsT[0:bs, 0:1], scale=1.0 / D),
                     reads=[stb, cst], writes=[stb])
            for b in range(nblk):
                stt, stb = sts[b]
                P.op("dve", lambda e, stt=stt: e.reciprocal(out=stt[0:bs, 7:8], in_=stt[0:bs, 6:7]), reads=[stb], writes=[stb])
                for hf in range(2):
                    pa, pab = pre(b, hf)
                    P.op("dve", lambda e, pa=pa, hf=hf, b=b, stt=stt: e.scalar_tensor_tensor(
                        out=xNt[0:bs, b, hf * 512:(hf + 1) * 512], in0=pa, scalar=stt[0:bs, 7:8], in1=xNt[0:bs, b, hf * 512:(hf + 1) * 512],
                        op0=ALU.mult, op1=ALU.add), reads=[pab, stb, xNb[b]], writes=[xNb[b]])
            for b in range(nblk):
                stt, stb = sts[b]
                P.op("act", lambda e, stt=stt, b=b: e.activation(out=junk[0:bs, :], in_=xNt[0:bs, b, :], func=AF.Square, accum_out=stt[0:bs, 2:3]),
                     reads=[xNb[b], stb], writes=[junk_b, stb])
            for b in range(nblk):
                stt, stb = sts[b]
                P.op("act", lambda e, stt=stt: e.activation(out=stt[0:bs, 3:4], in_=stt[0:bs, 2:3], func=AF.Sqrt, bias=epsT[0:bs, 0:1], scale=1.0 / D),
                     reads=[stb, cst], writes=[stb])
            zts_n = None
            if nxt_tile is not None:
                zts_n = gm_front_a(*nxt_tile)
            for b in range(nblk):
                stt, stb = sts[b]
                P.op("dve", lambda e, stt=stt: e.reciprocal(out=stt[0:bs, 4:5], in_=stt[0:bs, 3:4]), reads=[stb], writes=[stb])
                hb, hbb = h2bs[b % 2], h2bs_b[b % 2]
                P.op("dve", lambda e, b=b, stt=stt, hb=hb: e.tensor_scalar(out=hb[0:bs, :], in0=xNt[0:bs, b, :], scalar1=stt[0:bs, 4:5], scalar2=None, op0=ALU.mult),
                     reads=[xNb[b], stb], writes=[hbb])
                for kc in range(8):
                    P.op("pe", lambda e, kc=kc, hb=hb: e.transpose(out=psT_t[:, kc * 128: kc * 128 + bs], in_=hb[0:bs, kc * 128:(kc + 1) * 128], identity=ident[0:bs, 0:bs]),
                         reads=[hbb, cst], writes=[psT_b], inc=(kc == 7))
                P.op("dve", lambda e, b=b: e.tensor_tensor(out=h2T[:, :, b * bs:(b + 1) * bs], in0=psT_t[:, :].rearrange("p (k t) -> p k t", k=8)[:, :, 0:bs],
                                                           in1=bass.AP(gffn, 0, [[8, 128], [1, 8], [0, bs]]), op=ALU.mult),
                     reads=[psT_b, cst], writes=[h2T_b])
            P.inherit([uT_b], uTf_b)
            if nxt_tile is not None:
                gm_front_b(*nxt_tile, zts_n)
            for j in range(8):
                s = w_acquire("UP%d" % j)
                for cc in range(4):
                    fc = 4 * j + cc
                    pst, psb = next_ps()
                    for kc in range(8):
                        mm(pst[:, 0:nt], slots[s][:, kc * 512 + cc * 128: kc * 512 + cc * 128 + 128], h2T[:, kc, 0:nt], kc == 0, kc == 7,
                           [slot_b[s], h2T_b], [psb])
                    ri = nxt("r")
                    P.op("act", lambda e, pst=pst, ri=ri: e.activation(out=rt[ri][:, 0:nt], in_=pst[:, 0:nt], func=AF.Relu), reads=[psb], writes=[rt_b[ri]])
                    P.op("pool", lambda e, ri=ri, fc=fc: e.tensor_tensor(out=uT[:, fc, 0:nt], in0=rt[ri][:, 0:nt], in1=rt[ri][:, 0:nt], op=ALU.mult),
                         reads=[rt_b[ri]], writes=[uT_b])
                w_release(s)
            if nxt_tile is not None:
                gm_back(*nxt_tile[:4])
            sis = []
            for b in range(nblk):
                si_ = b % 2
                sis.append(si_)
            P.op("pool", lambda e: e.memset(st[0][:], 0.0), writes=[st_b[0]])
            P.op("pool", lambda e: e.memset(st[1][:], 0.0), writes=[st_b[1]])
            def scol(b, hf):
                return (b // 2) * 2 + hf
            for hf in range(2):
                acc = [next_ps() for _ in range(nblk)]
                for j in range(4):
                    s = w_acquire("DN%d_%d" % (hf, j))
                    for b in range(nblk):
                        pst, psb = acc[b]
                        for f8 in range(8):
                            fc = 8 * j + f8
                            mm(pst[0:bs, 0:512], uT[:, fc, b * bs:(b + 1) * bs], slots[s][:, f8 * 512:(f8 + 1) * 512], j == 0 and f8 == 0, j == 3 and f8 == 7,
                               [slot_b[s], uT_b], [psb])
                    w_release(s)
                for b in range(nblk):
                    pst, psb = acc[b]
                    stt, stb = st[b % 2], st_b[b % 2]
                    c0 = scol(b, hf)
                    P.op("act", lambda e, pst=pst, stt=stt, c0=c0: e.activation(out=junk[0:bs, 0:512], in_=pst[0:bs, 0:512], func=AF.Square,
                                                                                accum_out=stt[0:bs, c0:c0 + 1]),
                         reads=[psb, stb], writes=[junk_b, stb])
                    if hf == 0:
                        P.op("dve", lambda e, pst=pst, b=b: e.tensor_tensor(out=fst[0:bs, b, :], in0=pst[0:bs, 0:512], in1=gpf[0:bs, 0:512], op=ALU.mult),
                             reads=[psb, cst2], writes=[fsl_b[b]])
                    else:
                        ti = nxt("t")
                        P.op("dve", lambda e, pst=pst, ti=ti: e.tensor_tensor(out=tmpf[ti][0:bs, :], in0=pst[0:bs, 0:512], in1=gpf[0:bs, 512:1024], op=ALU.mult),
                             reads=[psb, cst2], writes=[tmpf_b[ti]])
                        rs = rstd_from_sums(b % 2, [scol(b, 0), scol(b, 1)], D, bs)
                        P.op("dve", lambda e, b=b, rs=rs: e.scalar_tensor_tensor(out=xNt[0:bs, b, 0:512], in0=fst[0:bs, b, :], scalar=rs, in1=xNt[0:bs, b, 0:512],
                                                                                 op0=ALU.mult, op1=ALU.add), reads=[fsl_b[b], stb, xNb[b]], writes=[xNb[b]])
                        P.op("dve", lambda e, b=b, rs=rs, ti=ti: e.scalar_tensor_tensor(out=xNt[0:bs, b, 512:1024], in0=tmpf[ti][0:bs, :], scalar=rs,
                                                                                        in1=xNt[0:bs, b, 512:1024], op0=ALU.mult, op1=ALU.add),
                             reads=[tmpf_b[ti], stb, xNb[b]], writes=[xNb[b]])
                        P.dma("pool", y_d[tok0 + b * bs: tok0 + (b + 1) * bs, :], xNt[0:bs, b, :], reads=[xNb[b]], post_writes=[y_ob], sb=ysem_b)

        compute_hT(xT, 0, NT, "reuse", 0)
        tiles = [(NT * i, NT, 4, 128, hC[i % 2]) for i in range(4)] + [(S, NS, 1, 64, hC[0])]
        zts0 = gm_front_a(*tiles[0])
        gm_front_b(*tiles[0], zts0)
        gm_back(*tiles[0][:4])
        for i in range(5):
            phase_c(*tiles[i], tiles[i + 1] if i < 4 else None)
        assert WS.i_use == len(worder), (WS.i_use, len(worder))

        P.finish([y_ob, kvp128_b, kvp512_b, kvp2048_b, kvs_ob, sg_ob])
        cC.close()
    return nc


_NC_CACHE = {}


def kernel(x_prompt, x_sample, cache_kv_w128, cache_kv_w512, cache_kv_w2048,
           norm_pre_mix, w_in, b_gate, ln_z_g, ln_z_b, w_spatial, b_spatial,
           w_ao, w_bo, w_out, norm_post_mix, norm_pre_ffn, w_up, w_down, norm_post_ffn):
    f = lambda a: np.ascontiguousarray(np.asarray(a, dtype=np.float32))
    x_prompt = np.asarray(x_prompt, dtype=np.float32)
    x_sample = np.asarray(x_sample, dtype=np.float32)
    c128 = np.asarray(cache_kv_w128, dtype=np.float32)[0].reshape(128, 128, 512)
    c512 = np.asarray(cache_kv_w512, dtype=np.float32)[0].reshape(128, 512, 512)
    c2048 = np.asarray(cache_kv_w2048, dtype=np.float32)[0].reshape(128, 2048, 512)
    rep = lambda v, n=128: f(np.broadcast_to(np.asarray(v, dtype=np.float32).reshape(1, -1), (n, np.asarray(v).size)))
    pk = lambda v: f(np.asarray(v, dtype=np.float32).reshape(-1, 128).T)
    wsp = np.asarray(w_spatial, dtype=np.float32)[0]
    bsp = np.asarray(b_spatial, dtype=np.float32)[0]
    wspT = f(wsp.transpose(2, 0, 1).reshape(128, 512))
    wsS = np.zeros((64, 4, 64), np.float32)
    for b in range(16):
        wsS[4 * b:4 * b + 4, :, 4 * b:4 * b + 4] = wsp[:, 0:4, 0:4].transpose(2, 0, 1)
    bspS = np.tile(bsp[:, 0:4], (1, 16)).reshape(1, 256)
    shared = {
        "w_in": f(np.asarray(w_in)[0]), "w_ao": f(np.asarray(w_ao)[0]), "w_bo": f(np.asarray(w_bo)[0]),
        "w_out": f(np.asarray(w_out)[0]), "w_up": f(np.asarray(w_up)[0]), "w_down": f(np.asarray(w_down)[0]),
        "gpre": pk(np.asarray(norm_pre_mix)[0]), "bgate": pk(np.asarray(b_gate)[0]),
        "gpm": rep(np.asarray(norm_post_mix)[0]), "gffn": pk(np.asarray(norm_pre_ffn)[0]),
        "gpf": rep(np.asarray(norm_post_ffn)[0]), "lng": rep(np.asarray(ln_z_g)[0]), "lnb": rep(np.asarray(ln_z_b)[0]),
        "wspT": wspT, "bsp": rep(bsp.reshape(-1)), "wsS": f(wsS.reshape(64, 256)), "bspS": rep(bspS),
    }
    in_maps = []
    for c in range(NCORES):
        xp = x_prompt[c]
        xs = x_sample[16 * c:16 * c + 16].reshape(NS, D)
        xNc = np.concatenate([xp, xs], axis=0)
        m = dict(shared)
        m["xN"] = f(xNc)
        m["xT"] = f(xNc.T)
        m["xT16"] = f(xp.T.reshape(D, 128, 16).transpose(0, 2, 1).reshape(D, S))
        m["c128"] = f(c128[16 * c:16 * c + 16])
        m["c512"] = f(c512[16 * c:16 * c + 16])
        m["c2048"] = f(c2048[16 * c:16 * c + 16])
        in_maps.append(m)
    if "nc" not in _NC_CACHE:
        _NC_CACHE["nc"] = build_program()
    nc = _NC_CACHE["nc"]
    ndbg = int(os.environ.get("MK_NCORES", str(NCORES)))
    if ndbg < NCORES:
        res = run_bass_kernel_spmd(nc, in_maps[:ndbg], core_ids=list(range(ndbg)))
        R = list(res.results) + [res.results[0]] * (NCORES - ndbg)
    else:
        res = run_bass_kernel_spmd(nc, in_maps, core_ids=list(range(NCORES)))
        R = res.results
    y_p = np.stack([R[c]["y"][0:S] for c in range(NCORES)], 0)
    y_s = np.concatenate([R[c]["y"][S:].reshape(16, 4, D) for c in range(NCORES)], 0)
    kvp128 = np.stack([R[c]["kvp128"].reshape(128, 2, 4, 64) for c in range(NCORES)], 0)[None]
    kvp512 = np.stack([R[c]["kvp512"].reshape(512, 2, 4, 64) for c in range(NCORES)], 0)[None]
    kvp2048 = np.stack([R[c]["kvp2048"].reshape(2048, 2, 4, 64) for c in range(NCORES)], 0)[None]
    kvs = [np.concatenate([R[c]["kvs"][g].reshape(16, 4, 2, 4, 64) for c in range(NCORES)], 0)[None] for g in range(3)]
    sg = np.concatenate([R[c]["sg"].reshape(16, 4, 512) for c in range(NCORES)], 0)[None]
    outs = (y_p, y_s, kvp128, kvp512, kvp2048, kvs[0], kvs[1], kvs[2], sg)
    return tuple(np.ascontiguousarray(o, dtype=np.float32) for o in outs)
```

```python
import numpy as np
from contextlib import ExitStack
import concourse.bass as bass
import concourse.mybir as mybir
from concourse.bass_utils import run_bass_kernel_spmd

F32 = mybir.dt.float32
BF16 = mybir.dt.bfloat16
AF = mybir.ActivationFunctionType
ALU = mybir.AluOpType
AX = mybir.AxisListType

NCORES = 8
D = 1024
S = 2048
NS = 64
NTOK = S + NS
NT = 512
EPS = 1e-6
NSLOT = 4
import os
STAGE = float(os.environ.get("MK_STAGE", "9"))
SLOTF = 4096


class Buf:
    __slots__ = ("name", "w", "r", "dsem", "dcnt", "ex", "nosync")

    def __init__(self, name, ex=False, nosync=False):
        self.name = name
        self.ex = ex
        self.nosync = nosync
        self.w = {}
        self.r = {}
        self.dsem = None
        self.dcnt = 0


def _merge(d, s):
    for k, v in s.items():
        if v > d.get(k, 0):
            d[k] = v


class Prog:
    CE = ("pe", "act", "dve", "pool")
    ENG = {"pe": "tensor", "act": "scalar", "dve": "vector", "pool": "gpsimd", "sp": "sync"}

    def __init__(self, nc, ctx):
        self.nc = nc
        self.ctx = ctx
        self.sems = {}
        self.cnt = {e: 0 for e in self.CE}
        self.waited = {e: {} for e in self.ENG}
        for e in self.CE:
            self.sems[e] = ctx.enter_context(nc.semaphore("s_" + e))
        self.nsem = 0
        self.pe_pending = None

    def newsem(self, name):
        self.nsem += 1
        key = "d%d_%s" % (self.nsem, name)
        self.sems[key] = self.ctx.enter_context(self.nc.semaphore(key))
        return key

    def _waits(self, e, reads, writes):
        d = {}
        for b in reads:
            _merge(d, b.w)
        for b in writes:
            for src in (b.w, b.r):
                for k, v in src.items():
                    if k == e and b.nosync:
                        continue
                    if v > d.get(k, 0):
                        d[k] = v
        if e == "pe":
            d.pop("pe", None)
        elif d.get("pe", 0) > self.cnt["pe"]:
            assert self.pe_pending is not None and d["pe"] == self.cnt["pe"] + 1
            self.pe_pending.then_inc(self.sems["pe"], 1)
            self.cnt["pe"] += 1
            self.pe_pending = None
        out = []
        wd = self.waited[e]
        for k, v in d.items():
            if v > wd.get(k, 0):
                wd[k] = v
                out.append((k, v))
        return out

    def _emit(self, e, waits, fn, inc):
        eng = getattr(self.nc, self.ENG[e])
        for k, v in waits:
            eng.wait_ge(self.sems[k], v)
        if fn is None:
            return None
        ins = fn(eng)
        if inc is not None:
            ins.then_inc(self.sems[inc[0]], inc[1])
        return ins

    def op(self, e, fn, reads=(), writes=(), inc=True):
        exr = [b for b in reads if b.ex]
        if exr:
            writes = list(writes) + exr
        waits = self._waits(e, reads, writes)
        if inc:
            self.cnt[e] += 1
            tok = self.cnt[e]
        else:
            tok = self.cnt[e] + 1
        ins = self._emit(e, waits, fn, (e, 1) if inc else None)
        if e == "pe":
            self.pe_pending = None if inc else ins
        for b in reads:
            if tok > b.r.get(e, 0):
                b.r[e] = tok
        for b in writes:
            if tok > b.w.get(e, 0):
                b.w[e] = tok

    def dma(self, q, out, in_, reads=(), writes=(), sb=None, post_writes=(), **kw):
        if sb is None:
            sb = writes[0] if writes else (post_writes[0] if post_writes else reads[0])
        if sb.dsem is None:
            sb.dsem = self.newsem(sb.name)
        waits = self._waits(q, reads, writes)
        sb.dcnt += 16
        tok = sb.dcnt
        key = sb.dsem
        self._emit(q, waits, lambda eng: eng.dma_start(out=out, in_=in_, **kw), (key, 16))
        for b in reads:
            if tok > b.r.get(key, 0):
                b.r[key] = tok
        for b in list(writes) + list(post_writes):
            if tok > b.w.get(key, 0):
                b.w[key] = tok

    def inherit(self, new_bufs, old_bufs):
        d = {}
        for b in old_bufs:
            _merge(d, b.w)
            _merge(d, b.r)
        for b in new_bufs:
            _merge(b.r, d)

    def finish(self, final_bufs):
        d = {}
        for b in final_bufs:
            _merge(d, b.w)
        self._emit("sp", list(d.items()), None, None)


def build_program():
    nc = bass.Bass("TRN2", target_bir_lowering=False)

    def din(name, shape):
        return nc.dram_tensor(name, shape, F32, kind="ExternalInput")

    def dout(name, shape):
        return nc.dram_tensor(name, shape, F32, kind="ExternalOutput")

    xT_h = din("xT", [D, NTOK]); xT = xT_h.ap()
    xT16 = din("xT16", [D, S]).ap()
    xN = din("xN", [NTOK, D]).ap()
    c128 = din("c128", [16, 128, 512]).ap()
    c512 = din("c512", [16, 512, 512]).ap()
    c2048 = din("c2048", [16, 2048, 512]).ap()
    w_in = din("w_in", [D, 5376]).ap()
    w_ao = din("w_ao", [256, D]).ap()
    w_bo = din("w_bo", [512, D]).ap()
    w_out = din("w_out", [D, D]).ap()
    w_up = din("w_up", [D, 4096]).ap()
    w_down = din("w_down", [4096, D]).ap()
    gpre_d = din("gpre", [128, 8]).ap()
    bgate_d = din("bgate", [128, 16]).ap()
    gpm_d = din("gpm", [128, D]).ap()
    gffn_d = din("gffn", [128, 8]).ap()
    gpf_d = din("gpf", [128, D]).ap()
    lng_d = din("lng", [128, 512]).ap()
    lnb_d = din("lnb", [128, 512]).ap()
    wspT_d = din("wspT", [128, 512]).ap()
    bsp_d = din("bsp", [128, 512]).ap()
    wsS_d = din("wsS", [64, 256]).ap()
    bspS_d = din("bspS", [128, 256]).ap()

    y_d = dout("y", [NTOK, D]).ap()
    kvp128_d = dout("kvp128", [128, 512]).ap()
    kvp512_d = dout("kvp512", [512, 512]).ap()
    kvp2048_d = dout("kvp2048", [2048, 512]).ap()
    kvs_h = dout("kvs", [3, 64, 512]); kvs_d = kvs_h.ap()
    sg_d = dout("sg", [64, 512]).ap()

    qs_h = nc.dram_tensor("qs_scr", [64, 768], F32, kind="Internal")
    od_h = nc.dram_tensor("od_scr", [64, 3, 4, 257], F32, kind="Internal")

    with ExitStack() as ctx:
        P = Prog(nc, ctx)

        def sbt(name, shape, dt, c=ctx):
            return c.enter_context(nc.sbuf_tensor("sb_" + name, shape, dt))

        w_in_v = w_in.rearrange("(kc p) n -> p kc n", p=128)
        chunks = {}

        def add_chunk(name, F, parts):
            t = nc.dram_tensor("sc_" + name, [128, F], BF16, kind="Internal").ap()
            chunks[name] = {"ap": t, "F": F, "buf": Buf("sc_" + name), "parts": parts}

        for g in range(3):
            add_chunk("KV%d" % g, 4096, [(w_in_v[:, :, 768 + 256 * g: 1024 + 256 * g], 8, 512, 0, 256),
                                         (w_in_v[:, :, 1536 + 256 * g: 1792 + 256 * g], 8, 512, 256, 256)])
        add_chunk("Q01", 4096, [(w_in_v[:, :, 0:512], 8, 512, 0, 512)])
        add_chunk("Q2", 2048, [(w_in_v[:, :, 512:768], 8, 256, 0, 256)])
        add_chunk("Z1", 4096, [(w_in_v[:, :, 2304:2816], 8, 512, 0, 512)])
        add_chunk("Z2", 4096, [(w_in_v[:, :, 2816:3328], 8, 512, 0, 512)])
        for j in range(4):
            add_chunk("GP%d" % j, 4096, [(w_in_v[:, :, 3328 + 256 * j: 3584 + 256 * j], 8, 512, 0, 256),
                                         (w_in_v[:, :, 4352 + 256 * j: 4608 + 256 * j], 8, 512, 256, 256)])
        add_chunk("AO", 2048, [(w_ao.rearrange("(hp p) n -> p hp n", p=128), 2, 1024, 0, 1024)])
        add_chunk("BO", 4096, [(w_bo.rearrange("(kc p) n -> p kc n", p=128), 4, 1024, 0, 1024)])
        w_out_v = w_out.rearrange("(kc p) n -> p kc n", p=128)
        for j in range(2):
            add_chunk("OUT%d" % j, 4096, [(w_out_v[:, :, 512 * j: 512 * j + 512], 8, 512, 0, 512)])
        w_up_v = w_up.rearrange("(kc p) n -> p kc n", p=128)
        for j in range(8):
            add_chunk("UP%d" % j, 4096, [(w_up_v[:, :, 512 * j: 512 * j + 512], 8, 512, 0, 512)])
        w_dn_v = w_down.rearrange("(fc p) n -> p fc n", p=128)
        for hf in range(2):
            for j in range(4):
                add_chunk("DN%d_%d" % (hf, j), 4096,
                          [(w_dn_v[:, 8 * j: 8 * j + 8, 512 * hf: 512 * hf + 512], 8, 512, 0, 512)])

        cast_order = (["KV0", "KV1", "Q01", "KV2", "Q2", "Z2", "Z1", "AO", "BO"] + ["GP%d" % j for j in range(4)]
                      + ["OUT0", "OUT1"] + ["UP%d" % j for j in range(8)]
                      + ["DN%d_%d" % (hf, j) for hf in range(2) for j in range(4)])

        c_head = ["AO", "BO", "GP0", "GP1", "GP2", "GP3", "OUT0", "OUT1"]
        c_tail = ["UP%d" % j for j in range(8)] + ["DN%d_%d" % (hf, j) for hf in range(2) for j in range(4)]
        worder = ["KV0", "KV1", "Q01"] * 4 + ["KV0", "KV1", "KV2", "Q01", "Q2"] + ["KV2", "Q2"] * 4 + ["Z2", "Z1"]
        for i in range(5):
            worder += c_head + (["Z2", "Z1"] if i < 4 else []) + c_tail
        slots = [sbt("wslot%d" % i, [128, SLOTF], BF16) for i in range(NSLOT)]
        slot_b = [Buf("wslot%d" % i) for i in range(NSLOT)]
        slotc_b = [Buf("wslotc%d" % i) for i in range(NSLOT)]

        class WS:
            i_load = 0
            i_use = 0
            free = list(range(NSLOT))
            loaded = {}

        def w_pump():
            while WS.free and WS.i_load < len(worder):
                s = WS.free.pop(0)
                ch = chunks[worder[WS.i_load]]
                if not ch.get("cast_done"):
                    nm = worder[WS.i_load]
                    uses = ch.get("uses", 0)
                    ch["uses"] = uses + 1
                    writeback = not ((nm.startswith("UP") or nm.startswith("DN")) and uses == 0)
                    if writeback:
                        ch["cast_done"] = True
                    first = True
                    for (src, kc, n, off, w) in ch["parts"]:
                        dst = slots[s][:, 0:ch["F"]].rearrange("p (kc n) -> p kc n", kc=kc)[:, :, off:off + w]
                        if first:
                            P.dma("pool", dst, src, writes=[slot_b[s]], sb=slotc_b[s])
                        else:
                            P.dma("pool", dst, src, post_writes=[slot_b[s]], sb=slotc_b[s])
                        first = False
                    if writeback:
                        P.dma("sp", ch["ap"], slots[s][:, 0:ch["F"]], reads=[slot_b[s]], writes=[ch["buf"]], sb=ch["buf"])
                else:
                    P.dma("sp", slots[s][:, 0:ch["F"]], ch["ap"], reads=[ch["buf"]], writes=[slot_b[s]])
                WS.loaded[WS.i_load] = s
                WS.i_load += 1

        def w_acquire(name):
            w_pump()
            assert worder[WS.i_use] == name, (worder[WS.i_use], name)
            s = WS.loaded.pop(WS.i_use)
            WS.i_use += 1
            return s

        def w_release(s):
            WS.free.append(s)
            w_pump()

        _saved_free = WS.free[1:]
        WS.free = WS.free[:1]
        w_pump()
        WS.free = _saved_free


        NPS = 7
        ps_t = [ctx.enter_context(nc.psum_tensor("ps%d" % i, [128, 512], F32)) for i in range(NPS)]
        ps_b = [Buf("ps%d" % i, ex=True) for i in range(NPS)]
        psT_t = ctx.enter_context(nc.psum_tensor("psT", [128, 1024], BF16))
        psT_b = Buf("psT", ex=True)
        ps_rr = [0]

        def next_ps():
            i = ps_rr[0]
            ps_rr[0] = (i + 1) % NPS
            return ps_t[i], ps_b[i]

        def mm(out, lhsT, rhs, start, stop, reads, writes):
            P.op("pe", lambda e: e.matmul(out, lhsT=lhsT, rhs=rhs, start=start, stop=stop),
                 reads=reads, writes=writes, inc=stop)

        ident = sbt("ident", [128, 128], BF16); ident_b = Buf("ident")
        identf = sbt("identf", [128, 128], F32)
        maskc = sbt("maskc", [128, 128], BF16)
        maskp = sbt("maskp", [128, 128], BF16)
        ones = sbt("ones", [128, 128], BF16)
        epsT = sbt("epsT", [128, 1], F32)
        tmpc = sbt("tmpc", [128, 512], F32)
        gpre = sbt("gpre", [128, 8], F32)
        bgate = sbt("bgate", [128, 16], F32)
        gffn = sbt("gffn", [128, 8], F32)
        wspT = sbt("wspT", [128, 512], BF16)
        bsp_hi = sbt("bsp_hi", [128, 512], BF16)
        bsp_lo = sbt("bsp_lo", [128, 512], BF16)
        wsS = sbt("wsS", [64, 256], BF16)
        bspS = sbt("bspS", [128, 256], F32)
        cst = Buf("consts")

        P.op("pool", lambda e: e.memset(tmpc[:, 0:128], 0.0), writes=[cst])
        P.op("pool", lambda e: e.affine_select(out=tmpc[:, 0:128], in_=tmpc[:, 0:128], compare_op=ALU.not_equal, fill=1.0,
                                                base=0, pattern=[[-1, 128]], channel_multiplier=1), reads=[cst], writes=[cst])
        P.op("pool", lambda e: e.tensor_copy(out=ident[:], in_=tmpc[:, 0:128]), reads=[cst], writes=[cst])
        P.op("pool", lambda e: e.tensor_copy(out=identf[:], in_=tmpc[:, 0:128]), reads=[cst], writes=[cst])
        P.op("pool", lambda e: e.memset(tmpc[:, 0:128], 0.0), reads=[cst], writes=[cst])
        P.op("pool", lambda e: e.affine_select(out=tmpc[:, 0:128], in_=tmpc[:, 0:128], compare_op=ALU.is_ge, fill=-30000.0,
                                                base=0, pattern=[[1, 128]], channel_multiplier=-1), reads=[cst], writes=[cst])
        P.op("pool", lambda e: e.tensor_copy(out=maskc[:], in_=tmpc[:, 0:128]), reads=[cst], writes=[cst])
        P.op("pool", lambda e: e.memset(tmpc[:, 0:128], 0.0), reads=[cst], writes=[cst])
        P.op("pool", lambda e: e.affine_select(out=tmpc[:, 0:128], in_=tmpc[:, 0:128], compare_op=ALU.is_ge, fill=-30000.0,
                                                base=0, pattern=[[-1, 128]], channel_multiplier=1), reads=[cst], writes=[cst])
        P.op("pool", lambda e: e.tensor_copy(out=maskp[:], in_=tmpc[:, 0:128]), reads=[cst], writes=[cst])
        P.op("pool", lambda e: e.memset(ones[:], 1.0), writes=[cst])
        P.op("pool", lambda e: e.memset(epsT[:], EPS), writes=[cst])
        for (t, d_) in ((gpre, gpre_d), (bgate, bgate_d), (gffn, gffn_d), (bspS, bspS_d)):
            P.dma("sp", t[:], d_, writes=[cst])
        cstL = Buf("consts_late")
        P.inherit([cstL], [cst])
        P.dma("sp", tmpc[:], wspT_d, reads=[cstL], writes=[cstL])
        for g4 in range(4):
            P.op("pool", lambda e, g4=g4: e.affine_select(out=tmpc[:, g4 * 128:(g4 + 1) * 128], in_=tmpc[:, g4 * 128:(g4 + 1) * 128],
                                                         compare_op=ALU.is_ge, fill=0.0, base=0, pattern=[[1, 128]],
                                                         channel_multiplier=-1), reads=[cstL], writes=[cstL])
        P.op("pool", lambda e: e.tensor_copy(out=wspT[:], in_=tmpc[:]), reads=[cstL], writes=[cstL])
        P.dma("sp", tmpc[:], bsp_d, reads=[cstL], writes=[cstL])
        P.op("pool", lambda e: e.tensor_copy(out=bsp_hi[:], in_=tmpc[:]), reads=[cstL], writes=[cstL])
        P.op("pool", lambda e: e.tensor_tensor(out=tmpc[:], in0=tmpc[:], in1=bsp_hi[:], op=ALU.subtract), reads=[cstL], writes=[cstL])
        P.op("pool", lambda e: e.tensor_copy(out=bsp_lo[:], in_=tmpc[:]), reads=[cstL], writes=[cstL])
        P.dma("sp", tmpc[0:64, 0:256], wsS_d, reads=[cstL], writes=[cstL])
        for g4 in range(4):
            P.op("pool", lambda e, g4=g4: e.affine_select(out=tmpc[0:64, g4 * 64:(g4 + 1) * 64], in_=tmpc[0:64, g4 * 64:(g4 + 1) * 64],
                                                         compare_op=ALU.is_ge, fill=0.0, base=0, pattern=[[1, 64]],
                                                         channel_multiplier=-1), reads=[cstL], writes=[cstL])
        P.op("pool", lambda e: e.tensor_copy(out=wsS[:], in_=tmpc[0:64, 0:256]), reads=[cstL], writes=[cstL])

        def emit_cast(name, dep=()):
            ch = chunks[name]
            for (src, kc, n, off, w) in ch["parts"]:
                dst = ch["ap"].rearrange("p (kc n) -> p kc n", kc=kc)[:, :, off:off + w]
                P.dma("pool", dst, src, reads=list(dep), post_writes=[ch["buf"]], sb=ch["buf"])

        deferred = []

        def emit_deferred(n, dep):
            for _ in range(n):
                if deferred:
                    emit_cast(deferred.pop(0), dep)

        xTs = sbt("xTs", [128, 8, NT], F32); xTs_b = Buf("xTs")
        sq = sbt("sq", [128, 4, NT], BF16); sq_b = Buf("sq")
        hTt = sbt("hTt", [128, 8, NT], BF16); hTt_b = Buf("hTt")
        sdt = sbt("sdt", [128, NT], F32); sdt_b = Buf("sdt")
        rstdb = sbt("rstdb", [128, NTOK], F32); rstdb_b = Buf("rstdb")
        OaT = sbt("OaT", [128, 2, NTOK], BF16); OaT_b = Buf("OaT")
        junk = sbt("junk", [128, 1024], BF16); junk_b = Buf("junk", nosync=True)

        def hT_load(src, col0, nt, mode=None, tok0=None, q="act"):
            srcv = src.rearrange("(kc p) n -> p kc n", p=128)
            P.dma(q, xTs[:, :, 0:nt], srcv[:, :, col0:col0 + nt], writes=[xTs_b])

        def hT_part1(src, col0, nt, mode, tok0=None, load=True):
            if load:
                hT_load(src, col0, nt)
            if mode == "reuse":
                return rstdb[:, tok0:tok0 + nt], rstdb_b
            if mode == "tmp":
                i16 = col0 // NT
                return rstdb[:, 0:S].rearrange("p (m r) -> p r m", r=16)[:, 4 * i16:4 * i16 + 4, :], rstdb_b
            pst, psb = next_ps()
            for hf in range(2):
                P.op("act", lambda e, hf=hf: e.activation(out=sq[:, :, 0:nt], in_=xTs[:, 4 * hf:4 * hf + 4, 0:nt], func=AF.Square),
                     reads=[xTs_b], writes=[sq_b])
                for j in range(4):
                    mm(pst[:, 0:nt], ones[:], sq[:, j, 0:nt], hf == 0 and j == 0, hf == 1 and j == 3, [sq_b, cst], [psb])
            P.op("act", lambda e: e.activation(out=sdt[:, 0:nt], in_=pst[:, 0:nt], func=AF.Sqrt, bias=epsT[:, 0:1], scale=1.0 / D),
                 reads=[psb, cst], writes=[sdt_b])
            if mode == "store":
                rs = rstdb[:, tok0:tok0 + nt]
                rs_b = rstdb_b
            else:
                rs = sdt[:, 0:nt]
                rs_b = sdt_b
            P.op("dve", lambda e: e.reciprocal(out=rs, in_=sdt[:, 0:nt]), reads=[sdt_b], writes=[rs_b])
            return rs, rs_b

        def hT_part2(nt, rs, rs_b, dst, dst_b):
            v3 = (len(rs.shape) == 3)
            for kc in range(8):
                o_ = dst[:, kc, 0:nt]
                i_ = xTs[:, kc, 0:nt]
                if v3:
                    o_ = o_.rearrange("p (r m) -> p r m", r=4)
                    i_ = i_.rearrange("p (r m) -> p r m", r=4)
                P.op("dve", lambda e, kc=kc, o_=o_, i_=i_: e.scalar_tensor_tensor(out=o_, in0=i_, scalar=gpre[:, kc:kc + 1],
                                                                                  in1=rs, op0=ALU.mult, op1=ALU.mult),
                     reads=[xTs_b, rs_b, cst], writes=[dst_b])

        def compute_hT(src, col0, nt, mode, tok0=None):
            rs, rs_b = hT_part1(src, col0, nt, mode, tok0)
            hT_part2(nt, rs, rs_b, hTt, hTt_b)

        cAB = ExitStack()
        qT = sbt("qT", [128, 6, S], BF16, cAB); qT_b = Buf("qT")
        kT = sbt("kT", [128, 6, S], BF16, cAB); kT_b = Buf("kT")
        VW = 48 * 260
        Vaug = sbt("Vaug", [128, VW], BF16, cAB); Vaug_b = Buf("Vaug")
        onesf = sbt("onesf", [128, 64], F32, cAB)
        Oacc = sbt("Oacc", [128, S], F32, cAB); Oacc_b = Buf("Oacc")
        rden = sbt("rden", [128, 512], F32, cAB); rden_b = Buf("rden")
        NPT = 3
        pT = [sbt("pT%d" % i, [128, 256], BF16, cAB) for i in range(NPT)]
        pT_b = [Buf("pT%d" % i) for i in range(NPT)]
        kvst = [sbt("kvst%d" % i, [128, 512], F32, cAB) for i in range(2)]
        kvst_b = [Buf("kvst%d" % i) for i in range(2)]
        kvs_s = sbt("kvs_s", [64, 3, 512], F32, cAB); kvs_sb = Buf("kvs_s")
        qs_s = sbt("qs_s", [64, 768], F32, cAB); qs_sb = Buf("qs_s")
        NKV = 7
        xflat = xTs[:].rearrange("p k t -> p (k t)")
        KVt = [xflat[:, i * 520:i * 520 + 513] for i in range(NKV)]
        KVt_b = [Buf("KVt%d" % i) for i in range(NKV)]
        qb = [sbt("qb%d" % i, [128, 768], F32, cAB) for i in range(2)]
        qb_b = [Buf("qb%d" % i) for i in range(2)]
        AB_bufs = [qT_b, kT_b, Vaug_b, Oacc_b, rden_b] + pT_b + kvst_b + [kvs_sb, qs_sb] + KVt_b + qb_b

        kvp128_b = Buf("kvp128"); kvp512_b = Buf("kvp512"); kvp2048_b = Buf("kvp2048")
        kvs_ob = Buf("kvs_o"); sg_ob = Buf("sg_o"); y_ob = Buf("y_o")
        qs_db = Buf("qs_d"); od_db = Buf("od_d")

        P.op("dve", lambda e: e.memset(Vaug[:, :], 1.0), writes=[Vaug_b])
        P.op("pool", lambda e: e.memset(onesf[:, :], 1.0), writes=[Vaug_b])

        def voff(g, blk):
            return (g * 16 + blk) * 260

        kvst_rr = [0]
        evac_rr = [0]

        def evac_copy(out, in_, reads, writes):
            evac_rr[0] ^= 1
            if evac_rr[0]:
                P.op("act", lambda e: e.activation(out=out, in_=in_, func=AF.Copy), reads=reads, writes=writes)
            else:
                P.op("dve", lambda e: e.tensor_copy(out=out, in_=in_), reads=reads, writes=writes)

        hT2 = Oacc[:].bitcast(BF16).rearrange("p (k t) -> p k t", k=8)
        hT2_b = Buf("hT2")
        hbufs = [(hTt, hTt_b), (hT2, hT2_b)]
        HC = {"t": hTt, "b": hTt_b}

        def featmajor_proj(s, colbase, ncols_per_kc, dst_fn):
            pst, psb = next_ps()
            for kc in range(8):
                mm(pst[:, 0:NT], slots[s][:, kc * ncols_per_kc + colbase: kc * ncols_per_kc + colbase + 128], HC["t"][:, kc, 0:NT],
                   kc == 0, kc == 7, [slot_b[s], HC["b"]], [psb])
            dst_fn(pst, psb)

        def kv_tok(s, lhs_fn, g, blk, out_dram=None, out_buf=None):
            pst, psb = next_ps()
            c0 = 0 if out_dram is not None else 256
            for kc in range(8):
                mm(pst[:, c0:512], lhs_fn(kc), slots[s][:, kc * 512 + c0:(kc + 1) * 512], kc == 0, kc == 7, [slot_b[s], HC["b"]], [psb])
            o = voff(g, blk)
            evac_copy(Vaug[:, o:o + 260].rearrange("p (h e) -> p h e", e=65)[:, :, 0:64],
                      pst[:, 256:512].rearrange("p (h d) -> p h d", d=64), [psb], [Vaug_b])
            if out_dram is not None and os.environ.get("MK_NOOUT") == "2" and g != 0:
                out_dram = None
            if out_dram is not None and os.environ.get("MK_NOOUT") == "3" and g == 0:
                out_dram = None
            if out_dram is not None and os.environ.get("MK_NOOUT") != "1":
                i = kvst_rr[0]
                kvst_rr[0] ^= 1
                evac_copy(kvst[i][:], pst[:, 0:512], [psb], [kvst_b[i]])
                P.dma(os.environ.get("MK_STQ", "pool"), out_dram, kvst[i][:], reads=[kvst_b[i]], post_writes=[out_buf], sb=kvst_b[i])

        if STAGE < 1:
            P.finish([y_ob, cst] + slot_b + [chunks[n]["buf"] for n in cast_order])
            cAB.close()
            return nc
        a_steps = [(xT, NT * i, NT, "store", NT * i) for i in range(4)] + [(xT, S, NS, "store", S)] + \
            [(xT16, NT * i, NT, "tmp", None) for i in range(4)]
        a_state = {"k": 0, "rs": None}

        def a_begin():
            k = a_state["k"]
            HC["t"], HC["b"] = hbufs[k % 2]
            if k == 0 and len(a_steps) > 1:
                hT_load(*a_steps[1])

        def a_mid():
            k = a_state["k"]
            if k + 1 < len(a_steps):
                rs, rs_b = hT_part1(*a_steps[k + 1], load=False)
                dst, dst_b = hbufs[(k + 1) % 2]
                hT_part2(a_steps[k + 1][2], rs, rs_b, dst, dst_b)
                if k + 2 < len(a_steps):
                    hT_load(*a_steps[k + 2], q="sp")
            a_state["k"] = k + 1

        rs0, rs0_b = hT_part1(*a_steps[0])
        hT_part2(NT, rs0, rs0_b, hTt, hTt_b)
        emit_deferred(1, [hTt_b])
        for i in range(4):
            tok0 = NT * i
            a_begin()
            if STAGE < 1.1:
                P.finish([y_ob, kvp128_b, kvp512_b, kvp2048_b, hTt_b, kT_b, qT_b, Vaug_b] + slot_b + ps_b)
                cAB.close()
                return nc
            s = w_acquire("KV0")
            for p in range(2):
                featmajor_proj(s, p * 128, 512, lambda pst, psb, p=p: evac_copy(kT[:, p, tok0:tok0 + NT], pst[:, 0:NT], [psb], [kT_b]))
            if STAGE < 1.2:
                P.finish([y_ob, kvp128_b, kvp512_b, kvp2048_b, hTt_b, kT_b, qT_b, Vaug_b] + slot_b + ps_b)
                cAB.close()
                return nc
            for b in range(4):
                last = (i == 3 and b == 3) or (os.environ.get('MK_FORCEOUT') and b == 3)
                kv_tok(s, lambda kc, b=b: HC["t"][:, kc, b * 128:(b + 1) * 128], 0, 4 * i + b,
                       kvp128_d if last else None, kvp128_b)
            w_release(s)
            a_mid()
            if STAGE < 1.3:
                P.finish([y_ob, kvp128_b, kvp512_b, kvp2048_b, hTt_b, kT_b, qT_b, Vaug_b] + slot_b + ps_b)
                cAB.close()
                return nc
            s = w_acquire("KV1")
            for p in range(2):
                featmajor_proj(s, p * 128, 512, lambda pst, psb, p=p: evac_copy(
                    kT[:, 2 + p, :].rearrange("q (r m) -> q r m", r=4)[:, :, 128 * i:128 * i + 128],
                    pst[:, 0:NT].rearrange("q (m r) -> q r m", r=4), [psb], [kT_b]))
            if STAGE < 1.4:
                P.finish([y_ob, kvp128_b, kvp512_b, kvp2048_b, hTt_b, kT_b, qT_b, Vaug_b] + slot_b + ps_b)
                cAB.close()
                return nc
            for r in range(4):
                kv_tok(s, lambda kc, r=r: HC["t"][:, kc, r:NT:4], 1, r * 4 + i,
                       kvp512_d[r:512:4, :] if i == 3 else None, kvp512_b)
            w_release(s)
            s = w_acquire("Q01")
            for p in range(2):
                featmajor_proj(s, p * 128, 512, lambda pst, psb, p=p: evac_copy(qT[:, p, tok0:tok0 + NT], pst[:, 0:NT], [psb], [qT_b]))
            for p in range(2):
                featmajor_proj(s, 256 + p * 128, 512, lambda pst, psb, p=p: evac_copy(
                    qT[:, 2 + p, :].rearrange("q (r m) -> q r m", r=4)[:, :, 128 * i:128 * i + 128],
                    pst[:, 0:NT].rearrange("q (m r) -> q r m", r=4), [psb], [qT_b]))
            w_release(s)

        a_begin()
        for g in range(3):
            s = w_acquire("KV%d" % g)
            pst, psb = next_ps()
            for kc in range(8):
                mm(pst[0:NS, 0:512], HC["t"][:, kc, 0:NS], slots[s][:, kc * 512:(kc + 1) * 512], kc == 0, kc == 7, [slot_b[s], HC["b"]], [psb])
            evac_copy(kvs_s[:, g, :], pst[0:NS, 0:512], [psb], [kvs_sb])
            w_release(s)
            if g == 0:
                a_mid()
        P.dma("pool", kvs_d.rearrange("g t n -> t g n"), kvs_s[:], reads=[kvs_sb], writes=[kvs_ob], sb=kvs_sb)
        s = w_acquire("Q01")
        pst, psb = next_ps()
        for kc in range(8):
            mm(pst[0:NS, 0:512], HC["t"][:, kc, 0:NS], slots[s][:, kc * 512:(kc + 1) * 512], kc == 0, kc == 7, [slot_b[s], HC["b"]], [psb])
        evac_copy(qs_s[:, 0:512], pst[0:NS, 0:512], [psb], [qs_sb])
        w_release(s)
        s = w_acquire("Q2")
        pst, psb = next_ps()
        for kc in range(8):
            mm(pst[0:NS, 0:256], HC["t"][:, kc, 0:NS], slots[s][:, kc * 256:(kc + 1) * 256], kc == 0, kc == 7, [slot_b[s], HC["b"]], [psb])
        evac_copy(qs_s[:, 512:768], pst[0:NS, 0:256], [psb], [qs_sb])
        w_release(s)
        P.dma("pool", qs_h.ap(), qs_s[:], reads=[qs_sb], writes=[qs_db], sb=qs_sb)

        for i in range(4):
            a_begin()
            s = w_acquire("KV2")
            for p in range(2):
                featmajor_proj(s, p * 128, 512, lambda pst, psb, p=p: evac_copy(kT[:, 4 + p, NT * i:NT * i + NT], pst[:, 0:NT], [psb], [kT_b]))
            for b in range(4):
                r = 4 * i + b
                kv_tok(s, lambda kc, b=b: HC["t"][:, kc, b * 128:(b + 1) * 128], 2, r, kvp2048_d[r:2048:16, :], kvp2048_b)
            w_release(s)
            a_mid()
            s = w_acquire("Q2")
            for p in range(2):
                featmajor_proj(s, p * 128, 256, lambda pst, psb, p=p: evac_copy(qT[:, 4 + p, NT * i:NT * i + NT], pst[:, 0:NT], [psb], [qT_b]))
            w_release(s)

        P.inherit(KVt_b, [xTs_b])
        P.inherit([Oacc_b], [hT2_b])
        for i in range(NKV):
            P.op("pool", lambda e, i=i: e.memset(KVt[i][:, 512:513], 1.0), writes=[KVt_b[i]])
        def vaug_ap(g, blk, h):
            o = voff(g, blk) + h * 65
            return Vaug[:, o:o + 65]

        pt_rr = [0]

        def attn_p1(u):
            h, g, qc, kc_cur, kc_prev, blk, blk_prev, tok_ap, first = u
            p = h // 2
            R0 = 64 * (h % 2)
            gi = 2 * g + p
            pst, psb = next_ps()
            W = 256 if kc_prev is not None else 128
            mm(pst[:, 0:128], kT[R0:R0 + 64, gi, kc_cur:kc_cur + 128], qT[R0:R0 + 64, gi, qc:qc + 128], True, False, [kT_b, qT_b], [psb])
            mm(pst[:, 0:128], ident[:], maskc[:], False, True, [cst], [psb])
            if kc_prev is not None:
                mm(pst[:, 128:256], kT[R0:R0 + 64, gi, kc_prev:kc_prev + 128], qT[R0:R0 + 64, gi, qc:qc + 128], True, False, [kT_b, qT_b], [psb])
                mm(pst[:, 128:256], ident[:], maskp[:], False, True, [cst], [psb])
            i = pt_rr[0]
            pt_rr[0] = (i + 1) % NPT
            P.op("act", lambda e: e.activation(out=pT[i][:, 0:W], in_=pst[:, 0:W], func=AF.Exp, scale=0.125), reads=[psb], writes=[pT_b[i]])
            return i

        def attn_p2(u, i):
            h, g, qc, kc_cur, kc_prev, blk, blk_prev, tok_ap, first = u
            pso, psob = next_ps()
            mm(pso[0:65, 0:128], vaug_ap(g, blk, h), pT[i][:, 0:128], True, kc_prev is None, [Vaug_b, pT_b[i]], [psob])
            if kc_prev is not None:
                mm(pso[0:65, 0:128], vaug_ap(g, blk_prev, h), pT[i][:, 128:256], False, True, [Vaug_b, pT_b[i]], [psob])
            if first:
                P.op("dve", lambda e: e.tensor_copy(out=tok_ap, in_=pso[0:65, 0:128]), reads=[psob], writes=[Oacc_b])
            else:
                P.op("dve", lambda e: e.tensor_tensor(out=tok_ap, in0=tok_ap, in1=pso[0:65, 0:128], op=ALU.add), reads=[psob, Oacc_b], writes=[Oacc_b])

        caches = (c128, c512, c2048)
        combos = [(b, t, g) for b in range(16) for t in range(4) for g in range(3)]
        if STAGE == 4:
            combos = []
        NCB = len(combos)
        NPR, NVB, NS4, NOS = 3, 4, 3, 2
        prod = [sbt("prod%d" % i, [128, 256], F32, cAB) for i in range(NPR)]
        prod_b = [Buf("prod%d" % i) for i in range(NPR)]
        Vb = [sbt("Vb%d" % i, [128, 257], BF16, cAB) for i in range(NVB)]
        Vb_b = [Buf("Vb%d" % i) for i in range(NVB)]
        s4 = [sbt("s4_%d" % i, [128, 4], F32, cAB) for i in range(NS4)]
        s4_b = [Buf("s4_%d" % i) for i in range(NS4)]
        p4 = [sbt("p4_%d" % i, [128, 4], BF16, cAB) for i in range(NS4)]
        p4_b = [Buf("p4_%d" % i) for i in range(NS4)]
        ost = [sbt("ost%d" % i, [4, 257], F32, cAB) for i in range(NOS)]
        ost_b = [Buf("ost%d" % i) for i in range(NOS)]
        AB_bufs += prod_b + Vb_b + s4_b + p4_b + ost_b
        ost_ap = [t[:] for t in ost] + [kvst[0][0:4, 0:257], kvst[1][0:4, 0:257], sdt[0:4, 0:257]]
        ost_b = ost_b + [kvst_b[0], kvst_b[1], sdt_b]
        NOS = len(ost_ap)
        ost_semb = [Buf("ostsem%d" % i) for i in range(NOS)]

        hflat = hTt[:].rearrange("p k t -> p (k t)").bitcast(F32)
        qb_ap = [qb[0][:], qb[1][:], hflat[:, 0:768], hflat[:, 1024:1792]]
        qbx_b = [Buf("qbx0"), Buf("qbx1")]
        P.inherit(qbx_b, [hTt_b])
        qb_bb = [qb_b[0], qb_b[1]] + qbx_b
        NQB = 4
        LEAD = 6

        def cb_A(k):
            b, t, g = combos[k]
            tok = 4 * b + t
            qi = (k // 3) % NQB
            if g == 0:
                P.dma("sp", qb_ap[qi], bass.AP(qs_h, tok * 768, [[0, 128], [1, 768]]), reads=[qs_db], writes=[qb_bb[qi]])
            ki = k % NKV
            if g == 0:
                if t == 0:
                    P.dma("sp", KVt[ki][:, 0:512], c128[b, :, :], writes=[KVt_b[ki]])
                else:
                    P.dma("sp", KVt[ki][16:128, 0:512], c128[b, 16:128, :], writes=[KVt_b[ki]])
                    P.dma("sp", KVt[ki][t:16, 0:512], c128[b, t:16, :], post_writes=[KVt_b[ki]])
                    P.dma("sp", KVt[ki][0:t, 0:512], kvs_d[0, 4 * b:4 * b + t, :], reads=[kvs_ob], post_writes=[KVt_b[ki]])
            elif g == 1:
                P.dma("sp", KVt[ki][:, 0:512], c512[b, t:512:4, :], writes=[KVt_b[ki]])
            else:
                P.dma("sp", KVt[ki][:, 0:512], c2048[b, t:2048:16, :], writes=[KVt_b[ki]])

        def cb_B(k):
            b, t, g = combos[k]
            qi = (k // 3) % NQB
            ki = k % NKV
            j = k % NPR
            v = k % NVB
            P.op("dve", lambda e: e.tensor_tensor(out=prod[j][:], in0=KVt[ki][:, 0:256], in1=qb_ap[qi][:, g * 256:(g + 1) * 256], op=ALU.mult),
                 reads=[KVt_b[ki], qb_bb[qi]], writes=[prod_b[j]])
            P.op("pool", lambda e: e.tensor_copy(out=Vb[v][:], in_=KVt[ki][:, 256:513]), reads=[KVt_b[ki]], writes=[Vb_b[v]])

        def cb_C(k):
            j = k % NPR
            q = k % NS4
            P.op("dve", lambda e: e.tensor_reduce(out=s4[q][:], in_=prod[j][:].rearrange("p (h d) -> p h d", h=4), axis=AX.X, op=ALU.add),
                 reads=[prod_b[j]], writes=[s4_b[q]])

        def cb_D(k):
            q = k % NS4
            P.op("act", lambda e: e.activation(out=p4[q][:], in_=s4[q][:], func=AF.Exp, scale=0.125), reads=[s4_b[q]], writes=[p4_b[q]])

        def cb_E(k):
            b, t, g = combos[k]
            tok = 4 * b + t
            q = k % NS4
            v = k % NVB
            o = k % NOS
            pst, psb = next_ps()
            mm(pst[0:4, 0:257], p4[q][:, 0:4], Vb[v][:, 0:257], True, True, [p4_b[q], Vb_b[v]], [psb])
            P.op("act", lambda e: e.activation(out=ost_ap[o], in_=pst[0:4, 0:257], func=AF.Copy), reads=[psb], writes=[ost_b[o]])
            P.dma("act", od_h.ap()[tok, g, :, :], ost_ap[o], reads=[ost_b[o]], post_writes=[od_db], sb=ost_semb[o])

        combo_i = [0]

        def emit_combo():
            k = combo_i[0]
            if k >= NCB + LEAD + 3:
                return
            combo_i[0] += 1
            for lag, fn in ((0, cb_A), (LEAD, cb_B), (LEAD + 1, cb_C), (LEAD + 2, cb_D), (LEAD + 3, cb_E)):
                kk = k - lag
                if 0 <= kk < NCB:
                    fn(kk)
            if k % 8 == 7 and k >= LEAD:
                emit_deferred(1, [Vb_b[(k - LEAD) % NVB]])

        SK = 2
        for _ in range(LEAD + 3):
            emit_combo()
        for h in range(4):
            p = h // 2
            units = []
            for r in range(16):
                units.append((h, 2, r * 128, r * 128, None, r, None, Oacc[0:65, r:S:16], True))
            for r in range(4):
                for j in range(4):
                    units.append((h, 1, r * 512 + 128 * j, r * 512 + 128 * j, (r * 512 + 128 * (j - 1)) if j > 0 else None,
                                  r * 4 + j, r * 4 + j - 1, Oacc[0:65, 512 * j + r:512 * j + 512:4], False))
            for j in range(16):
                units.append((h, 0, 128 * j, 128 * j, 128 * (j - 1) if j > 0 else None, j, j - 1, Oacc[0:65, 128 * j:128 * j + 128], False))
            pend = []
            for u in units:
                pend.append((u, attn_p1(u)))
                if len(pend) > SK:
                    attn_p2(*pend.pop(0))
                    emit_combo()
            while pend:
                attn_p2(*pend.pop(0))
                emit_combo()
            Ro = slice(0, 64) if h % 2 == 0 else slice(64, 128)
            for c4 in range(4):
                cs = slice(512 * c4, 512 * c4 + 512)
                pst, psb = next_ps()
                mm(pst[0:64, 0:512], onesf[64:65, 0:64], Oacc[64:65, cs], True, True, [Oacc_b, Vaug_b], [psb])
                P.op("dve", lambda e, pst=pst: e.reciprocal(out=rden[0:64, :], in_=pst[0:64, 0:512]), reads=[psb], writes=[rden_b])
                P.op("dve", lambda e, cs=cs, Ro=Ro: e.tensor_tensor(out=OaT[Ro, p, cs], in0=Oacc[0:64, cs], in1=rden[0:64, :], op=ALU.mult),
                     reads=[Oacc_b, rden_b], writes=[OaT_b])
        while combo_i[0] < NCB + LEAD + 3:
            emit_combo()
        emit_deferred(100, [])

        if STAGE < 6:
            P.finish([y_ob, kvp128_b, kvp512_b, kvp2048_b, kvs_ob, sg_ob, qs_db, od_db, OaT_b] + slot_b)
            cAB.close()
            return nc
        if os.environ.get("MK_DBG"):
            print("SBUF remaining in AB region (before finalisation tensors):", nc.sbuf_bytes_remaining)
        pv = sbt("pv", [64, 768], F32, cAB); pv_b = Buf("pv")
        dn = sbt("dn", [64, 12], F32, cAB); dn_b = Buf("dn")
        sprod = sbt("sprod", [64, 768], F32, cAB); sprod_b = Buf("sprod")
        sself = sbt("sself", [64, 12], F32, cAB); sself_b = Buf("sself")
        pself = sbt("pself", [64, 12], F32, cAB); pself_b = Buf("pself")
        Osm = sbt("Osm", [64, 256], F32, cAB); Osm_b = Buf("Osm")
        den4 = sbt("den4", [64, 4], F32, cAB); den4_b = Buf("den4")
        AB_bufs += [pv_b, dn_b, sprod_b, sself_b, pself_b, Osm_b, den4_b]
        for g in range(3):
            P.dma("sp", pv[:, g * 256:(g + 1) * 256].rearrange("t (h d) -> t h d", h=4),
                  bass.AP(od_h, g * 1028, [[3084, 64], [321, 4], [1, 64]]), reads=[od_db], writes=[pv_b])
            P.dma("sp", dn[:, g * 4:(g + 1) * 4],
                  bass.AP(od_h, g * 1028 + 256, [[3084, 64], [257, 4]]), reads=[od_db], writes=[dn_b], allow_slow_non_contiguous=True)
        P.op("dve", lambda e: e.tensor_tensor(out=sprod[:].rearrange("t (g n) -> t g n", g=3), in0=qs_s[:].rearrange("t (g n) -> t g n", g=3),
                                              in1=kvs_s[:, :, 0:256], op=ALU.mult), reads=[qs_sb, kvs_sb], writes=[sprod_b])
        P.op("dve", lambda e: e.tensor_reduce(out=sself[:], in_=sprod[:].rearrange("t (a d) -> t a d", d=64), axis=AX.X, op=ALU.add),
             reads=[sprod_b], writes=[sself_b])
        P.op("act", lambda e: e.activation(out=pself[:], in_=sself[:], func=AF.Exp, scale=0.125), reads=[sself_b], writes=[pself_b])
        P.op("dve", lambda e: e.tensor_tensor(out=sprod[:].rearrange("t (g h d) -> t g h d", g=3, h=4),
                                              in0=kvs_s[:, :, 256:512].rearrange("t g (h d) -> t g h d", h=4),
                                              in1=bass.AP(pself, 0, [[12, 64], [4, 3], [1, 4], [0, 64]]), op=ALU.mult),
             reads=[kvs_sb, pself_b, sprod_b], writes=[sprod_b])
        P.op("dve", lambda e: e.tensor_tensor(out=pv[:], in0=pv[:], in1=sprod[:], op=ALU.add), reads=[pv_b, sprod_b], writes=[pv_b])
        P.op("dve", lambda e: e.tensor_tensor(out=Osm[:], in0=pv[:, 0:256], in1=pv[:, 256:512], op=ALU.add), reads=[pv_b], writes=[Osm_b])
        P.op("dve", lambda e: e.tensor_tensor(out=Osm[:], in0=Osm[:], in1=pv[:, 512:768], op=ALU.add), reads=[pv_b, Osm_b], writes=[Osm_b])
        P.op("dve", lambda e: e.tensor_tensor(out=dn[:], in0=dn[:], in1=pself[:], op=ALU.add), reads=[dn_b, pself_b], writes=[dn_b])
        P.op("dve", lambda e: e.tensor_tensor(out=den4[:], in0=dn[:, 0:4], in1=dn[:, 4:8], op=ALU.add), reads=[dn_b], writes=[den4_b])
        P.op("dve", lambda e: e.tensor_tensor(out=den4[:], in0=den4[:], in1=dn[:, 8:12], op=ALU.add), reads=[dn_b, den4_b], writes=[den4_b])
        P.op("dve", lambda e: e.reciprocal(out=den4[:], in_=den4[:]), reads=[den4_b], writes=[den4_b])
        P.op("dve", lambda e: e.tensor_tensor(out=Osm[:].rearrange("t (h d) -> t h d", h=4), in0=Osm[:].rearrange("t (h d) -> t h d", h=4),
                                              in1=bass.AP(den4, 0, [[4, 64], [1, 4], [0, 64]]), op=ALU.mult),
             reads=[Osm_b, den4_b], writes=[Osm_b])
        for pp in range(2):
            pst, psb = next_ps()
            mm(pst[:, 0:64], Osm[0:64, pp * 128:(pp + 1) * 128], identf[0:64, 0:64], True, True, [Osm_b, cst], [psb])
            evac_copy(OaT[:, pp, S:S + NS], pst[:, 0:64], [psb], [OaT_b])

        if STAGE < 7:
            P.finish([y_ob, kvp128_b, kvp512_b, kvp2048_b, kvs_ob, sg_ob, qs_db, od_db, OaT_b] + slot_b)
            cAB.close()
            return nc
        cAB.close()

        cC = ExitStack()
        xNt = sbt("xNt", [128, 4, D], F32, cC); xNt_b = Buf("xNt")
        fst = sbt("fst", [128, 4, 512], F32, cC); fst_b = Buf("fst")
        fsl_b = [Buf("fsl%d" % i) for i in range(4)]
        mT = sbt("mT", [128, 8, NT], BF16, cC); mT_b = Buf("mT")
        h2b = sbt("h2b", [128, D], BF16, cC); h2b_b = Buf("h2b")
        h2T = sbt("h2T", [128, 8, NT], BF16, cC); h2T_b = Buf("h2T")
        uT = sbt("uT", [128, 32, NT], BF16, cC); uT_b = Buf("uT")
        rt = [sbt("rt%d" % i, [128, NT], BF16, cC) for i in range(2)]
        rt_b = [Buf("rt%d" % i) for i in range(2)]
        gA = [sbt("gA%d" % i, [128, NT], F32, cC) for i in range(2)]
        gA_b = [Buf("gA%d" % i) for i in range(2)]
        gB = [sbt("gB%d" % i, [128, NT], F32, cC) for i in range(2)]
        gB_b = [Buf("gB%d" % i) for i in range(2)]
        obt = sbt("obt", [128, 4, NT], BF16, cC); obt_b = Buf("obt")
        z2f = [fst[:, i, :] for i in range(2)]
        z2f_b = [fsl_b[0], fsl_b[1]]
        hTt2 = sbt("hTt2", [128, 8, NT], BF16, cC); hTt2_b = Buf("hTt2")
        hC = [(hTt, hTt_b), (hTt2, hTt2_b)]
        znb = [sbt("znb%d" % i, [128, 512], BF16, cC) for i in range(2)]
        znb_b = [Buf("znb%d" % i) for i in range(2)]
        tmpf = [sbt("tmpf%d" % i, [128, 512], F32, cC) for i in range(2)]
        tmpf_b = [Buf("tmpf%d" % i) for i in range(2)]
        st = [sbt("st%d" % i, [128, 8], F32, cC) for i in range(2)]
        st_b = [Buf("st%d" % i) for i in range(2)]
        gpm = sbt("gpm", [128, D], F32, cC)
        gpf = sbt("gpf", [128, D], F32, cC)
        lng = sbt("lng", [128, 512], F32, cC)
        lnb = sbt("lnb", [128, 512], F32, cC)
        cst2 = Buf("consts2")
        xNb = [Buf("xNb%d" % i) for i in range(4)]
        C_bufs = xNb + [xNt_b, fst_b, mT_b] + fsl_b + [h2b_b, h2T_b, uT_b, obt_b, cst2] + rt_b + gA_b + gB_b + z2f_b + znb_b + tmpf_b + st_b
        P.inherit(C_bufs, AB_bufs)
        P.inherit([xTs_b], KVt_b)
        P.inherit([hTt_b], qbx_b)
        for (t, d_) in ((gpm, gpm_d), (gpf, gpf_d), (lng, lng_d), (lnb, lnb_d)):
            P.dma("sp", t[:], d_, writes=[cst2])

        rr = {"z": 0, "g": 0, "t": 0, "r": 0, "s": 0}

        def nxt(k, n=2):
            v = rr[k]
            rr[k] = (v + 1) % n
            return v

        def rstd_from_sums(si, cols, n, bs):
            stt, stb = st[si], st_b[si]
            if len(cols) == 2:
                P.op("dve", lambda e: e.tensor_tensor(out=stt[0:bs, 5:6], in0=stt[0:bs, cols[0]:cols[0] + 1], in1=stt[0:bs, cols[1]:cols[1] + 1], op=ALU.add),
                     reads=[stb], writes=[stb])
                c = 5
            else:
                c = cols[0]
            P.op("act", lambda e: e.activation(out=stt[0:bs, 6:7], in_=stt[0:bs, c:c + 1], func=AF.Sqrt, bias=epsT[0:bs, 0:1], scale=1.0 / n),
                 reads=[stb, cst], writes=[stb])
            P.op("dve", lambda e: e.reciprocal(out=stt[0:bs, 7:8], in_=stt[0:bs, 6:7]), reads=[stb], writes=[stb])
            return stt[0:bs, 7:8]

        z2f_all = [fst[:, i, :] for i in range(4)]
        z2f_allb = list(fsl_b)
        P.inherit([hTt2_b], AB_bufs)
        st4 = [sbt("st4_%d" % i, [128, 8], F32, cC) for i in range(4)]
        st4_b = [Buf("st4_%d" % i) for i in range(4)]
        st5 = [sbt("st5_%d" % i, [128, 8], F32, cC) for i in range(4)]
        st5_b = [Buf("st5_%d" % i) for i in range(4)]
        P.inherit(st5_b, AB_bufs)
        P.inherit(z2f_allb[2:] + st4_b, AB_bufs)
        znb4 = [sbt("znb4_%d" % i, [128, 512], BF16, cC) for i in range(2)]
        znb_all = znb + znb4
        znb_allb = znb_b + [Buf("znb4_%d" % i) for i in range(2)]
        P.inherit(znb_allb[2:], AB_bufs)
        h2b2 = sbt("h2b2", [128, D], BF16, cC)
        h2bs = [h2b, h2b2]
        h2bs_b = [h2b_b, Buf("h2b2")]
        P.inherit([h2bs_b[1]], AB_bufs)

        if os.environ.get("MK_DBG"):
            print("SBUF remaining in C region:", nc.sbuf_bytes_remaining)

        ysem_b = Buf("ysem")
        uTf_b = [Buf("uTf%d" % i) for i in range(8)]

        def gm_front_a(tok0, nt, nblk, bs, H):
            sample = (nblk == 1)
            s = w_acquire("Z2")
            zps = []
            for b in range(nblk):
                pst, psb = next_ps()
                for kc in range(8):
                    mm(pst[0:bs, 0:512], H[0][:, kc, b * bs:(b + 1) * bs], slots[s][:, kc * 512:(kc + 1) * 512], kc == 0, kc == 7,
                       [slot_b[s], H[1]], [psb])
                zps.append((pst, psb))
            w_release(s)
            zts = [(z2f_all[b], z2f_allb[b], st4[b], st4_b[b]) for b in range(nblk)]
            for b in range(nblk):
                P.op("pool", lambda e, b=b: e.memset(zts[b][2][:], 0.0), writes=[zts[b][3]])
            for b in range(nblk):
                pst, psb = zps[b]
                zt, ztb, stt, stb = zts[b]
                P.op("act", lambda e, zt=zt, pst=pst, stt=stt: e.activation(out=zt[0:bs, :], in_=pst[0:bs, 0:512], func=AF.Gelu, accum_out=stt[0:bs, 0:1]),
                     reads=[psb, stb], writes=[ztb, stb])
                P.op("act", lambda e, zt=zt, stt=stt: e.activation(out=junk[0:bs, 0:512], in_=zt[0:bs, :], func=AF.Square, accum_out=stt[0:bs, 1:2]),
                     reads=[ztb, stb], writes=[junk_b, stb])
            return zts

        def gm_front_b(tok0, nt, nblk, bs, H, zts):
            sample = (nblk == 1)
            s = w_acquire("Z1")
            for c in range(4):
                pst, psb = next_ps()
                for kc in range(8):
                    mm(pst[:, 0:nt], slots[s][:, kc * 512 + c * 128: kc * 512 + c * 128 + 128], H[0][:, kc, 0:nt], kc == 0, kc == 7,
                       [slot_b[s], H[1]], [psb])
                P.op("act", lambda e, c=c, pst=pst: e.activation(out=obt[:, c, 0:nt], in_=pst[:, 0:nt], func=AF.Gelu), reads=[psb], writes=[obt_b])
            w_release(s)
            for b in range(nblk):
                zt, ztb, stt, stb = zts[b]
                P.op("dve", lambda e, stt=stt: e.tensor_scalar(out=stt[0:bs, 2:3], in0=stt[0:bs, 0:1], scalar1=1.0 / 512, scalar2=None, op0=ALU.mult),
                     reads=[stb], writes=[stb])
                P.op("dve", lambda e, stt=stt: e.tensor_tensor(out=stt[0:bs, 3:4], in0=stt[0:bs, 2:3], in1=stt[0:bs, 2:3], op=ALU.mult),
                     reads=[stb], writes=[stb])
                P.op("dve", lambda e, stt=stt: e.scalar_tensor_tensor(out=stt[0:bs, 4:5], in0=stt[0:bs, 1:2], scalar=1.0 / 512, in1=stt[0:bs, 3:4],
                                                                      op0=ALU.mult, op1=ALU.subtract), reads=[stb], writes=[stb])
            for b in range(nblk):
                zt, ztb, stt, stb = zts[b]
                P.op("act", lambda e, stt=stt: e.activation(out=stt[0:bs, 6:7], in_=stt[0:bs, 4:5], func=AF.Sqrt, bias=epsT[0:bs, 0:1], scale=1.0),
                     reads=[stb, cst], writes=[stb])
            for b in range(nblk):
                zt, ztb, stt, stb = zts[b]
                P.op("dve", lambda e, stt=stt: e.reciprocal(out=stt[0:bs, 7:8], in_=stt[0:bs, 6:7]), reads=[stb], writes=[stb])
                P.op("dve", lambda e, zt=zt, stt=stt: e.tensor_scalar(out=zt[0:bs, :], in0=zt[0:bs, :], scalar1=stt[0:bs, 2:3], scalar2=stt[0:bs, 7:8],
                                                                      op0=ALU.subtract, op1=ALU.mult), reads=[ztb, stb], writes=[ztb])
                P.op("dve", lambda e, zt=zt: e.tensor_tensor(out=zt[0:bs, :], in0=zt[0:bs, :], in1=lng[0:bs, :], op=ALU.mult), reads=[ztb, cst2], writes=[ztb])
                zb, zbb = znb_all[b], znb_allb[b]
                if not sample:
                    P.op("pool", lambda e, zt=zt, zb=zb: e.tensor_tensor(out=zb[0:bs, :], in0=zt[0:bs, :], in1=lnb[0:bs, :], op=ALU.add), reads=[ztb, cst2], writes=[zbb])
                else:
                    P.op("pool", lambda e, zt=zt: e.tensor_tensor(out=zt[0:bs, :], in0=zt[0:bs, :], in1=lnb[0:bs, :], op=ALU.add), reads=[ztb, cst2], writes=[ztb])
                    P.op("pool", lambda e, zt=zt, zb=zb: e.tensor_copy(out=zb[0:bs, :], in_=zt[0:bs, :]), reads=[ztb], writes=[zbb])
                    P.dma("pool", sg_d, zt[0:bs, :], reads=[ztb], writes=[sg_ob], sb=ztb)

        def gm_back(tok0, nt, nblk, bs):
            sample = (nblk == 1)
            for b in range(nblk):
                zb, zbb = znb_all[b], znb_allb[b]
                pm, pmb = next_ps()
                if not sample:
                    for g4 in range(4):
                        cs = slice(g4 * 128, (g4 + 1) * 128)
                        mm(pm[:, cs], zb[:, cs], wspT[:, cs], True, False, [zbb, cst, cstL], [pmb])
                        mm(pm[:, cs], ident[:], bsp_hi[:, cs], False, False, [cst, cstL], [pmb])
                        mm(pm[:, cs], ident[:], bsp_lo[:, cs], False, True, [cst, cstL], [pmb])
                    ov = obt[:, :, b * 128:(b + 1) * 128]
                    P.op("dve", lambda e, ov=ov, pm=pm: e.tensor_tensor(out=ov, in0=ov, in1=pm[:, 0:512].rearrange("p (g t) -> p g t", g=4), op=ALU.mult),
                         reads=[pmb, obt_b], writes=[obt_b])
                else:
                    for g4 in range(4):
                        mm(pm[:, g4 * 64:(g4 + 1) * 64], zb[0:64, g4 * 128:(g4 + 1) * 128], wsS[0:64, g4 * 64:(g4 + 1) * 64], True, True, [zbb, cst, cstL], [pmb])
                    ti = nxt("t")
                    P.op("dve", lambda e, pm=pm, ti=ti: e.tensor_tensor(out=tmpf[ti][:, 0:256], in0=pm[:, 0:256], in1=bspS[:, 0:256], op=ALU.add),
                         reads=[pmb, cst, cstL], writes=[tmpf_b[ti]])
                    ov = obt[:, :, 0:64]
                    P.op("dve", lambda e, ov=ov, ti=ti: e.tensor_tensor(out=ov, in0=ov, in1=tmpf[ti][:, 0:256].rearrange("p (g t) -> p g t", g=4), op=ALU.mult),
                         reads=[tmpf_b[ti], obt_b], writes=[obt_b])

        def phase_c(tok0, nt, nblk, bs, H, nxt_tile):
            sample = (nblk == 1)
            for b in range(nblk):
                P.dma("sp", xNt[0:bs, b, :], xN[tok0 + b * bs: tok0 + (b + 1) * bs, :], writes=[xNb[b]])
            nx_rs = None
            if nxt_tile is not None:
                nx_rs = hT_part1(xT, nxt_tile[0], nxt_tile[1], "reuse", nxt_tile[0])
            sA = w_acquire("AO")
            sB = w_acquire("BO")
            for j in range(4):
                sG = w_acquire("GP%d" % j)
                for cc in range(2):
                    c = 2 * j + cc
                    pga, pgab = next_ps()
                    for kc in range(8):
                        mm(pga[:, 0:nt], slots[sG][:, kc * 512 + cc * 128: kc * 512 + cc * 128 + 128], H[0][:, kc, 0:nt], kc == 0, kc == 7,
                           [slot_b[sG], H[1]], [pgab])
                    pgb, pgbb = next_ps()
                    for kc in range(8):
                        mm(pgb[:, 0:nt], slots[sG][:, kc * 512 + 256 + cc * 128: kc * 512 + 256 + cc * 128 + 128], H[0][:, kc, 0:nt], kc == 0, kc == 7,
                           [slot_b[sG], H[1]], [pgbb])
                    pa, pab = next_ps()
                    for pp in range(2):
                        mm(pa[:, 0:nt], slots[sA][:, pp * 1024 + c * 128: pp * 1024 + c * 128 + 128], OaT[:, pp, tok0:tok0 + nt], pp == 0, pp == 1,
                           [slot_b[sA], OaT_b], [pab])
                    pb, pbb = next_ps()
                    for kc in range(4):
                        mm(pb[:, 0:nt], slots[sB][:, kc * 1024 + c * 128: kc * 1024 + c * 128 + 128], obt[:, kc, 0:nt], kc == 0, kc == 3,
                           [slot_b[sB], obt_b], [pbb])
                    gi = nxt("g")
                    P.op("act", lambda e, gi=gi, pga=pga, c=c: e.activation(out=gA[gi][:, 0:nt], in_=pga[:, 0:nt], func=AF.Sigmoid, bias=bgate[:, c:c + 1], scale=1.0),
                         reads=[pgab, cst], writes=[gA_b[gi]])
                    P.op("act", lambda e, gi=gi, pgb=pgb, c=c: e.activation(out=gB[gi][:, 0:nt], in_=pgb[:, 0:nt], func=AF.Sigmoid, bias=bgate[:, 8 + c:9 + c], scale=1.0),
                         reads=[pgbb, cst], writes=[gB_b[gi]])
                    P.op("dve", lambda e, gi=gi, pa=pa: e.tensor_tensor(out=gA[gi][:, 0:nt], in0=gA[gi][:, 0:nt], in1=pa[:, 0:nt], op=ALU.mult),
                         reads=[gA_b[gi], pab], writes=[gA_b[gi]])
                    P.op("dve", lambda e, gi=gi, pb=pb: e.tensor_tensor(out=gB[gi][:, 0:nt], in0=gB[gi][:, 0:nt], in1=pb[:, 0:nt], op=ALU.mult),
                         reads=[gB_b[gi], pbb], writes=[gB_b[gi]])
                    P.op("pool", lambda e, gi=gi, c=c: e.tensor_tensor(out=mT[:, c, 0:nt], in0=gA[gi][:, 0:nt], in1=gB[gi][:, 0:nt], op=ALU.add),
                         reads=[gA_b[gi], gB_b[gi]], writes=[mT_b])
                w_release(sG)
                if j == 0 and nxt_tile is not None:
                    hT_part2(nxt_tile[1], nx_rs[0], nx_rs[1], nxt_tile[4][0], nxt_tile[4][1])
            w_release(sA)
            w_release(sB)
            so = [w_acquire("OUT0"), w_acquire("OUT1")]
            blk_state = {}

            uTf = uT[:].rearrange("p f t -> p (f t)").bitcast(F32)
            P.inherit(uTf_b, [uT_b])
            sts = [(st5[b], st5_b[b]) for b in range(nblk)]

            def pre(b, hf):
                i0 = 2 * b + hf
                return uTf[0:bs, i0 * 512:(i0 + 1) * 512], uTf_b[i0]

            for b in range(nblk):
                stt, stb = sts[b]
                P.op("pool", lambda e, stt=stt: e.memset(stt[:], 0.0), writes=[stb])
                for hf in range(2):
                    pst, psb = next_ps()
                    for kc in range(8):
                        mm(pst[0:bs, 0:512], mT[:, kc, b * bs:(b + 1) * bs], slots[so[hf]][:, kc * 512:(kc + 1) * 512], kc == 0, kc == 7,
                           [slot_b[so[hf]], mT_b], [psb])
                    P.op("act", lambda e, pst=pst, stt=stt, hf=hf: e.activation(out=junk[0:bs, 0:512], in_=pst[0:bs, 0:512], func=AF.Square,
                                                                                accum_out=stt[0:bs, hf:hf + 1]),
                         reads=[psb, stb], writes=[junk_b, stb])
                    pa, pab = pre(b, hf)
                    P.op("dve", lambda e, pst=pst, pa=pa, hf=hf: e.tensor_tensor(out=pa, in0=pst[0:bs, 0:512], in1=gpm[0:bs, hf * 512:(hf + 1) * 512],
                                                                                 op=ALU.mult), reads=[psb, cst2], writes=[pab])
            w_release(so[0])
            w_release(so[1])
            for b in range(nblk):
                stt, stb = sts[b]
                P.op("dve", lambda e, stt=stt: e.tensor_tensor(out=stt[0:bs, 5:6], in0=stt[0:bs, 0:1], in1=stt[0:bs, 1:2], op=ALU.add), reads=[stb], writes=[stb])
            for b in range(nblk):
                stt, stb = sts[b]
                P.op("act", lambda e, stt=stt: e.activation(out=stt[0:bs, 6:7], in_=stt[0:bs, 5:6], func=AF.Sqrt, bias=epsT[0:bs, 0:1], scale=1.0 / D),
                     reads=[stb, cst], writes=[stb])
            for b in range(nblk):
                stt, stb = sts[b]
                P.op("dve", lambda e, stt=stt: e.reciprocal(out=stt[0:bs, 7:8], in_=stt[0:bs, 6:7]), reads=[stb], writes=[stb])
                for hf in range(2):
                    pa, pab = pre(b, hf)
                    P.op("dve", lambda e, pa=pa, hf=hf, b=b, stt=stt: e.scalar_tensor_tensor(
                        out=xNt[0:bs, b, hf * 512:(hf + 1) * 512], in0=pa, scalar=stt[0:bs, 7:8], in1=xNt[0:bs, b, hf * 512:(hf + 1) * 512],
                        op0=ALU.mult, op1=ALU.add), reads=[pab, stb, xNb[b]], writes=[xNb[b]])
            for b in range(nblk):
                stt, stb = sts[b]
                P.op("act", lambda e, stt=stt, b=b: e.activation(out=junk[0:bs, :], in_=xNt[0:bs, b, :], func=AF.Square, accum_out=stt[0:bs, 2:3]),
                     reads=[xNb[b], stb], writes=[junk_b, stb])
            for b in range(nblk):
                stt, stb = sts[b]
                P.op("act", lambda e, stt=stt: e.activation(out=stt[0:bs, 3:4], in_=stt[0:bs, 2:3], func=AF.Sqrt, bias=epsT[0:bs, 0:1], scale=1.0 / D),
                     reads=[stb, cst], writes=[stb])
            zts_n = None
            if nxt_tile is not None:
                zts_n = gm_front_a(*nxt_tile)
            for b in range(nblk):
                stt, stb = sts[b]
                P.op("dve", lambda e, stt=stt: e.reciprocal(out=stt[0:bs, 4:5], in_=stt[0:bs, 3:4]), reads=[stb], writes=[stb])
                hb, hbb = h2bs[b % 2], h2bs_b[b % 2]
                P.op("dve", lambda e, b=b, stt=stt, hb=hb: e.tensor_scalar(out=hb[0:bs, :], in0=xNt[0:bs, b, :], scalar1=stt[0:bs, 4:5], scalar2=None, op0=ALU.mult),
                     reads=[xNb[b], stb], writes=[hbb])
                for kc in range(8):
                    P.op("pe", lambda e, kc=kc, hb=hb: e.transpose(out=psT_t[:, kc * 128: kc * 128 + bs], in_=hb[0:bs, kc * 128:(kc + 1) * 128], identity=ident[0:bs, 0:bs]),
                         reads=[hbb, cst], writes=[psT_b], inc=(kc == 7))
                P.op("dve", lambda e, b=b: e.tensor_tensor(out=h2T[:, :, b * bs:(b + 1) * bs], in0=psT_t[:, :].rearrange("p (k t) -> p k t", k=8)[:, :, 0:bs],
                                                           in1=bass.AP(gffn, 0, [[8, 128], [1, 8], [0, bs]]), op=ALU.mult),
                     reads=[psT_b, cst], writes=[h2T_b])
            P.inherit([uT_b], uTf_b)
            if nxt_tile is not None:
                gm_front_b(*nxt_tile, zts_n)
            for j in range(8):
                s = w_acquire("UP%d" % j)
                for cc in range(4):
                    fc = 4 * j + cc
                    pst, psb = next_ps()
                    for kc in range(8):
                        mm(pst[:, 0:nt], slots[s][:, kc * 512 + cc * 128: kc * 512 + cc * 128 + 128], h2T[:, kc, 0:nt], kc == 0, kc == 7,
                           [slot_b[s], h2T_b], [psb])
                    ri = nxt("r")
                    P.op("act", lambda e, pst=pst, ri=ri: e.activation(out=rt[ri][:, 0:nt], in_=pst[:, 0:nt], func=AF.Relu), reads=[psb], writes=[rt_b[ri]])
                    P.op("pool", lambda e, ri=ri, fc=fc: e.tensor_tensor(out=uT[:, fc, 0:nt], in0=rt[ri][:, 0:nt], in1=rt[ri][:, 0:nt], op=ALU.mult),
                         reads=[rt_b[ri]], writes=[uT_b])
                w_release(s)
            if nxt_tile is not None:
                gm_back(*nxt_tile[:4])
            sis = []
            for b in range(nblk):
                si_ = b % 2
                sis.append(si_)
            P.op("pool", lambda e: e.memset(st[0][:], 0.0), writes=[st_b[0]])
            P.op("pool", lambda e: e.memset(st[1][:], 0.0), writes=[st_b[1]])
            def scol(b, hf):
                return (b // 2) * 2 + hf
            for hf in range(2):
                acc = [next_ps() for _ in range(nblk)]
                for j in range(4):
                    s = w_acquire("DN%d_%d" % (hf, j))
                    for b in range(nblk):
                        pst, psb = acc[b]
                        for f8 in range(8):
                            fc = 8 * j + f8
                            mm(pst[0:bs, 0:512], uT[:, fc, b * bs:(b + 1) * bs], slots[s][:, f8 * 512:(f8 + 1) * 512], j == 0 and f8 == 0, j == 3 and f8 == 7,
                               [slot_b[s], uT_b], [psb])
                    w_release(s)
                for b in range(nblk):
                    pst, psb = acc[b]
                    stt, stb = st[b % 2], st_b[b % 2]
                    c0 = scol(b, hf)
                    P.op("act", lambda e, pst=pst, stt=stt, c0=c0: e.activation(out=junk[0:bs, 0:512], in_=pst[0:bs, 0:512], func=AF.Square,
                                                                                accum_out=stt[0:bs, c0:c0 + 1]),
                         reads=[psb, stb], writes=[junk_b, stb])
                    if hf == 0:
                        P.op("dve", lambda e, pst=pst, b=b: e.tensor_tensor(out=fst[0:bs, b, :], in0=pst[0:bs, 0:512], in1=gpf[0:bs, 0:512], op=ALU.mult),
                             reads=[psb, cst2], writes=[fsl_b[b]])
                    else:
                        ti = nxt("t")
                        P.op("dve", lambda e, pst=pst, ti=ti: e.tensor_tensor(out=tmpf[ti][0:bs, :], in0=pst[0:bs, 0:512], in1=gpf[0:bs, 512:1024], op=ALU.mult),
                             reads=[psb, cst2], writes=[tmpf_b[ti]])
                        rs = rstd_from_sums(b % 2, [scol(b, 0), scol(b, 1)], D, bs)
                        P.op("dve", lambda e, b=b, rs=rs: e.scalar_tensor_tensor(out=xNt[0:bs, b, 0:512], in0=fst[0:bs, b, :], scalar=rs, in1=xNt[0:bs, b, 0:512],
                                                                                 op0=ALU.mult, op1=ALU.add), reads=[fsl_b[b], stb, xNb[b]], writes=[xNb[b]])
                        P.op("dve", lambda e, b=b, rs=rs, ti=ti: e.scalar_tensor_tensor(out=xNt[0:bs, b, 512:1024], in0=tmpf[ti][0:bs, :], scalar=rs,
                                                                                        in1=xNt[0:bs, b, 512:1024], op0=ALU.mult, op1=ALU.add),
                             reads=[tmpf_b[ti], stb, xNb[b]], writes=[xNb[b]])
                        P.dma("pool", y_d[tok0 + b * bs: tok0 + (b + 1) * bs, :], xNt[0:bs, b, :], reads=[xNb[b]], post_writes=[y_ob], sb=ysem_b)

        compute_hT(xT, 0, NT, "reuse", 0)
        tiles = [(NT * i, NT, 4, 128, hC[i % 2]) for i in range(4)] + [(S, NS, 1, 64, hC[0])]
        zts0 = gm_front_a(*tiles[0])
        gm_front_b(*tiles[0], zts0)
        gm_back(*tiles[0][:4])
        for i in range(5):
            phase_c(*tiles[i], tiles[i + 1] if i < 4 else None)
        assert WS.i_use == len(worder), (WS.i_use, len(worder))

        P.finish([y_ob, kvp128_b, kvp512_b, kvp2048_b, kvs_ob, sg_ob])
        cC.close()
    return nc


_NC_CACHE = {}


def kernel(x_prompt, x_sample, cache_kv_w128, cache_kv_w512, cache_kv_w2048,
           norm_pre_mix, w_in, b_gate, ln_z_g, ln_z_b, w_spatial, b_spatial,
           w_ao, w_bo, w_out, norm_post_mix, norm_pre_ffn, w_up, w_down, norm_post_ffn):
    f = lambda a: np.ascontiguousarray(np.asarray(a, dtype=np.float32))
    x_prompt = np.asarray(x_prompt, dtype=np.float32)
    x_sample = np.asarray(x_sample, dtype=np.float32)
    c128 = np.asarray(cache_kv_w128, dtype=np.float32)[0].reshape(128, 128, 512)
    c512 = np.asarray(cache_kv_w512, dtype=np.float32)[0].reshape(128, 512, 512)
    c2048 = np.asarray(cache_kv_w2048, dtype=np.float32)[0].reshape(128, 2048, 512)
    rep = lambda v, n=128: f(np.broadcast_to(np.asarray(v, dtype=np.float32).reshape(1, -1), (n, np.asarray(v).size)))
    pk = lambda v: f(np.asarray(v, dtype=np.float32).reshape(-1, 128).T)
    wsp = np.asarray(w_spatial, dtype=np.float32)[0]
    bsp = np.asarray(b_spatial, dtype=np.float32)[0]
    wspT = f(wsp.transpose(2, 0, 1).reshape(128, 512))
    wsS = np.zeros((64, 4, 64), np.float32)
    for b in range(16):
        wsS[4 * b:4 * b + 4, :, 4 * b:4 * b + 4] = wsp[:, 0:4, 0:4].transpose(2, 0, 1)
    bspS = np.tile(bsp[:, 0:4], (1, 16)).reshape(1, 256)
    shared = {
        "w_in": f(np.asarray(w_in)[0]), "w_ao": f(np.asarray(w_ao)[0]), "w_bo": f(np.asarray(w_bo)[0]),
        "w_out": f(np.asarray(w_out)[0]), "w_up": f(np.asarray(w_up)[0]), "w_down": f(np.asarray(w_down)[0]),
        "gpre": pk(np.asarray(norm_pre_mix)[0]), "bgate": pk(np.asarray(b_gate)[0]),
        "gpm": rep(np.asarray(norm_post_mix)[0]), "gffn": pk(np.asarray(norm_pre_ffn)[0]),
        "gpf": rep(np.asarray(norm_post_ffn)[0]), "lng": rep(np.asarray(ln_z_g)[0]), "lnb": rep(np.asarray(ln_z_b)[0]),
        "wspT": wspT, "bsp": rep(bsp.reshape(-1)), "wsS": f(wsS.reshape(64, 256)), "bspS": rep(bspS),
    }
    in_maps = []
    for c in range(NCORES):
        xp = x_prompt[c]
        xs = x_sample[16 * c:16 * c + 16].reshape(NS, D)
        xNc = np.concatenate([xp, xs], axis=0)
        m = dict(shared)
        m["xN"] = f(xNc)
        m["xT"] = f(xNc.T)
        m["xT16"] = f(xp.T.reshape(D, 128, 16).transpose(0, 2, 1).reshape(D, S))
        m["c128"] = f(c128[16 * c:16 * c + 16])
        m["c512"] = f(c512[16 * c:16 * c + 16])
        m["c2048"] = f(c2048[16 * c:16 * c + 16])
        in_maps.append(m)
    if "nc" not in _NC_CACHE:
        _NC_CACHE["nc"] = build_program()
    nc = _NC_CACHE["nc"]
    ndbg = int(os.environ.get("MK_NCORES", str(NCORES)))
    if ndbg < NCORES:
        res = run_bass_kernel_spmd(nc, in_maps[:ndbg], core_ids=list(range(ndbg)))
        R = list(res.results) + [res.results[0]] * (NCORES - ndbg)
    else:
        res = run_bass_kernel_spmd(nc, in_maps, core_ids=list(range(NCORES)))
        R = res.results
    y_p = np.stack([R[c]["y"][0:S] for c in range(NCORES)], 0)
    y_s = np.concatenate([R[c]["y"][S:].reshape(16, 4, D) for c in range(NCORES)], 0)
    kvp128 = np.stack([R[c]["kvp128"].reshape(128, 2, 4, 64) for c in range(NCORES)], 0)[None]
    kvp512 = np.stack([R[c]["kvp512"].reshape(512, 2, 4, 64) for c in range(NCORES)], 0)[None]
    kvp2048 = np.stack([R[c]["kvp2048"].reshape(2048, 2, 4, 64) for c in range(NCORES)], 0)[None]
    kvs = [np.concatenate([R[c]["kvs"][g].reshape(16, 4, 2, 4, 64) for c in range(NCORES)], 0)[None] for g in range(3)]
    sg = np.concatenate([R[c]["sg"].reshape(16, 4, 512) for c in range(NCORES)], 0)[None]
    outs = (y_p, y_s, kvp128, kvp512, kvp2048, kvs[0], kvs[1], kvs[2], sg)
    return tuple(np.ascontiguousarray(o, dtype=np.float32) for o in outs)
```
